# Optimizing a Trainium2 kernel written in Bass

```python
import math
import jax, jax.numpy as jnp
from jax import lax
import numpy as np

D_MODEL = 1024
BATCH = 8
SEQ = 2048
DEPTH = 4
DEC_BATCH = 4
DEC_SEQ = 8192
PAST_LEN = 128

EPS = 1e-6
N_BRANCH = 3
D_CONV = 768
CONV_K = 31
D_SG = 768
SG_CHUNK = 128
SG_GROUPS = 6
SG_GROUP_DIM = D_SG // SG_GROUPS
N_HEADS = 12
QK_NOPE = 64
QK_ROPE = 32
QK_HEAD = QK_NOPE + QK_ROPE
V_HEAD = 64
Q_LORA = 256
KV_LORA = 128
ROPE_BASE = 10000.0
Q_BLOCK = 128
D_FF = ((8 * D_MODEL // 3 + 255) // 256) * 256
D_IN = N_BRANCH * D_MODEL + 2 * D_CONV + 2 * D_SG + Q_LORA + KV_LORA + QK_ROPE

kernel_name = 'hybrid_conv_gmlp_mla_encoder'


def rmsnorm(x, g):
    xf = x.astype(jnp.float32)
    y = xf * lax.rsqrt(jnp.mean(xf * xf, axis=-1, keepdims=True) + EPS)
    return (y * g.astype(jnp.float32)).astype(x.dtype)


def layernorm(x, g, b):
    xf = x.astype(jnp.float32)
    mu = jnp.mean(xf, axis=-1, keepdims=True)
    var = jnp.mean(jnp.square(xf - mu), axis=-1, keepdims=True)
    y = (xf - mu) * lax.rsqrt(var + EPS)
    return (y * g.astype(jnp.float32) + b.astype(jnp.float32)).astype(x.dtype)


def rope_tables(S, dtype):
    pos = jnp.arange(S, dtype=jnp.float32)
    inv = ROPE_BASE ** (-jnp.arange(0, QK_ROPE, 2, dtype=jnp.float32) / QK_ROPE)
    ang = pos[:, None] * inv[None, :]
    return jnp.cos(ang).astype(dtype), jnp.sin(ang).astype(dtype)


def apply_rope(x, cos, sin):
    c = cos[None, :, None, :]
    s = sin[None, :, None, :]
    x1, x2 = jnp.split(x, 2, axis=-1)
    return jnp.concatenate([x1 * c - x2 * s, x2 * c + x1 * s], axis=-1)


def conv_branch(h, conv_w, conv_b, conv_norm_g, conv_norm_b, w_conv_out):
    a, gt = jnp.split(h, 2, axis=-1)
    z = a * jax.nn.sigmoid(gt)
    z = lax.conv_general_dilated(
        z, conv_w[:, None, :], window_strides=(1,),
        padding=((CONV_K // 2, CONV_K // 2),),
        dimension_numbers=('NWC', 'WIO', 'NWC'),
        feature_group_count=D_CONV) + conv_b
    z = jax.nn.silu(layernorm(z, conv_norm_g, conv_norm_b))
    return z @ w_conv_out


def sg_branch(h, sg_norm_g, w_spatial, b_spatial, w_sg_out):
    B, S, _ = h.shape
    u, v = jnp.split(jax.nn.gelu(h), 2, axis=-1)
    v = rmsnorm(v, sg_norm_g)
    vb = v.reshape(B, S // SG_CHUNK, SG_CHUNK, SG_GROUPS, SG_GROUP_DIM)
    sv = jnp.einsum('gpq,bnqgc->bnpgc', w_spatial, vb) + b_spatial.T[:, :, None]
    return (u * sv.reshape(B, S, D_SG)) @ w_sg_out


def mla_branch(q_lat, kv_lat, k_pe, cos, sin, q_norm_g, w_uq, kv_norm_g, w_ukv,
               qk_q_g, qk_k_g, w_o):
    B, S, _ = q_lat.shape
    q = (rmsnorm(q_lat, q_norm_g) @ w_uq).reshape(B, S, N_HEADS, QK_HEAD)
    kv = (rmsnorm(kv_lat, kv_norm_g) @ w_ukv).reshape(B, S, N_HEADS, QK_NOPE + V_HEAD)
    k_nope, v = jnp.split(kv, [QK_NOPE], axis=-1)
    k = jnp.concatenate(
        [k_nope, jnp.broadcast_to(k_pe[:, :, None, :], (B, S, N_HEADS, QK_ROPE))], axis=-1)
    q = rmsnorm(q, qk_q_g)
    k = rmsnorm(k, qk_k_g)
    q = jnp.concatenate([q[..., :QK_NOPE], apply_rope(q[..., QK_NOPE:], cos, sin)], axis=-1)
    k = jnp.concatenate([k[..., :QK_NOPE], apply_rope(k[..., QK_NOPE:], cos, sin)], axis=-1)
    q = q * (1.0 / math.sqrt(QK_HEAD))
    q = q.transpose(0, 2, 1, 3)
    k = k.transpose(0, 2, 1, 3)
    v = v.transpose(0, 2, 1, 3)
    nb = S // Q_BLOCK
    qb = q.reshape(B, N_HEADS, nb, Q_BLOCK, QK_HEAD).transpose(2, 0, 1, 3, 4)

    def attend(qblk):
        s = jnp.einsum('bhqd,bhkd->bhqk', qblk, k).astype(jnp.float32)
        p = jax.nn.softmax(s, axis=-1)
        return jnp.einsum('bhqk,bhkv->bhqv', p.astype(v.dtype), v)

    o = lax.map(attend, qb)
    o = o.transpose(1, 0, 3, 2, 4).reshape(B, S, N_HEADS * V_HEAD)
    return o @ w_o


def encoder_layer(x, cos, sin, ln1_g, w_in, b_gate, conv_w, conv_b, conv_norm_g, conv_norm_b,
                  w_conv_out, sg_norm_g, w_spatial, b_spatial, w_sg_out, q_norm_g, w_uq,
                  kv_norm_g, w_ukv, qk_q_g, qk_k_g, w_o, w_out, ln2_g, w_ffn_in, w_ffn_out):
    B, S, _ = x.shape
    h = rmsnorm(x, ln1_g) @ w_in
    cuts = [N_BRANCH * D_MODEL,
            N_BRANCH * D_MODEL + 2 * D_CONV,
            N_BRANCH * D_MODEL + 2 * D_CONV + 2 * D_SG,
            N_BRANCH * D_MODEL + 2 * D_CONV + 2 * D_SG + Q_LORA,
            N_BRANCH * D_MODEL + 2 * D_CONV + 2 * D_SG + Q_LORA + KV_LORA]
    g_lin, h_conv, h_sg, q_lat, kv_lat, k_pe = jnp.split(h, cuts, axis=-1)
    gates = jax.nn.sigmoid(g_lin + b_gate).reshape(B, S, N_BRANCH, D_MODEL)
    y_a = conv_branch(h_conv, conv_w, conv_b, conv_norm_g, conv_norm_b, w_conv_out)
    y_b = sg_branch(h_sg, sg_norm_g, w_spatial, b_spatial, w_sg_out)
    y_c = mla_branch(q_lat, kv_lat, k_pe, cos, sin, q_norm_g, w_uq, kv_norm_g, w_ukv,
                     qk_q_g, qk_k_g, w_o)
    merged = gates[:, :, 0] * y_a + gates[:, :, 1] * y_b + gates[:, :, 2] * y_c
    x = x + merged @ w_out
    f_in, f_gate = jnp.split(rmsnorm(x, ln2_g) @ w_ffn_in, 2, axis=-1)
    x = x + (jax.nn.silu(f_gate) * f_in) @ w_ffn_out
    return x


def trunk(x, params):
    cos, sin = rope_tables(x.shape[1], x.dtype)
    for l in range(DEPTH):
        x = encoder_layer(x, cos, sin, *[p[l] for p in params])
    return x


def setup_inputs(seed: int = 0) -> dict:
    key = jax.random.key(seed)
    ks = jax.random.split(key, 32)

    def nrm(k, shape, scale):
        return jax.random.normal(k, shape, dtype=jnp.float32) * scale

    def gain(k, shape):
        return 1.0 + 0.02 * jax.random.normal(k, shape, dtype=jnp.float32)

    L = DEPTH
    return {
        'x_prompt': nrm(ks[0], (BATCH, SEQ, D_MODEL), 1.0),
        'x_sample': nrm(ks[1], (DEC_BATCH, DEC_SEQ, D_MODEL), 1.0),
        'ln1_g': gain(ks[2], (L, D_MODEL)),
        'w_in': nrm(ks[3], (L, D_MODEL, D_IN), D_MODEL ** -0.5),
        'b_gate': nrm(ks[4], (L, N_BRANCH * D_MODEL), 0.02),
        'conv_w': nrm(ks[5], (L, CONV_K, D_CONV), CONV_K ** -0.5),
        'conv_b': nrm(ks[6], (L, D_CONV), 0.02),
        'conv_norm_g': gain(ks[7], (L, D_CONV)),
        'conv_norm_b': nrm(ks[8], (L, D_CONV), 0.02),
        'w_conv_out': nrm(ks[9], (L, D_CONV, D_MODEL), D_CONV ** -0.5),
        'sg_norm_g': gain(ks[10], (L, D_SG)),
        'w_spatial': nrm(ks[11], (L, SG_GROUPS, SG_CHUNK, SG_CHUNK), SG_CHUNK ** -0.5),
        'b_spatial': gain(ks[12], (L, SG_GROUPS, SG_CHUNK)),
        'w_sg_out': nrm(ks[13], (L, D_SG, D_MODEL), D_SG ** -0.5),
        'q_norm_g': gain(ks[14], (L, Q_LORA)),
        'w_uq': nrm(ks[15], (L, Q_LORA, N_HEADS * QK_HEAD), Q_LORA ** -0.5),
        'kv_norm_g': gain(ks[16], (L, KV_LORA)),
        'w_ukv': nrm(ks[17], (L, KV_LORA, N_HEADS * (QK_NOPE + V_HEAD)), KV_LORA ** -0.5),
        'qk_q_g': gain(ks[18], (L, QK_HEAD)),
        'qk_k_g': gain(ks[19], (L, QK_HEAD)),
        'w_o': nrm(ks[20], (L, N_HEADS * V_HEAD, D_MODEL), (N_HEADS * V_HEAD) ** -0.5),
        'w_out': nrm(ks[21], (L, D_MODEL, D_MODEL), D_MODEL ** -0.5),
        'ln2_g': gain(ks[22], (L, D_MODEL)),
        'w_ffn_in': nrm(ks[23], (L, D_MODEL, 2 * D_FF), D_MODEL ** -0.5),
        'w_ffn_out': nrm(ks[24], (L, D_FF, D_MODEL), D_FF ** -0.5),
    }


def reference(x_prompt, x_sample, ln1_g, w_in, b_gate, conv_w, conv_b, conv_norm_g, conv_norm_b,
              w_conv_out, sg_norm_g, w_spatial, b_spatial, w_sg_out, q_norm_g, w_uq, kv_norm_g,
              w_ukv, qk_q_g, qk_k_g, w_o, w_out, ln2_g, w_ffn_in, w_ffn_out):
    params = (ln1_g, w_in, b_gate, conv_w, conv_b, conv_norm_g, conv_norm_b, w_conv_out,
              sg_norm_g, w_spatial, b_spatial, w_sg_out, q_norm_g, w_uq, kv_norm_g, w_ukv,
              qk_q_g, qk_k_g, w_o, w_out, ln2_g, w_ffn_in, w_ffn_out)
    y_prompt = trunk(x_prompt, params)
    y_sample = trunk(x_sample, params)
    return (y_prompt, y_sample)
```

```python
import numpy as np
import ml_dtypes
import concourse.bass as bass
import concourse.mybir as mybir
from concourse.bass_utils import run_bass_kernel_spmd

F32 = mybir.dt.float32
BF16 = mybir.dt.bfloat16
AF = mybir.ActivationFunctionType
ALU = mybir.AluOpType

D = 1024
DIN = 6560
DC = 768
CK = 31
NH = 12
DFF = 2816
EPS = 1e-6
NWA = 3488


class Sem:
    def __init__(s, h, idx):
        s.h = h
        s.idx = idx


class Stamp:
    __slots__ = ("eng", "sem", "val")

    def __init__(s, eng):
        s.eng = eng
        s.sem = None
        s.val = None


class Buf:
    __slots__ = ("name", "w", "r", "slot")

    def __init__(s, name=""):
        s.name = name
        s.w = {}
        s.r = {}
        s.slot = None


class Slot:
    def __init__(s):
        s.sem = None
        s.cnt = 0


class KB:
    def __init__(s, nc):
        s.nc = nc
        s.eng = {"pe": nc.tensor, "act": nc.scalar, "dve": nc.vector, "pool": nc.gpsimd, "sp": nc.sync}
        s.nsem = 0
        s.sem = {}
        s.cnt = {}
        s.seen = {k: {} for k in s.eng}
        s.pending = {k: [] for k in s.eng}
        s.slots = []
        s.free_slots = []
        s.owners = []
        s.allsems = []
        for k in s.eng:
            s.sem[k] = s.newsem()
            s.cnt[k] = 0
        s.nwaits = 0
        s.nops = 0

    def newsem(s):
        h = s.nc.semaphore(f"ks{s.nsem}").__enter__()
        sm = Sem(h, s.nsem)
        s.nsem += 1
        s.allsems.append(sm)
        return sm

    def _need(s, e, need, st):
        if st.eng == e and not s.selfsync:
            return
        assert st.val is not None, "dependency on a PE op with no milestone yet"
        k = st.sem.idx
        if k not in need or need[k][1] < st.val:
            need[k] = (st.sem, st.val)

    selfsync = False

    def _waits(s, e, reads, writes):
        need = {}
        for b in reads:
            for st in b.w.values():
                s._need(e, need, st)
        for b in writes:
            for st in b.w.values():
                s._need(e, need, st)
            for st in b.r.values():
                s._need(e, need, st)
        seen = s.seen[e]
        for k, (sm, val) in need.items():
            if seen.get(k, 0) < val:
                s.eng[e].wait_ge(sm.h, val)
                seen[k] = val
                s.nwaits += 1

    def _stamp(s, st, reads, writes):
        for b in reads:
            b.r[st.eng] = st
        for b in writes:
            b.w[st.eng] = st

    def op(s, e, fn, reads=(), writes=(), inc=True, selfsync=False):
        s.selfsync = selfsync
        s._waits(e, reads, writes)
        s.selfsync = False
        ins = fn(s.eng[e])
        s.nops += 1
        st = Stamp(e)
        if inc:
            if s.cnt[e] >= 30000:
                s.sem[e] = s.newsem()
                s.cnt[e] = 0
            s.cnt[e] += 1
            ins.then_inc(s.sem[e].h, 1)
            st.sem = s.sem[e]
            st.val = s.cnt[e]
            for p in s.pending[e]:
                p.sem = st.sem
                p.val = st.val
            s.pending[e] = []
        else:
            s.pending[e].append(st)
        s._stamp(st, reads, writes)
        return ins

    def dma(s, q, out, in_, reads, writes, slotbuf):
        if slotbuf.slot is None:
            if s.free_slots:
                slotbuf.slot = s.free_slots.pop()
            else:
                slotbuf.slot = Slot()
                slotbuf.slot.sem = s.newsem()
                s.slots.append(slotbuf.slot)
            s.owners.append(slotbuf)
        sl = slotbuf.slot
        if sl.cnt >= 30000:
            sl.sem = s.newsem()
            sl.cnt = 0
        s._waits(q, reads, writes)
        ins = s.eng[q].dma_start(out=out, in_=in_)
        s.nops += 1
        sl.cnt += 16
        ins.then_inc(sl.sem.h, 16)
        st = Stamp(("dma", sl.sem.idx))
        st.sem = sl.sem
        st.val = sl.cnt
        s._stamp(st, reads, writes)
        return ins

    def barrier(s):
        for e in s.eng:
            assert not s.pending[e], f"pending stamps on {e} at barrier"
        tgt = []
        for e in s.eng:
            if s.cnt[e] > 0:
                tgt.append((e, s.sem[e], s.cnt[e]))
        for sl in s.slots:
            if sl.cnt > 0:
                tgt.append((None, sl.sem, sl.cnt))
        for e in s.eng:
            seen = s.seen[e]
            for (src, sm, val) in tgt:
                if src == e:
                    continue
                if seen.get(sm.idx, 0) < val:
                    s.eng[e].wait_ge(sm.h, val)
                    seen[sm.idx] = val
                    s.nwaits += 1
        for b in s.owners:
            s.free_slots.append(b.slot)
            b.slot = None
        s.owners = []


class Pool:
    def __init__(s, nc):
        s.nc = nc
        s.stack = []
        s.uid = 0
        s.base = 0

    def _alloc(s, n, dt):
        cm = s.nc.sbuf_tensor(f"t{s.uid}", [128, n], dt)
        s.uid += 1
        t = cm.__enter__()
        s.stack.append(cm)
        return t[:, :]

    def f32(s, n):
        return s._alloc(n, F32)

    def bf16(s, n):
        return s._alloc(n, BF16)

    def reset(s):
        while len(s.stack) > s.base:
            s.stack.pop().__exit__(None, None, None)


PASSES = '1234'


def build(NTOK, NL, final_only=True):
    T1 = 256
    T = 512
    NT1 = NTOK // T1
    NT = NTOK // T
    NKB = NTOK // 128
    nc = bass.Bass("TRN2", target_bir_lowering=False)

    def din(name, shape, dt=F32):
        return nc.dram_tensor(name, list(shape), dt, kind="ExternalInput").ap()

    x_in = din("x", [NTOK, D])
    y_out = nc.dram_tensor("y", [NTOK, D], F32, kind="ExternalOutput").ap()
    cosT_d = din("cosT", [96, NTOK])
    sinT_d = din("sinT", [96, NTOK])
    qmask_d = din("qmask", [4, NTOK])
    kmask_d = din("kmask", [4, NTOK])
    hflag_d = din("hflag", [32, NT1])
    zrows_d = din("zrows", [16, D])
    ident_d = din("ident", [128, 128])
    ln1_g = din("ln1_g", [NL, D])
    w_in = din("w_in", [NL, D, DIN])
    w_pe = din("w_pe", [NL, D, 2, 96])
    b_gate = din("b_gateT", [NL, 128, 24])
    conv_wT = din("conv_wT", [NL, 128, 6, CK])
    conv_b = din("conv_bT", [NL, 128, 6])
    cn_g = din("cn_gT", [NL, 128, 6])
    cn_b = din("cn_bT", [NL, 128, 6])
    w_co = din("w_conv_out", [NL, DC, D])
    sg_g = din("sg_norm_g", [NL, DC])
    w_spT = din("w_spT", [NL, 128, 6, 128])
    b_sp = din("b_spatial", [NL, 1, 6 * 128])
    w_so = din("w_sg_out", [NL, DC, D])
    qn_g = din("qn_gT", [NL, 128, 2])
    w_uq = din("w_uq", [NL, 256, NH * 96])
    w_uqs = din("w_uqs", [NL, 256, NH * 96])
    kvn_g = din("kvn_gT", [NL, 128, 1])
    w_ukn = din("w_ukn", [NL, 128, NH * 64])
    w_ukv = din("w_ukv_v", [NL, 128, NH * 64])
    gq = din("gq", [NL, 96, 2])
    gk = din("gk", [NL, 96, 2])
    w_o = din("w_o", [NL, DC, D])
    w_out = din("w_out", [NL, D, D])
    ln2_g = din("ln2_g", [NL, D])
    w_fi = din("w_ffn_in", [NL, D, 2 * DFF])
    w_fo = din("w_ffn_out", [NL, DFF, D])

    xs = [nc.dram_tensor(f"xs{i}", [NTOK, D], F32).ap() for i in range(2)]
    xmid = nc.dram_tensor("xmid", [NTOK, D], F32).ap()
    yaT = nc.dram_tensor("yaT", [D, NTOK], BF16).ap()
    ybT = nc.dram_tensor("ybT", [D, NTOK], BF16).ap()
    qT_d = nc.dram_tensor("qTd", [NH, 96, NTOK], BF16).ap()
    kT_d = nc.dram_tensor("kTd", [NH, 96, NTOK], BF16).ap()
    v_d = nc.dram_tensor("vd", [NTOK, DC], BF16).ap()
    oT_d = nc.dram_tensor("oTd", [DC, NTOK], BF16).ap()

    P = Pool(nc)
    banks = []
    bank_cms = []
    bank_gen = [0]

    sbig = []
    obank = []

    def newbanks(p2=False):
        while bank_cms:
            bank_cms.pop().__exit__(None, None, None)
        banks[:] = []
        sbig[:] = []
        obank[:] = []
        if p2:
            for i in range(3):
                cm = nc.psum_tensor(f"sbig{bank_gen[0]}_{i}", [128, 1024], F32)
                sbig.append(cm.__enter__())
                bank_cms.append(cm)
            for i in range(2):
                cm = nc.psum_tensor(f"obank{bank_gen[0]}_{i}", [128, 512], F32)
                obank.append(cm.__enter__())
                bank_cms.append(cm)
        else:
            for i in range(8):
                cm = nc.psum_tensor(f"bank{bank_gen[0]}_{i}", [128, 512], F32)
                banks.append(cm.__enter__())
                bank_cms.append(cm)
        bank_gen[0] += 1

    newbanks()
    bankB = [Buf(f"bank{i}") for i in range(8)]
    K = KB(nc)

    ident = P.bf16(128)
    ones = P.bf16(128)
    epsT = P.f32(1)
    hflag = P.f32(NT1)
    B_const = Buf("const")
    K.dma("pool", ident, ident_d, [], [B_const], B_const)
    K.op("dve", lambda e: e.memset(ones, 1.0), [], [B_const])
    K.op("dve", lambda e: e.memset(epsT, EPS), [], [B_const])
    K.dma("sp", hflag[0:32, :], hflag_d, [], [B_const], B_const)
    P.base = len(P.stack)

    bank_rr = [0]

    def nextbank(lst=(0, 1, 2, 3, 4, 5, 6, 7)):
        i = lst[bank_rr[0] % len(lst)]
        bank_rr[0] += 1
        return i

    D_x = {}

    def dbuf(key):
        if key not in D_x:
            D_x[key] = Buf(str(key))
        return D_x[key]

    def rstd_from(ss_ap, out_ap, n, eng_bufs_r, eng_bufs_w, scale):
        K.op("act", lambda e: e.activation(out=out_ap, in_=ss_ap, func=AF.Sqrt, bias=epsT[0:n, :], scale=scale),
             eng_bufs_r + [B_const], eng_bufs_w, selfsync=True)
        K.op("dve", lambda e: e.reciprocal(out=out_ap, in_=out_ap), eng_bufs_w, eng_bufs_w, selfsync=True)

    def load_w_cast(dst3, src2, kch, ncols, bufw):
        K.dma("pool", dst3.rearrange("p (k n) -> p k n", n=ncols), src2.rearrange("(k p) n -> p k n", p=128),
              [], [bufw], bufw)

    def norm_transposed(xsrc, tok0, nb, g_rep, xin, xn, xnT, B_xin, B_xn, B_xnT, B_w, tilekey, ss, B_ss, junk, B_junk):
        Tn = nb * 128
        K.dma("sp", xin.rearrange("p (b d) -> p b d", d=D)[:, 0:nb, :],
              xsrc[tok0:tok0 + Tn, :].rearrange("(b p) d -> p b d", p=128), [dbuf(tilekey)], [B_xin], B_xin)
        for b in range(nb):
            xb = xin[:, b * D:(b + 1) * D]
            K.op("act", lambda e: e.activation(out=junk, in_=xb, func=AF.Square, accum_out=ss[:, b:b + 1]),
                 [B_xin], [B_junk, B_ss])
        rstd_from(ss[:, 0:nb], ss[:, 0:nb], 128, [B_ss], [B_ss], 1.0 / D)
        for b in range(nb):
            xb = xin[:, b * D:(b + 1) * D]
            K.op("dve", lambda e: e.scalar_tensor_tensor(out=xn[:, b * D:(b + 1) * D], in0=xb, scalar=ss[:, b:b + 1],
                                                         in1=g_rep, op0=ALU.mult, op1=ALU.mult),
                 [B_xin, B_ss, B_w], [B_xn], selfsync=True)
        for c in range(8):
            bk = nextbank()
            pb = banks[bk][:, 0:256].bitcast(BF16)
            for b in range(nb):
                K.op("pe", lambda e: e.transpose(out=pb[:, b * 128:(b + 1) * 128],
                                                 in_=xn[:, b * D + c * 128:b * D + (c + 1) * 128], identity=ident),
                     [B_xn, B_const], [bankB[bk]], inc=(b == nb - 1))
            eng = "act" if c % 2 == 0 else "dve"
            if eng == "act":
                K.op("act", lambda e: e.copy(out=xnT[:, c * Tn:(c + 1) * Tn], in_=pb[:, 0:Tn]), [bankB[bk]], [B_xnT])
            else:
                K.op("dve", lambda e: e.tensor_copy(out=xnT[:, c * Tn:(c + 1) * Tn], in_=pb[:, 0:Tn]), [bankB[bk]], [B_xnT])

    for l in range(NL):
        xsrc = x_in if l == 0 else xs[(l - 1) % 2]
        xdst = y_out if l == NL - 1 else xs[l % 2]
        skey = ("x", l)
        dkey = ("x", l + 1)

        if '1' in PASSES:
            K.barrier()
            P.reset()
            newbanks()
            nb = T1 // 128
            wA = P.bf16(8 * NWA)
            wpe = P.bf16(8 * 192)
            wco = P.bf16(6 * D)
            wso = P.bf16(6 * D)
            wsp = P.bf16(6 * 128)
            wuq = P.bf16(2 * NH * 96)
            wuqs = P.bf16(2 * NH * 96)
            wukn = P.bf16(NH * 64)
            wukv = P.bf16(NH * 64)
            bsp = P.bf16(6 * 128)
            ln1rep = P.f32(D)
            sgrep = P.f32(DC)
            cw = P.f32(6 * CK)
            cb = P.f32(6)
            cng = P.f32(6)
            cnb = P.f32(6)
            qng = P.f32(2)
            kvng = P.f32(1)
            gqt = P.f32(2)
            gkt = P.f32(2)
            B_w = Buf("w1")
            load_w_cast(wA, w_in[l][:, 3072:DIN], 8, NWA, B_w)
            K.dma("pool", wpe.rearrange("p (k n) -> p k n", n=192),
                  w_pe[l].rearrange("(k p) a n -> p k (a n)", p=128), [], [B_w], B_w)
            load_w_cast(wco, w_co[l], 6, D, B_w)
            load_w_cast(wso, w_so[l], 6, D, B_w)
            K.dma("pool", wsp, w_spT[l].rearrange("p g q -> p (g q)"), [], [B_w], B_w)
            load_w_cast(wuq, w_uq[l], 2, NH * 96, B_w)
            load_w_cast(wuqs, w_uqs[l], 2, NH * 96, B_w)
            K.dma("pool", wukn, w_ukn[l], [], [B_w], B_w)
            K.dma("pool", wukv, w_ukv[l], [], [B_w], B_w)
            K.dma("pool", bsp[0:1, :], b_sp[l], [], [B_w], B_w)
            K.dma("sp", ln1rep, ln1_g[l:l + 1, :].broadcast_to([128, D]), [], [B_w], B_w)
            K.dma("sp", sgrep, sg_g[l:l + 1, :].broadcast_to([128, DC]), [], [B_w], B_w)
            K.dma("sp", cw, conv_wT[l].rearrange("p c j -> p (c j)"), [], [B_w], B_w)
            K.dma("sp", cb, conv_b[l], [], [B_w], B_w)
            K.dma("sp", cng, cn_g[l], [], [B_w], B_w)
            K.dma("sp", cnb, cn_b[l], [], [B_w], B_w)
            K.dma("sp", qng, qn_g[l], [], [B_w], B_w)
            K.dma("sp", kvng, kvn_g[l], [], [B_w], B_w)
            K.dma("sp", gqt[0:96, :], gq[l], [], [B_w], B_w)
            K.dma("sp", gkt[0:96, :], gk[l], [], [B_w], B_w)
            K.op("dve", lambda e: e.tensor_single_scalar(out=gqt[0:96, :], in_=gqt[0:96, :], scalar=float(96 ** -0.5),
                                                         op=ALU.mult), [B_w], [B_w])

            xin = P.f32(nb * D)
            xn = P.bf16(nb * D)
            xnT = P.bf16(8 * T1)
            xh = P.f32(D)
            xnh = P.bf16(D)
            xnTh = P.bf16(8 * 32)
            ss = P.f32(4)
            ssh = P.f32(1)
            junk = P.bf16(D)
            zpad = P.f32(6 * 542)
            sgs = [P.f32(288) for _ in range(2)]
            acc = P.f32(6 * T1)
            sqb = [P.bf16(T1) for _ in range(2)]
            acb = [P.bf16(T1) for _ in range(2)]
            mean = P.f32(T1)
            rstd = P.f32(T1)
            tln = [P.f32(T1) for _ in range(2)]
            aconv = P.bf16(6 * T1)
            ubf = P.bf16(6 * T1)
            vf = P.f32(DC)
            ssv = P.f32(1)
            vn = P.bf16(nb * DC)
            uv = P.bf16(6 * T1)
            yst = [P.bf16(8 * T1) for _ in range(2)]
            qn = P.bf16(2 * T1)
            kvn = P.bf16(T1)
            rq = P.f32(T1)
            sqpe = P.bf16(T1)
            kr = P.f32(T1)
            t1s = [P.f32(T1) for _ in range(2)]
            t2s = [P.f32(T1) for _ in range(2)]
            sqh = [P.bf16(T1) for _ in range(2)]
            rh = [P.f32(T1) for _ in range(2)]
            qst = [P.bf16(T1) for _ in range(3)]
            kst = [P.bf16(T1) for _ in range(3)]
            vst = P.bf16(nb * DC)
            cosT = P.f32(T1)
            sinT = P.f32(T1)
            (B_xin, B_xn, B_xnT, B_xh, B_xnh, B_xnTh, B_ss, B_ssh, B_junk, B_mean, B_rstd, B_aconv, B_ubf, B_vf,
             B_ssv, B_vn, B_uv, B_qn, B_kvn, B_rq, B_sqpe, B_kr, B_vst, B_cs) = [Buf(f"p1_{i}") for i in range(24)]
            B_zpad = [Buf() for _ in range(6)]
            B_acc = [Buf() for _ in range(6)]
            B_sgs = [Buf() for _ in range(2)]
            B_sqb = [Buf() for _ in range(2)]
            B_acb = [Buf() for _ in range(2)]
            B_tln = [Buf() for _ in range(2)]
            B_yst = [Buf() for _ in range(2)]
            B_t1 = [Buf() for _ in range(2)]
            B_t2 = [Buf() for _ in range(2)]
            B_sqh = [Buf() for _ in range(2)]
            B_rh = [Buf() for _ in range(2)]
            B_qst = [Buf() for _ in range(3)]
            B_kst = [Buf() for _ in range(3)]
            rr = [0, 0, 0, 0]

            wA3 = wA.rearrange("p (k n) -> p k n", n=NWA)
            wpe3 = wpe.rearrange("p (k n) -> p k n", n=192)
            xnT3 = xnT.rearrange("p (k n) -> p k n", n=T1)
            xnTh3 = xnTh.rearrange("p (k n) -> p k n", n=32)

            def mm8(bk, cols, ncol, col0, M, lhs_src, n0, n, rhs3, first=True, last=True, wcol0=None):
                for k in range(8):
                    K.op("pe", lambda e: e.matmul(banks[bk][0:M, col0:col0 + n], lhsT=lhs_src[:, k, cols:cols + ncol],
                                                  rhs=rhs3[:, k, n0:n0 + n], start=(k == 0), stop=(k == 7)),
                         [B_w, B_xnT, B_xnTh], [bankB[bk]], inc=(k == 7 and last))

            for t in range(NT1):
                tok0 = t * T1
                norm_transposed(xsrc, tok0, nb, ln1rep, xin, xn, xnT, B_xin, B_xn, B_xnT, B_w, (skey, tok0 // T), ss, B_ss,
                                junk, B_junk)
                if t > 0:
                    K.dma("sp", xh[0:15, :], xsrc[tok0 - 15:tok0, :], [dbuf((skey, (tok0 - 15) // T))], [B_xh], B_xh)
                else:
                    K.dma("sp", xh[0:15, :], zrows_d[0:15, :], [], [B_xh], B_xh)
                if t < NT1 - 1:
                    K.dma("sp", xh[15:30, :], xsrc[tok0 + T1:tok0 + T1 + 15, :], [dbuf((skey, (tok0 + T1) // T))], [B_xh], B_xh)
                else:
                    K.dma("sp", xh[15:30, :], zrows_d[0:15, :], [], [B_xh], B_xh)
                K.dma("sp", cosT[0:96, :], cosT_d[:, tok0:tok0 + T1], [], [B_cs], B_cs)
                K.dma("sp", sinT[0:96, :], sinT_d[:, tok0:tok0 + T1], [], [B_cs], B_cs)
                K.op("act", lambda e: e.activation(out=junk[0:30, :], in_=xh[0:30, :], func=AF.Square, accum_out=ssh[0:30, :]),
                     [B_xh], [B_junk, B_ssh])
                rstd_from(ssh[0:30, :], ssh[0:30, :], 30, [B_ssh], [B_ssh], 1.0 / D)
                K.op("dve", lambda e: e.tensor_tensor(out=ssh[0:30, :], in0=ssh[0:30, :], in1=hflag[0:30, t:t + 1], op=ALU.mult),
                     [B_ssh, B_const], [B_ssh], selfsync=True)
                K.op("dve", lambda e: e.scalar_tensor_tensor(out=xnh[0:30, :], in0=xh[0:30, :], scalar=ssh[0:30, 0:1],
                                                             in1=ln1rep[0:30, :], op0=ALU.mult, op1=ALU.mult),
                     [B_xh, B_ssh, B_w], [B_xnh], selfsync=True)
                bk = nextbank()
                pb = banks[bk][:, 0:128].bitcast(BF16)
                for c in range(8):
                    K.op("pe", lambda e: e.transpose(out=pb[:, c * 32:c * 32 + 30], in_=xnh[0:30, c * 128:(c + 1) * 128],
                                                     identity=ident[0:30, 0:30]),
                         [B_xnh, B_const], [bankB[bk]], inc=(c == 7))
                K.op("dve", lambda e: e.tensor_copy(out=xnTh3[:, :, 0:30], in_=pb.rearrange("p (k n) -> p k n", n=32)[:, :, 0:30]),
                     [bankB[bk]], [B_xnTh])

                for c in range(6):
                    ba = nextbank()
                    mm8(ba, c * 128, 128, 0, 128, wA3, 0, T1, xnT3)
                    mm8(ba, c * 128, 128, T1, 128, wA3, 0, 30, xnTh3)
                    bg = nextbank()
                    mm8(bg, 768 + c * 128, 128, 0, 128, wA3, 0, T1, xnT3)
                    mm8(bg, 768 + c * 128, 128, T1, 128, wA3, 0, 30, xnTh3)
                    si = rr[0] % 2
                    rr[0] += 1
                    K.op("act", lambda e: e.activation(out=sgs[si][:, 0:T1 + 30], in_=banks[bg][:, 0:T1 + 30], func=AF.Sigmoid),
                         [bankB[bg]], [B_sgs[si]])
                    zc = zpad[:, c * 542:(c + 1) * 542]
                    K.op("dve", lambda e: e.tensor_tensor(out=zc[:, 15:15 + T1], in0=banks[ba][:, 0:T1], in1=sgs[si][:, 0:T1],
                                                          op=ALU.mult), [bankB[ba], B_sgs[si]], [B_zpad[c]])
                    K.op("dve", lambda e: e.tensor_tensor(
                        out=zc.rearrange("p (a b) -> p a b", b=271)[:, :, 0:15],
                        in0=banks[ba][:, T1:T1 + 30].rearrange("p (a b) -> p a b", b=15),
                        in1=sgs[si][:, T1:T1 + 30].rearrange("p (a b) -> p a b", b=15), op=ALU.mult),
                        [bankB[ba], B_sgs[si]], [B_zpad[c]])
                for c in range(6):
                    en = "dve"
                    zc = zpad[:, c * 542:(c + 1) * 542]
                    ac = acc[:, c * T1:(c + 1) * T1]
                    K.op(en, lambda e: e.tensor_scalar(out=ac, in0=zc[:, 0:T1], scalar1=cw[:, c * CK:c * CK + 1],
                                                       scalar2=cb[:, c:c + 1], op0=ALU.mult, op1=ALU.add),
                         [B_zpad[c], B_w], [B_acc[c]])
                    for j in range(1, CK):
                        if en == "dve":
                            K.op(en, lambda e: e.scalar_tensor_tensor(out=ac, in0=zc[:, j:j + T1],
                                                                      scalar=cw[:, c * CK + j:c * CK + j + 1],
                                                                      in1=ac, op0=ALU.mult, op1=ALU.add),
                                 [B_zpad[c], B_w], [B_acc[c]])
                        else:
                            K.op(en, lambda e: e.tensor_scalar(out=tln[0], in0=zc[:, j:j + T1], scalar1=cw[:, c * CK + j:c * CK + j + 1],
                                                               scalar2=None, op0=ALU.mult), [B_zpad[c], B_w], [B_tln[0]])
                            K.op(en, lambda e: e.tensor_tensor(out=ac, in0=ac, in1=tln[0], op=ALU.add), [B_tln[0]], [B_acc[c]])
                b1 = nextbank()
                b2 = nextbank()
                for c in range(6):
                    ac = acc[:, c * T1:(c + 1) * T1]
                    si = rr[1] % 2
                    rr[1] += 1
                    K.op("act", lambda e: e.activation(out=sqb[si], in_=ac, func=AF.Square), [B_acc[c]], [B_sqb[si]])
                    K.op("pool", lambda e: e.tensor_copy(out=acb[si], in_=ac), [B_acc[c]], [B_acb[si]])
                    K.op("pe", lambda e: e.matmul(banks[b1][:, 0:T1], lhsT=ones, rhs=acb[si], start=(c == 0), stop=(c == 5)),
                         [B_const, B_acb[si]], [bankB[b1]], inc=True)
                    K.op("pe", lambda e: e.matmul(banks[b2][:, 0:T1], lhsT=ones, rhs=sqb[si], start=(c == 0), stop=(c == 5)),
                         [B_const, B_sqb[si]], [bankB[b2]], inc=True)
                K.op("act", lambda e: e.mul(out=mean, in_=banks[b1][:, 0:T1], mul=1.0 / DC), [bankB[b1]], [B_mean])
                K.op("dve", lambda e: e.tensor_tensor(out=rstd, in0=mean, in1=mean, op=ALU.mult), [B_mean], [B_rstd])
                K.op("dve", lambda e: e.scalar_tensor_tensor(out=rstd, in0=banks[b2][:, 0:T1], scalar=1.0 / DC, in1=rstd,
                                                             op0=ALU.mult, op1=ALU.subtract), [bankB[b2], B_rstd], [B_rstd])
                rstd_from(rstd, rstd, 128, [B_rstd], [B_rstd], 1.0)
                for c in range(6):
                    ac = acc[:, c * T1:(c + 1) * T1]
                    si = rr[2] % 2
                    rr[2] += 1
                    K.op("pool", lambda e: e.tensor_tensor(out=tln[si], in0=ac, in1=mean, op=ALU.subtract),
                         [B_acc[c], B_mean], [B_tln[si]])
                    K.op("pool", lambda e: e.tensor_tensor(out=tln[si], in0=tln[si], in1=rstd, op=ALU.mult),
                         [B_tln[si], B_rstd], [B_tln[si]])
                    K.op("act", lambda e: e.activation(out=aconv[:, c * T1:(c + 1) * T1], in_=tln[si], func=AF.Silu,
                                                       scale=cng[:, c:c + 1], bias=cnb[:, c:c + 1]),
                         [B_tln[si], B_w], [B_aconv])

                def out_proj(src, B_src, w, dstT, key):
                    w3 = w.rearrange("p (k n) -> p k n", n=D)
                    yi = rr[3] % 2
                    rr[3] += 1
                    for oc in range(8):
                        bk = nextbank()
                        for c in range(6):
                            K.op("pe", lambda e: e.matmul(banks[bk][:, 0:T1], lhsT=w3[:, c, oc * 128:(oc + 1) * 128],
                                                          rhs=src[:, c * T1:(c + 1) * T1], start=(c == 0), stop=(c == 5)),
                                 [B_w, B_src], [bankB[bk]], inc=(c == 5))
                        if oc % 2 == 0:
                            K.op("act", lambda e: e.copy(out=yst[yi][:, oc * T1:(oc + 1) * T1], in_=banks[bk][:, 0:T1]),
                                 [bankB[bk]], [B_yst[yi]])
                        else:
                            K.op("dve", lambda e: e.tensor_copy(out=yst[yi][:, oc * T1:(oc + 1) * T1], in_=banks[bk][:, 0:T1]),
                                 [bankB[bk]], [B_yst[yi]])
                    K.dma("sp", dstT.rearrange("(c p) n -> p c n", p=128)[:, :, tok0:tok0 + T1],
                          yst[yi].rearrange("p (c n) -> p c n", n=T1), [B_yst[yi]], [dbuf((key, l, tok0 // T))], B_yst[yi])

                out_proj(aconv, B_aconv, wco, yaT, "ya")

                for c in range(6):
                    bk = nextbank()
                    mm8(bk, 1536 + c * 128, 128, 0, 128, wA3, 0, T1, xnT3)
                    K.op("act", lambda e: e.activation(out=ubf[:, c * T1:(c + 1) * T1], in_=banks[bk][:, 0:T1],
                                                       func=AF.Gelu_apprx_tanh), [bankB[bk]], [B_ubf])
                for b in range(nb):
                    bA = nextbank()
                    bB = nextbank()
                    for k in range(8):
                        K.op("pe", lambda e: e.matmul(banks[bA][:, 0:512], lhsT=xnT3[:, k, b * 128:(b + 1) * 128],
                                                      rhs=wA3[:, k, 2304:2816], start=(k == 0), stop=(k == 7)),
                             [B_w, B_xnT], [bankB[bA]], inc=(k == 7))
                    for k in range(8):
                        K.op("pe", lambda e: e.matmul(banks[bB][:, 0:256], lhsT=xnT3[:, k, b * 128:(b + 1) * 128],
                                                      rhs=wA3[:, k, 2816:3072], start=(k == 0), stop=(k == 7)),
                             [B_w, B_xnT], [bankB[bB]], inc=(k == 7))
                    K.op("act", lambda e: e.activation(out=vf[:, 0:512], in_=banks[bA][:, 0:512], func=AF.Gelu_apprx_tanh),
                         [bankB[bA]], [B_vf])
                    K.op("act", lambda e: e.activation(out=vf[:, 512:768], in_=banks[bB][:, 0:256], func=AF.Gelu_apprx_tanh),
                         [bankB[bB]], [B_vf])
                    K.op("act", lambda e: e.activation(out=junk[:, 0:DC], in_=vf, func=AF.Square, accum_out=ssv),
                         [B_vf], [B_junk, B_ssv])
                    rstd_from(ssv, ssv, 128, [B_ssv], [B_ssv], 1.0 / DC)
                    K.op("dve", lambda e: e.scalar_tensor_tensor(out=vn[:, b * DC:(b + 1) * DC], in0=vf, scalar=ssv[:, 0:1],
                                                                 in1=sgrep, op0=ALU.mult, op1=ALU.mult),
                         [B_vf, B_ssv, B_w], [B_vn], selfsync=True)
                for g in range(6):
                    bk = nextbank()
                    for b in range(nb):
                        K.op("pe", lambda e: e.matmul(banks[bk][:, b * 128:(b + 1) * 128], lhsT=ones[0:1, :],
                                                      rhs=bsp[0:1, g * 128:(g + 1) * 128], start=True, stop=False),
                             [B_const, B_w], [bankB[bk]], inc=False)
                        K.op("pe", lambda e: e.matmul(banks[bk][:, b * 128:(b + 1) * 128],
                                                      lhsT=vn[:, b * DC + g * 128:b * DC + (g + 1) * 128],
                                                      rhs=wsp[:, g * 128:(g + 1) * 128], start=False, stop=True),
                             [B_vn, B_w], [bankB[bk]], inc=(b == nb - 1))
                    K.op("dve", lambda e: e.tensor_tensor(out=uv[:, g * T1:(g + 1) * T1], in0=banks[bk][:, 0:T1],
                                                          in1=ubf[:, g * T1:(g + 1) * T1], op=ALU.mult),
                         [bankB[bk], B_ubf], [B_uv])
                out_proj(uv, B_uv, wso, ybT, "yb")

                bq = [nextbank(), nextbank()]
                for c in range(2):
                    mm8(bq[c], 3072 + c * 128, 128, 0, 128, wA3, 0, T1, xnT3)
                bss = nextbank()
                for c in range(2):
                    si = rr[1] % 2
                    rr[1] += 1
                    K.op("act", lambda e: e.activation(out=sqb[si], in_=banks[bq[c]][:, 0:T1], func=AF.Square),
                         [bankB[bq[c]]], [B_sqb[si]])
                    K.op("pe", lambda e: e.matmul(banks[bss][:, 0:T1], lhsT=ones, rhs=sqb[si], start=(c == 0), stop=(c == 1)),
                         [B_const, B_sqb[si]], [bankB[bss]], inc=True)
                rstd_from(banks[bss][:, 0:T1], rq, 128, [bankB[bss]], [B_rq], 1.0 / 256)
                for c in range(2):
                    K.op("dve", lambda e: e.scalar_tensor_tensor(out=qn[:, c * T1:(c + 1) * T1], in0=banks[bq[c]][:, 0:T1],
                                                                 scalar=qng[:, c:c + 1], in1=rq, op0=ALU.mult, op1=ALU.mult),
                         [bankB[bq[c]], B_rq, B_w], [B_qn])
                bkv = nextbank()
                mm8(bkv, 3328, 128, 0, 128, wA3, 0, T1, xnT3)
                bss = nextbank()
                si = rr[1] % 2
                rr[1] += 1
                K.op("act", lambda e: e.activation(out=sqb[si], in_=banks[bkv][:, 0:T1], func=AF.Square),
                     [bankB[bkv]], [B_sqb[si]])
                K.op("pe", lambda e: e.matmul(banks[bss][:, 0:T1], lhsT=ones, rhs=sqb[si], start=True, stop=True),
                     [B_const, B_sqb[si]], [bankB[bss]], inc=True)
                rstd_from(banks[bss][:, 0:T1], rq, 128, [bankB[bss]], [B_rq], 1.0 / 128)
                K.op("dve", lambda e: e.scalar_tensor_tensor(out=kvn, in0=banks[bkv][:, 0:T1], scalar=kvng[:, 0:1], in1=rq,
                                                             op0=ALU.mult, op1=ALU.mult), [bankB[bkv], B_rq, B_w], [B_kvn])
                bpe = nextbank()
                mm8(bpe, 0, 96, 0, 96, wpe3, 0, T1, xnT3)
                bpes = nextbank()
                mm8(bpes, 96, 96, 0, 96, wpe3, 0, T1, xnT3)
                K.op("act", lambda e: e.activation(out=sqpe[64:96, :], in_=banks[bpe][64:96, 0:T1], func=AF.Square),
                     [bankB[bpe]], [B_sqpe])
                K.op("dve", lambda e: e.scalar_tensor_tensor(out=kr[64:96, :], in0=banks[bpe][64:96, 0:T1], scalar=gkt[64:96, 0:1],
                                                             in1=cosT[64:96, :], op0=ALU.mult, op1=ALU.mult),
                     [bankB[bpe], B_w, B_cs], [B_kr])
                K.op("dve", lambda e: e.scalar_tensor_tensor(out=t1s[0][64:96, :], in0=banks[bpes][64:96, 0:T1],
                                                             scalar=gkt[64:96, 1:2], in1=sinT[64:96, :], op0=ALU.mult, op1=ALU.mult),
                     [bankB[bpes], B_w, B_cs], [B_t1[0]])
                K.op("dve", lambda e: e.tensor_tensor(out=kr[64:96, :], in0=kr[64:96, :], in1=t1s[0][64:96, :], op=ALU.add),
                     [B_kr, B_t1[0]], [B_kr])
                for b in range(nb):
                    bA = nextbank()
                    bB = nextbank()
                    K.op("pe", lambda e: e.matmul(banks[bA][:, 0:512], lhsT=kvn[:, b * 128:(b + 1) * 128], rhs=wukv[:, 0:512],
                                                  start=True, stop=True), [B_kvn, B_w], [bankB[bA]])
                    K.op("pe", lambda e: e.matmul(banks[bB][:, 0:256], lhsT=kvn[:, b * 128:(b + 1) * 128], rhs=wukv[:, 512:768],
                                                  start=True, stop=True), [B_kvn, B_w], [bankB[bB]])
                    K.op("act", lambda e: e.copy(out=vst[:, b * DC:b * DC + 512], in_=banks[bA][:, 0:512]), [bankB[bA]], [B_vst])
                    K.op("dve", lambda e: e.tensor_copy(out=vst[:, b * DC + 512:(b + 1) * DC], in_=banks[bB][:, 0:256]),
                         [bankB[bB]], [B_vst])
                K.dma("sp", v_d[tok0:tok0 + T1, :].rearrange("(b p) d -> p b d", p=128), vst.rearrange("p (b d) -> p b d", d=DC),
                      [B_vst], [dbuf(("v", l))], B_vst)
                wuq3 = wuq.rearrange("p (k n) -> p k n", n=NH * 96)
                wuqs3 = wuqs.rearrange("p (k n) -> p k n", n=NH * 96)
                for h in range(NH):
                    i2 = h % 2
                    i3 = h % 3
                    bkn = nextbank()
                    K.op("pe", lambda e: e.matmul(banks[bkn][0:64, 0:T1], lhsT=wukn[:, h * 64:(h + 1) * 64], rhs=kvn,
                                                  start=True, stop=True), [B_w, B_kvn], [bankB[bkn]])
                    K.op("act", lambda e: e.activation(out=sqh[i2][0:64, :], in_=banks[bkn][0:64, 0:T1], func=AF.Square),
                         [bankB[bkn]], [B_sqh[i2]])
                    bs = nextbank()
                    K.op("pe", lambda e: e.matmul(banks[bs][0:96, 0:T1], lhsT=ones[0:64, 0:96], rhs=sqh[i2][0:64, :],
                                                  start=True, stop=False), [B_const, B_sqh[i2]], [bankB[bs]], inc=False)
                    K.op("pe", lambda e: e.matmul(banks[bs][0:96, 0:T1], lhsT=ones[64:96, 0:96], rhs=sqpe[64:96, :],
                                                  start=False, stop=True), [B_const, B_sqpe], [bankB[bs]])
                    rstd_from(banks[bs][0:96, 0:T1], rh[i2][0:96, :], 96, [bankB[bs]], [B_rh[i2]], 1.0 / 96)
                    K.op("dve", lambda e: e.scalar_tensor_tensor(out=kst[i3][0:64, :], in0=banks[bkn][0:64, 0:T1],
                                                                 scalar=gkt[0:64, 0:1], in1=rh[i2][0:64, :],
                                                                 op0=ALU.mult, op1=ALU.mult),
                         [bankB[bkn], B_w, B_rh[i2]], [B_kst[i3]])
                    K.op("pool", lambda e: e.tensor_tensor(out=kst[i3][64:96, :], in0=kr[64:96, :], in1=rh[i2][64:96, :],
                                                           op=ALU.mult), [B_kr, B_rh[i2]], [B_kst[i3]])
                    K.dma("sp", kT_d[h, :, tok0:tok0 + T1], kst[i3][0:96, :], [B_kst[i3]], [dbuf(("k", l, h))], B_kst[i3])
                    bqh = nextbank()
                    bqs = nextbank()
                    for c in range(2):
                        K.op("pe", lambda e: e.matmul(banks[bqh][0:96, 0:T1], lhsT=wuq3[:, c, h * 96:(h + 1) * 96],
                                                      rhs=qn[:, c * T1:(c + 1) * T1], start=(c == 0), stop=(c == 1)),
                             [B_w, B_qn], [bankB[bqh]], inc=(c == 1))
                    for c in range(2):
                        K.op("pe", lambda e: e.matmul(banks[bqs][0:96, 0:T1], lhsT=wuqs3[:, c, h * 96:(h + 1) * 96],
                                                      rhs=qn[:, c * T1:(c + 1) * T1], start=(c == 0), stop=(c == 1)),
                             [B_w, B_qn], [bankB[bqs]], inc=(c == 1))
                    j2 = (h + 1) % 2
                    K.op("act", lambda e: e.activation(out=sqh[j2][0:96, :], in_=banks[bqh][0:96, 0:T1], func=AF.Square),
                         [bankB[bqh]], [B_sqh[j2]])
                    bs = nextbank()
                    K.op("pe", lambda e: e.matmul(banks[bs][0:96, 0:T1], lhsT=ones[0:96, 0:96], rhs=sqh[j2][0:96, :],
                                                  start=True, stop=True), [B_const, B_sqh[j2]], [bankB[bs]])
                    rstd_from(banks[bs][0:96, 0:T1], rh[j2][0:96, :], 96, [bankB[bs]], [B_rh[j2]], 1.0 / 96)
                    K.op("dve", lambda e: e.scalar_tensor_tensor(out=t1s[i2][0:96, :], in0=banks[bqh][0:96, 0:T1],
                                                                 scalar=gqt[0:96, 0:1], in1=cosT[0:96, :],
                                                                 op0=ALU.mult, op1=ALU.mult),
                         [bankB[bqh], B_w, B_cs], [B_t1[i2]])
                    K.op("dve", lambda e: e.scalar_tensor_tensor(out=t2s[i2][0:96, :], in0=banks[bqs][0:96, 0:T1],
                                                                 scalar=gqt[0:96, 1:2], in1=sinT[0:96, :],
                                                                 op0=ALU.mult, op1=ALU.mult),
                         [bankB[bqs], B_w, B_cs], [B_t2[i2]])
                    K.op("pool", lambda e: e.tensor_tensor(out=t1s[i2][0:96, :], in0=t1s[i2][0:96, :], in1=t2s[i2][0:96, :],
                                                           op=ALU.add), [B_t1[i2], B_t2[i2]], [B_t1[i2]])
                    K.op("pool", lambda e: e.tensor_tensor(out=qst[i3][0:96, :], in0=t1s[i2][0:96, :], in1=rh[j2][0:96, :],
                                                           op=ALU.mult), [B_t1[i2], B_rh[j2]], [B_qst[i3]])
                    K.dma("sp", qT_d[h, :, tok0:tok0 + T1], qst[i3][0:96, :], [B_qst[i3]], [dbuf(("q", l, h))], B_qst[i3])

        if '2' in PASSES:
            K.barrier()
            P.reset()
            newbanks(p2=True)
            kT = [P.bf16(NTOK) for _ in range(2)]
            vA = [P.bf16(NKB * 128) for _ in range(2)]
            qT = [P.bf16(T) for _ in range(3)]
            pT = [P.bf16(2 * T) for _ in range(3)]
            rinv = [P.f32(T) for _ in range(2)]
            ost = [P.bf16(T) for _ in range(2)]
            B_kT = [Buf() for _ in range(2)]
            B_vA = [Buf() for _ in range(2)]
            B_qT = [Buf() for _ in range(3)]
            B_pT = [Buf() for _ in range(3)]
            B_sb = [Buf() for _ in range(3)]
            B_ob = [Buf() for _ in range(2)]
            B_rinv = [Buf() for _ in range(2)]
            B_ost = [Buf() for _ in range(2)]
            for i in range(2):
                K.dma("pool", kT[i][96:100, :], kmask_d, [], [B_kT[i]], B_kT[i])
                K.op("dve", lambda e: e.memset(vA[i].rearrange("p (k d) -> p k d", d=128)[:, :, 64:128], 1.0), [], [B_vA[i]])
            NP = NKB // 2
            NTILE = NH * NT

            def load_head(h):
                hs = h % 2
                K.dma("sp", kT[hs][0:96, :], kT_d[h], [dbuf(("k", l, h))], [B_kT[hs]], B_kT[hs])
                for part in range(0, NKB, 16):
                    pe_ = min(NKB, part + 16)
                    K.dma("sp", vA[hs].rearrange("p (k d) -> p k d", d=128)[:, part:pe_, 0:64],
                          v_d[part * 128:pe_ * 128, h * 64:(h + 1) * 64].rearrange("(k p) d -> p k d", p=128),
                          [dbuf(("v", l))], [B_vA[hs]], B_vA[hs])

            def load_q(n):
                h, qt = divmod(n, NT)
                qs = n % 3
                K.dma("sp", qT[qs][0:96, :], qT_d[h, :, qt * T:(qt + 1) * T], [dbuf(("q", l, h))], [B_qT[qs]], B_qT[qs])
                K.dma("pool", qT[qs][96:100, :], qmask_d[:, qt * T:(qt + 1) * T], [], [B_qT[qs]], B_qT[qs])

            def S_(i):
                n, p = divmod(i, NP)
                h, qt = divmod(n, NT)
                hs = h % 2
                qs = n % 3
                if p == 0:
                    if n + 1 < NTILE:
                        load_q(n + 1)
                    if qt == NT - 1 and h + 1 < NH:
                        load_head(h + 1)
                sb = i % 3
                for j in range(2):
                    kb = 2 * p + j
                    K.op("pe", lambda e: e.matmul(sbig[sb][:, j * T:(j + 1) * T], lhsT=kT[hs][0:100, kb * 128:(kb + 1) * 128],
                                                  rhs=qT[qs][0:100, :], start=True, stop=True),
                         [B_kT[hs], B_qT[qs]], [B_sb[sb]], inc=(j == 1))

            def PV_(i):
                n, p = divmod(i, NP)
                h, qt = divmod(n, NT)
                hs = h % 2
                sb = i % 3
                ob = n % 2
                vA3 = vA[hs].rearrange("p (k d) -> p k d", d=128)
                K.op("act", lambda e: e.activation(out=pT[sb], in_=sbig[sb][:, 0:2 * T], func=AF.Exp), [B_sb[sb]], [B_pT[sb]])
                for j in range(2):
                    kb = 2 * p + j
                    K.op("pe", lambda e: e.matmul(obank[ob][:, 0:T], lhsT=vA3[:, kb, :], rhs=pT[sb][:, j * T:(j + 1) * T],
                                                  start=(kb == 0), stop=(kb == NKB - 1)),
                         [B_vA[hs], B_pT[sb]], [B_ob[ob]], inc=(j == 1))
                if p == NP - 1:
                    K.op("dve", lambda e: e.reciprocal(out=rinv[ob][0:64, :], in_=obank[ob][64:128, 0:T]),
                         [B_ob[ob]], [B_rinv[ob]])
                    K.op("dve", lambda e: e.tensor_tensor(out=ost[ob][0:64, :], in0=obank[ob][0:64, 0:T], in1=rinv[ob][0:64, :],
                                                          op=ALU.mult), [B_ob[ob], B_rinv[ob]], [B_ost[ob]])
                    K.dma("sp", oT_d[h * 64:(h + 1) * 64, qt * T:(qt + 1) * T], ost[ob][0:64, :], [B_ost[ob]],
                          [dbuf(("o", l, qt))], B_ost[ob])

            load_head(0)
            load_q(0)
            NI = NTILE * NP
            LOOK = 2
            for i in range(min(LOOK, NI)):
                S_(i)
            for i in range(NI):
                if i + LOOK < NI:
                    S_(i + LOOK)
                PV_(i)

        if '3' in PASSES:
            K.barrier()
            P.reset()
            newbanks()
            nb = T // 128
            wg = P.bf16(8 * 3072)
            wo = P.bf16(6 * D)
            wout = P.bf16(8 * D)
            ln1rep = P.f32(D)
            bg = P.f32(24)
            B_w = Buf("w3")
            load_w_cast(wg, w_in[l][:, 0:3072], 8, 3072, B_w)
            load_w_cast(wo, w_o[l], 6, D, B_w)
            load_w_cast(wout, w_out[l], 8, D, B_w)
            K.dma("sp", ln1rep, ln1_g[l:l + 1, :].broadcast_to([128, D]), [], [B_w], B_w)
            K.dma("sp", bg, b_gate[l], [], [B_w], B_w)
            xin = P.f32(nb * D)
            xn = P.bf16(nb * D)
            xnT = P.bf16(8 * T)
            ss = P.f32(4)
            junk = P.bf16(D)
            oT = P.bf16(6 * T)
            ya = P.bf16(8 * T)
            yb = P.bf16(8 * T)
            mg = P.bf16(8 * T)
            gs = [[P.bf16(T) for _ in range(3)] for _ in range(2)]
            m1 = [P.f32(T) for _ in range(2)]
            m2 = [P.f32(T) for _ in range(2)]
            B_xin, B_xn, B_xnT, B_ss, B_junk, B_oT, B_ya, B_yb, B_mg = [Buf() for _ in range(9)]
            B_gs = [[Buf() for _ in range(3)] for _ in range(2)]
            B_m1 = [Buf() for _ in range(2)]
            B_m2 = [Buf() for _ in range(2)]
            wg3 = wg.rearrange("p (k n) -> p k n", n=3072)
            wo3 = wo.rearrange("p (k n) -> p k n", n=D)
            wout3 = wout.rearrange("p (k n) -> p k n", n=D)
            xnT3 = xnT.rearrange("p (k n) -> p k n", n=T)
            for t in range(NT):
                tok0 = t * T
                norm_transposed(xsrc, tok0, nb, ln1rep, xin, xn, xnT, B_xin, B_xn, B_xnT, B_w, (skey, t), ss, B_ss, junk, B_junk)
                K.dma("sp", oT.rearrange("p (c n) -> p c n", n=T), oT_d.rearrange("(c p) n -> p c n", p=128)[:, :, tok0:tok0 + T],
                      [dbuf(("o", l, t))], [B_oT], B_oT)
                K.dma("sp", ya.rearrange("p (c n) -> p c n", n=T), yaT.rearrange("(c p) n -> p c n", p=128)[:, :, tok0:tok0 + T],
                      [dbuf(("ya", l, t))], [B_ya], B_ya)
                K.dma("sp", yb.rearrange("p (c n) -> p c n", n=T), ybT.rearrange("(c p) n -> p c n", p=128)[:, :, tok0:tok0 + T],
                      [dbuf(("yb", l, t))], [B_yb], B_yb)
                for c in range(8):
                    i2 = c % 2
                    gb = []
                    for j in range(3):
                        bk = nextbank()
                        gb.append(bk)
                        for k in range(8):
                            K.op("pe", lambda e: e.matmul(banks[bk][:, 0:T], lhsT=wg3[:, k, j * D + c * 128:j * D + (c + 1) * 128],
                                                          rhs=xnT3[:, k, :], start=(k == 0), stop=(k == 7)),
                                 [B_w, B_xnT], [bankB[bk]], inc=(k == 7))
                        K.op("act", lambda e: e.activation(out=gs[i2][j], in_=banks[bk][:, 0:T], func=AF.Sigmoid,
                                                           bias=bg[:, j * 8 + c:j * 8 + c + 1]),
                             [bankB[bk], B_w], [B_gs[i2][j]])
                    bk = nextbank()
                    for k in range(6):
                        K.op("pe", lambda e: e.matmul(banks[bk][:, 0:T], lhsT=wo3[:, k, c * 128:(c + 1) * 128],
                                                      rhs=oT[:, k * T:(k + 1) * T], start=(k == 0), stop=(k == 5)),
                             [B_w, B_oT], [bankB[bk]], inc=(k == 5))
                    K.op("pool", lambda e: e.tensor_tensor(out=m1[i2], in0=gs[i2][0], in1=ya[:, c * T:(c + 1) * T], op=ALU.mult),
                         [B_gs[i2][0], B_ya], [B_m1[i2]])
                    K.op("pool", lambda e: e.tensor_tensor(out=m2[i2], in0=gs[i2][1], in1=yb[:, c * T:(c + 1) * T], op=ALU.mult),
                         [B_gs[i2][1], B_yb], [B_m2[i2]])
                    K.op("pool", lambda e: e.tensor_tensor(out=m1[i2], in0=m1[i2], in1=m2[i2], op=ALU.add),
                         [B_m1[i2], B_m2[i2]], [B_m1[i2]])
                    K.op("dve", lambda e: e.tensor_tensor(out=m2[i2], in0=banks[bk][:, 0:T], in1=gs[i2][2], op=ALU.mult),
                         [bankB[bk], B_gs[i2][2], B_m1[i2]], [B_m2[i2]])
                    K.op("dve", lambda e: e.tensor_tensor(out=mg[:, c * T:(c + 1) * T], in0=m1[i2], in1=m2[i2], op=ALU.add),
                         [B_m1[i2], B_m2[i2]], [B_mg])
                for b in range(nb):
                    for hf in range(2):
                        bk = nextbank()
                        for c in range(8):
                            K.op("pe", lambda e: e.matmul(banks[bk][:, 0:512], lhsT=mg[:, c * T + b * 128:c * T + (b + 1) * 128],
                                                          rhs=wout3[:, c, hf * 512:(hf + 1) * 512], start=(c == 0), stop=(c == 7)),
                                 [B_mg, B_w], [bankB[bk]], inc=(c == 7))
                        K.op("dve", lambda e: e.tensor_tensor(out=xin[:, b * D + hf * 512:b * D + (hf + 1) * 512],
                                                              in0=banks[bk][:, 0:512],
                                                              in1=xin[:, b * D + hf * 512:b * D + (hf + 1) * 512], op=ALU.add),
                             [bankB[bk], B_xin], [B_xin])
                K.dma("sp", xmid[tok0:tok0 + T, :].rearrange("(b p) d -> p b d", p=128), xin.rearrange("p (b d) -> p b d", d=D),
                      [B_xin], [dbuf(("xmid", l, t))], B_xin)

        if '4' in PASSES:
            K.barrier()
            P.reset()
            newbanks()
            wfi = P.bf16(8 * 2 * DFF)
            wfo = P.bf16(22 * D)
            ln2rep = P.f32(D)
            B_w = Buf("w4")
            wfi3 = wfi.rearrange("p (k n) -> p k n", n=2 * DFF)
            for k in range(8):
                K.dma("pool", wfi3[:, k, :], w_fi[l][k * 128:(k + 1) * 128, :], [], [B_w], B_w)
            load_w_cast(wfo, w_fo[l], 22, D, B_w)
            K.dma("sp", ln2rep, ln2_g[l:l + 1, :].broadcast_to([128, D]), [], [B_w], B_w)
            xin = P.f32(nb * D)
            xn = P.bf16(nb * D)
            xnT = P.bf16(8 * T)
            ss = P.f32(4)
            junk = P.bf16(D)
            aa = P.bf16(22 * T)
            sl = [P.f32(T) for _ in range(2)]
            B_xin, B_xn, B_xnT, B_ss, B_junk, B_aa = [Buf() for _ in range(6)]
            B_sl = [Buf() for _ in range(2)]
            wfo3 = wfo.rearrange("p (k n) -> p k n", n=D)
            xnT3 = xnT.rearrange("p (k n) -> p k n", n=T)
            for t in range(NT):
                tok0 = t * T
                norm_transposed(xmid, tok0, nb, ln2rep, xin, xn, xnT, B_xin, B_xn, B_xnT, B_w, ("xmid", l, t), ss, B_ss,
                                junk, B_junk)
                for j in range(22):
                    i2 = j % 2
                    bi = nextbank()
                    bgt = nextbank()
                    for k in range(8):
                        K.op("pe", lambda e: e.matmul(banks[bi][:, 0:T], lhsT=wfi3[:, k, j * 128:(j + 1) * 128], rhs=xnT3[:, k, :],
                                                      start=(k == 0), stop=(k == 7)), [B_w, B_xnT], [bankB[bi]], inc=(k == 7))
                    for k in range(8):
                        K.op("pe", lambda e: e.matmul(banks[bgt][:, 0:T], lhsT=wfi3[:, k, DFF + j * 128:DFF + (j + 1) * 128],
                                                      rhs=xnT3[:, k, :], start=(k == 0), stop=(k == 7)),
                             [B_w, B_xnT], [bankB[bgt]], inc=(k == 7))
                    K.op("act", lambda e: e.activation(out=sl[i2], in_=banks[bgt][:, 0:T], func=AF.Silu), [bankB[bgt]], [B_sl[i2]])
                    K.op("dve", lambda e: e.tensor_tensor(out=aa[:, j * T:(j + 1) * T], in0=banks[bi][:, 0:T], in1=sl[i2],
                                                          op=ALU.mult), [bankB[bi], B_sl[i2]], [B_aa])
                for b in range(nb):
                    for hf in range(2):
                        bk = nextbank()
                        for j in range(22):
                            K.op("pe", lambda e: e.matmul(banks[bk][:, 0:512], lhsT=aa[:, j * T + b * 128:j * T + (b + 1) * 128],
                                                          rhs=wfo3[:, j, hf * 512:(hf + 1) * 512], start=(j == 0), stop=(j == 21)),
                                 [B_aa, B_w], [bankB[bk]], inc=(j == 21))
                        K.op("dve", lambda e: e.tensor_tensor(out=xin[:, b * D + hf * 512:b * D + (hf + 1) * 512],
                                                              in0=banks[bk][:, 0:512],
                                                              in1=xin[:, b * D + hf * 512:b * D + (hf + 1) * 512], op=ALU.add),
                             [bankB[bk], B_xin], [B_xin])
                K.dma("sp", xdst[tok0:tok0 + T, :].rearrange("(b p) d -> p b d", p=128), xin.rearrange("p (b d) -> p b d", d=D),
                      [B_xin], [dbuf((dkey, t))], B_xin)

    K.barrier()
    return nc, K


def rope_tables_np(S):
    pos = np.arange(S, dtype=np.float32)
    inv = (np.float32(10000.0) ** (-np.arange(0, 32, 2, dtype=np.float32) / np.float32(32))).astype(np.float32)
    ang = (pos[:, None] * inv[None, :]).astype(np.float32)
    return np.cos(ang).astype(np.float32), np.sin(ang).astype(np.float32)


def core_tables(NTOK, nsub, T1=256):
    S = NTOK // nsub
    cos, sin = rope_tables_np(S)
    cosT = np.ones((96, NTOK), np.float32)
    sinT = np.zeros((96, NTOK), np.float32)
    c = np.tile(cos.T, (1, nsub))
    s_ = np.tile(sin.T, (1, nsub))
    cosT[64:80] = c
    cosT[80:96] = c
    sinT[64:80] = -s_
    sinT[80:96] = s_
    seq = np.arange(NTOK) // S
    qmask = np.zeros((4, NTOK), np.float32)
    kmask = np.zeros((4, NTOK), np.float32)
    for j in range(4):
        qmask[j] = (seq == j)
        kmask[j] = np.where(seq == j, 0.0, -30000.0) if nsub > 1 else 0.0
    if nsub == 1:
        qmask[:] = 0.0
    NT1 = NTOK // T1
    hflag = np.zeros((32, NT1), np.float32)
    for t in range(NT1):
        tok0 = t * T1
        if tok0 % S != 0:
            hflag[0:15, t] = 1.0
        if (tok0 + T1) % S != 0:
            hflag[15:30, t] = 1.0
    return dict(cosT=cosT, sinT=sinT, qmask=qmask, kmask=kmask, hflag=hflag)


def layout_weights(w):
    NL = w["w_in"].shape[0]
    o = {}
    o["ln1_g"] = np.ascontiguousarray(w["ln1_g"])
    o["w_in"] = np.ascontiguousarray(w["w_in"])
    pe = w["w_in"][:, :, 6528:6560]
    w_pe = np.zeros((NL, D, 2, 96), np.float32)
    w_pe[:, :, 0, 64:96] = pe
    w_pe[:, :, 1, 64:80] = pe[:, :, 16:32]
    w_pe[:, :, 1, 80:96] = pe[:, :, 0:16]
    o["w_pe"] = w_pe
    o["b_gateT"] = np.ascontiguousarray(w["b_gate"].reshape(NL, 24, 128).transpose(0, 2, 1))
    o["conv_wT"] = np.ascontiguousarray(w["conv_w"].reshape(NL, CK, 6, 128).transpose(0, 3, 2, 1))
    o["conv_bT"] = np.ascontiguousarray(w["conv_b"].reshape(NL, 6, 128).transpose(0, 2, 1))
    o["cn_gT"] = np.ascontiguousarray(w["conv_norm_g"].reshape(NL, 6, 128).transpose(0, 2, 1))
    o["cn_bT"] = np.ascontiguousarray(w["conv_norm_b"].reshape(NL, 6, 128).transpose(0, 2, 1))
    o["w_conv_out"] = np.ascontiguousarray(w["w_conv_out"])
    o["sg_norm_g"] = np.ascontiguousarray(w["sg_norm_g"])
    o["w_spT"] = np.ascontiguousarray(w["w_spatial"].transpose(0, 3, 1, 2))
    o["b_spatial"] = np.ascontiguousarray(w["b_spatial"].reshape(NL, 1, 6 * 128))
    o["w_sg_out"] = np.ascontiguousarray(w["w_sg_out"])
    o["qn_gT"] = np.ascontiguousarray(w["q_norm_g"].reshape(NL, 2, 128).transpose(0, 2, 1))
    uq = w["w_uq"].reshape(NL, 256, NH, 96)
    o["w_uq"] = np.ascontiguousarray(w["w_uq"])
    uqs = uq.copy()
    uqs[..., 64:80] = uq[..., 80:96]
    uqs[..., 80:96] = uq[..., 64:80]
    o["w_uqs"] = np.ascontiguousarray(uqs.reshape(NL, 256, NH * 96))
    o["kvn_gT"] = np.ascontiguousarray(w["kv_norm_g"].reshape(NL, 128, 1))
    ukv = w["w_ukv"].reshape(NL, 128, NH, 128)
    o["w_ukn"] = np.ascontiguousarray(ukv[..., 0:64].reshape(NL, 128, NH * 64))
    o["w_ukv_v"] = np.ascontiguousarray(ukv[..., 64:128].reshape(NL, 128, NH * 64))

    def sw(g):
        gs = g.copy()
        gs[:, 64:80] = g[:, 80:96]
        gs[:, 80:96] = g[:, 64:80]
        return np.ascontiguousarray(np.stack([g, gs], axis=-1))
    o["gq"] = sw(w["qk_q_g"])
    o["gk"] = sw(w["qk_k_g"])
    o["w_o"] = np.ascontiguousarray(w["w_o"])
    o["w_out"] = np.ascontiguousarray(w["w_out"])
    o["ln2_g"] = np.ascontiguousarray(w["ln2_g"])
    o["w_ffn_in"] = np.ascontiguousarray(w["w_ffn_in"])
    o["w_ffn_out"] = np.ascontiguousarray(w["w_ffn_out"])
    o["zrows"] = np.zeros((16, D), np.float32)
    o["ident"] = np.eye(128, dtype=np.float32)
    return o


_CACHE = {}


def run_cores(xs_list, nsubs, weights, NTOK, NL):
    key = (NTOK, NL)
    if key not in _CACHE:
        _CACHE[key] = build(NTOK, NL)[0]
    nc = _CACHE[key]
    wl = layout_weights(weights)
    in_maps = []
    for x, ns in zip(xs_list, nsubs):
        m = dict(wl)
        m.update(core_tables(NTOK, ns))
        m["x"] = np.ascontiguousarray(x, dtype=np.float32)
        in_maps.append(m)
    res = run_bass_kernel_spmd(nc, in_maps, core_ids=list(range(len(in_maps))))
    return [r["y"] for r in res.results]


def kernel(x_prompt, x_sample, **weights):
    weights = {k: np.asarray(v, dtype=np.float32) for k, v in weights.items()}
    x_prompt = np.asarray(x_prompt, dtype=np.float32)
    x_sample = np.asarray(x_sample, dtype=np.float32)
    NTOK = 8192
    NL = weights["w_in"].shape[0]
    xs_list = [x_sample[i] for i in range(4)]
    nsubs = [1, 1, 1, 1]
    for i in range(2):
        xs_list.append(x_prompt[4 * i:4 * i + 4].reshape(NTOK, D))
        nsubs.append(4)
    for i in range(2):
        xs_list.append(x_prompt[4 * i:4 * i + 4].reshape(NTOK, D))
        nsubs.append(4)
    ys = run_cores(xs_list, nsubs, weights, NTOK, NL)
    y_sample = np.stack(ys[0:4], axis=0).astype(np.float32)
    y_prompt = np.concatenate([ys[4].reshape(4, 2048, D), ys[5].reshape(4, 2048, D)], axis=0).astype(np.float32)
    return (y_prompt, y_sample)
```

```python
import numpy as np
import ml_dtypes
import concourse.bass as bass
import concourse.mybir as mybir
from concourse.bass_utils import run_bass_kernel_spmd

F32 = mybir.dt.float32
BF16 = mybir.dt.bfloat16
AF = mybir.ActivationFunctionType
ALU = mybir.AluOpType

D = 1024
DIN = 6560
DC = 768
CK = 31
NH = 12
DFF = 2816
EPS = 1e-6
NWA = 3488


class Sem:
    def __init__(s, h, idx):
        s.h = h
        s.idx = idx


class Stamp:
    __slots__ = ("eng", "sem", "val")

    def __init__(s, eng):
        s.eng = eng
        s.sem = None
        s.val = None


class Buf:
    __slots__ = ("name", "w", "r", "slot")

    def __init__(s, name=""):
        s.name = name
        s.w = {}
        s.r = {}
        s.slot = None


class Slot:
    def __init__(s):
        s.sem = None
        s.cnt = 0


class KB:
    def __init__(s, nc):
        s.nc = nc
        s.eng = {"pe": nc.tensor, "act": nc.scalar, "dve": nc.vector, "pool": nc.gpsimd, "sp": nc.sync}
        s.nsem = 0
        s.sem = {}
        s.cnt = {}
        s.seen = {k: {} for k in s.eng}
        s.pending = {k: [] for k in s.eng}
        s.slots = []
        s.free_slots = []
        s.owners = []
        s.allsems = []
        for k in s.eng:
            s.sem[k] = s.newsem()
            s.cnt[k] = 0
        s.nwaits = 0
        s.nops = 0

    def newsem(s):
        h = s.nc.semaphore(f"ks{s.nsem}").__enter__()
        sm = Sem(h, s.nsem)
        s.nsem += 1
        s.allsems.append(sm)
        return sm

    def _need(s, e, need, st):
        if st.eng == e and not s.selfsync:
            return
        assert st.val is not None, "dependency on a PE op with no milestone yet"
        k = st.sem.idx
        if k not in need or need[k][1] < st.val:
            need[k] = (st.sem, st.val)

    selfsync = False

    def _waits(s, e, reads, writes):
        need = {}
        for b in reads:
            for st in b.w.values():
                s._need(e, need, st)
        for b in writes:
            for st in b.w.values():
                s._need(e, need, st)
            for st in b.r.values():
                s._need(e, need, st)
        seen = s.seen[e]
        for k, (sm, val) in need.items():
            if seen.get(k, 0) < val:
                s.eng[e].wait_ge(sm.h, val)
                seen[k] = val
                s.nwaits += 1

    def _stamp(s, st, reads, writes):
        for b in reads:
            b.r[st.eng] = st
        for b in writes:
            b.w[st.eng] = st

    def op(s, e, fn, reads=(), writes=(), inc=True, selfsync=False):
        s.selfsync = selfsync
        s._waits(e, reads, writes)
        s.selfsync = False
        ins = fn(s.eng[e])
        s.nops += 1
        st = Stamp(e)
        if inc:
            if s.cnt[e] >= 30000:
                s.sem[e] = s.newsem()
                s.cnt[e] = 0
            s.cnt[e] += 1
            ins.then_inc(s.sem[e].h, 1)
            st.sem = s.sem[e]
            st.val = s.cnt[e]
            for p in s.pending[e]:
                p.sem = st.sem
                p.val = st.val
            s.pending[e] = []
        else:
            s.pending[e].append(st)
        s._stamp(st, reads, writes)
        return ins

    def dma(s, q, out, in_, reads, writes, slotbuf):
        if slotbuf.slot is None:
            if s.free_slots:
                slotbuf.slot = s.free_slots.pop()
            else:
                slotbuf.slot = Slot()
                slotbuf.slot.sem = s.newsem()
                s.slots.append(slotbuf.slot)
            s.owners.append(slotbuf)
        sl = slotbuf.slot
        if sl.cnt >= 30000:
            sl.sem = s.newsem()
            sl.cnt = 0
        s._waits(q, reads, writes)
        ins = s.eng[q].dma_start(out=out, in_=in_)
        s.nops += 1
        sl.cnt += 16
        ins.then_inc(sl.sem.h, 16)
        st = Stamp(("dma", sl.sem.idx))
        st.sem = sl.sem
        st.val = sl.cnt
        s._stamp(st, reads, writes)
        return ins

    def barrier(s):
        for e in s.eng:
            assert not s.pending[e], f"pending stamps on {e} at barrier"
        tgt = []
        for e in s.eng:
            if s.cnt[e] > 0:
                tgt.append((e, s.sem[e], s.cnt[e]))
        for sl in s.slots:
            if sl.cnt > 0:
                tgt.append((None, sl.sem, sl.cnt))
        for e in s.eng:
            seen = s.seen[e]
            for (src, sm, val) in tgt:
                if src == e:
                    continue
                if seen.get(sm.idx, 0) < val:
                    s.eng[e].wait_ge(sm.h, val)
                    seen[sm.idx] = val
                    s.nwaits += 1
        for b in s.owners:
            s.free_slots.append(b.slot)
            b.slot = None
        s.owners = []


class Pool:
    def __init__(s, nc):
        s.nc = nc
        s.stack = []
        s.uid = 0
        s.base = 0

    def _alloc(s, n, dt):
        cm = s.nc.sbuf_tensor(f"t{s.uid}", [128, n], dt)
        s.uid += 1
        t = cm.__enter__()
        s.stack.append(cm)
        return t[:, :]

    def f32(s, n):
        return s._alloc(n, F32)

    def bf16(s, n):
        return s._alloc(n, BF16)

    def reset(s):
        while len(s.stack) > s.base:
            s.stack.pop().__exit__(None, None, None)


PASSES = '1234'


def build(NTOK, NL, final_only=True):
    T1 = 512
    T = 512
    NT1 = NTOK // T1
    NT = NTOK // T
    NKB = NTOK // 128
    nc = bass.Bass("TRN2", target_bir_lowering=False)

    def din(name, shape, dt=F32):
        return nc.dram_tensor(name, list(shape), dt, kind="ExternalInput").ap()

    x_in = din("x", [NTOK, D])
    y_out = nc.dram_tensor("y", [NTOK, D], F32, kind="ExternalOutput").ap()
    cosT_d = din("cosT", [96, NTOK])
    sinT_d = din("sinT", [96, NTOK])
    qmask_d = din("qmask", [4, NTOK])
    kmask_d = din("kmask", [4, NTOK])
    hflag_d = din("hflag", [32, NT1])
    zrows_d = din("zrows", [16, D])
    ident_d = din("ident", [128, 128])
    ln1_g = din("ln1_g", [NL, D])
    w_in = din("w_in", [NL, D, DIN])
    w_pe = din("w_pe", [NL, D, 2, 96])
    b_gate = din("b_gateT", [NL, 128, 24])
    conv_wT = din("conv_wT", [NL, 128, 6, CK])
    conv_b = din("conv_bT", [NL, 128, 6])
    cn_g = din("cn_gT", [NL, 128, 6])
    cn_b = din("cn_bT", [NL, 128, 6])
    w_co = din("w_conv_out", [NL, DC, D])
    sg_g = din("sg_norm_g", [NL, DC])
    w_spT = din("w_spT", [NL, 128, 6, 128])
    b_sp = din("b_spatial", [NL, 1, 6 * 128])
    w_so = din("w_sg_out", [NL, DC, D])
    qn_g = din("qn_gT", [NL, 128, 2])
    w_uq = din("w_uq", [NL, 256, NH * 96])
    w_uqs = din("w_uqs", [NL, 256, NH * 96])
    kvn_g = din("kvn_gT", [NL, 128, 1])
    w_ukn = din("w_ukn", [NL, 128, NH * 64])
    w_ukv = din("w_ukv_v", [NL, 128, NH * 64])
    gq = din("gq", [NL, 96, 2])
    gk = din("gk", [NL, 96, 2])
    w_o = din("w_o", [NL, DC, D])
    w_out = din("w_out", [NL, D, D])
    ln2_g = din("ln2_g", [NL, D])
    w_fi = din("w_ffn_in", [NL, D, 2 * DFF])
    w_fo = din("w_ffn_out", [NL, DFF, D])

    xs = [nc.dram_tensor(f"xs{i}", [NTOK, D], F32).ap() for i in range(2)]
    xmid = nc.dram_tensor("xmid", [NTOK, D], F32).ap()
    yaT = nc.dram_tensor("yaT", [D, NTOK], BF16).ap()
    ybT = nc.dram_tensor("ybT", [D, NTOK], BF16).ap()
    qT_d = nc.dram_tensor("qTd", [NH, 96, NTOK], BF16).ap()
    kT_d = nc.dram_tensor("kTd", [NH, 96, NTOK], BF16).ap()
    v_d = nc.dram_tensor("vd", [NTOK, DC], BF16).ap()
    oT_d = nc.dram_tensor("oTd", [DC, NTOK], BF16).ap()

    P = Pool(nc)
    banks = []
    bank_cms = []
    bank_gen = [0]

    sbig = []
    obank = []

    def newbanks(p2=False):
        while bank_cms:
            bank_cms.pop().__exit__(None, None, None)
        banks[:] = []
        sbig[:] = []
        obank[:] = []
        if p2:
            for i in range(3):
                cm = nc.psum_tensor(f"sbig{bank_gen[0]}_{i}", [128, 1024], F32)
                sbig.append(cm.__enter__())
                bank_cms.append(cm)
            for i in range(2):
                cm = nc.psum_tensor(f"obank{bank_gen[0]}_{i}", [128, 512], F32)
                obank.append(cm.__enter__())
                bank_cms.append(cm)
        else:
            for i in range(8):
                cm = nc.psum_tensor(f"bank{bank_gen[0]}_{i}", [128, 512], F32)
                banks.append(cm.__enter__())
                bank_cms.append(cm)
        bank_gen[0] += 1

    newbanks()
    bankB = [Buf(f"bank{i}") for i in range(8)]
    K = KB(nc)

    ident = P.bf16(128)
    ones = P.bf16(128)
    epsT = P.f32(1)
    hflag = P.f32(NT1)
    B_const = Buf("const")
    K.dma("pool", ident, ident_d, [], [B_const], B_const)
    K.op("dve", lambda e: e.memset(ones, 1.0), [], [B_const])
    K.op("dve", lambda e: e.memset(epsT, EPS), [], [B_const])
    K.dma("sp", hflag[0:32, :], hflag_d, [], [B_const], B_const)
    P.base = len(P.stack)

    bank_rr = [0]

    def nextbank(lst=(0, 1, 2, 3, 4, 5, 6, 7)):
        i = lst[bank_rr[0] % len(lst)]
        bank_rr[0] += 1
        return i

    D_x = {}

    def dbuf(key):
        if key not in D_x:
            D_x[key] = Buf(str(key))
        return D_x[key]

    def rstd_from(ss_ap, out_ap, n, eng_bufs_r, eng_bufs_w, scale):
        K.op("act", lambda e: e.activation(out=out_ap, in_=ss_ap, func=AF.Sqrt, bias=epsT[0:n, :], scale=scale),
             eng_bufs_r + [B_const], eng_bufs_w, selfsync=True)
        K.op("dve", lambda e: e.reciprocal(out=out_ap, in_=out_ap), eng_bufs_w, eng_bufs_w, selfsync=True)

    def load_w_cast(dst3, src2, kch, ncols, bufw):
        K.dma("pool", dst3.rearrange("p (k n) -> p k n", n=ncols), src2.rearrange("(k p) n -> p k n", p=128),
              [], [bufw], bufw)

    def norm_transposed(xsrc, tok0, nb, g_rep, xin, xn, xnT, B_xin, B_xn, B_xnT, B_w, tilekey, ss, B_ss, junk, B_junk):
        Tn = nb * 128
        K.dma("sp", xin.rearrange("p (b d) -> p b d", d=D)[:, 0:nb, :],
              xsrc[tok0:tok0 + Tn, :].rearrange("(b p) d -> p b d", p=128), [dbuf(tilekey)], [B_xin], B_xin)
        for b in range(nb):
            xb = xin[:, b * D:(b + 1) * D]
            K.op("act", lambda e: e.activation(out=junk, in_=xb, func=AF.Square, accum_out=ss[:, b:b + 1]),
                 [B_xin], [B_junk, B_ss])
        rstd_from(ss[:, 0:nb], ss[:, 0:nb], 128, [B_ss], [B_ss], 1.0 / D)
        for b in range(nb):
            xb = xin[:, b * D:(b + 1) * D]
            K.op("dve", lambda e: e.scalar_tensor_tensor(out=xn[:, b * D:(b + 1) * D], in0=xb, scalar=ss[:, b:b + 1],
                                                         in1=g_rep, op0=ALU.mult, op1=ALU.mult),
                 [B_xin, B_ss, B_w], [B_xn], selfsync=True)
        for c in range(8):
            bk = nextbank()
            pb = banks[bk][:, 0:256].bitcast(BF16)
            for b in range(nb):
                K.op("pe", lambda e: e.transpose(out=pb[:, b * 128:(b + 1) * 128],
                                                 in_=xn[:, b * D + c * 128:b * D + (c + 1) * 128], identity=ident),
                     [B_xn, B_const], [bankB[bk]], inc=(b == nb - 1))
            eng = "act" if c % 2 == 0 else "dve"
            if eng == "act":
                K.op("act", lambda e: e.copy(out=xnT[:, c * Tn:(c + 1) * Tn], in_=pb[:, 0:Tn]), [bankB[bk]], [B_xnT])
            else:
                K.op("dve", lambda e: e.tensor_copy(out=xnT[:, c * Tn:(c + 1) * Tn], in_=pb[:, 0:Tn]), [bankB[bk]], [B_xnT])

    for l in range(NL):
        xsrc = x_in if l == 0 else xs[(l - 1) % 2]
        xdst = y_out if l == NL - 1 else xs[l % 2]
        skey = ("x", l)
        dkey = ("x", l + 1)

        if '1' in PASSES:
          for sub in ('a', 'b'):
              K.barrier()
              P.reset()
              newbanks()
              nb = T1 // 128
              isa = (sub == 'a')

              def ab(use, n, bf):
                  if not use:
                      return None
                  return P.bf16(n) if bf else P.f32(n)

              NWX = 3072 if isa else 416
              wA = P.bf16(8 * NWX)
              wpe = ab(not isa, 8 * 192, True)
              wco = ab(isa, 6 * D, True)
              wso = ab(isa, 6 * D, True)
              wsp = ab(isa, 6 * 128, True)
              wuq = ab(not isa, 2 * NH * 96, True)
              wuqs = ab(not isa, 2 * NH * 96, True)
              wukn = ab(not isa, NH * 64, True)
              wukv = ab(not isa, NH * 64, True)
              bsp = ab(isa, 6 * 128, True)
              ln1rep = P.f32(D)
              sgrep = ab(isa, DC, False)
              cw = ab(isa, 6 * CK, False)
              cb = ab(isa, 6, False)
              cng = ab(isa, 6, False)
              cnb = ab(isa, 6, False)
              qng = ab(not isa, 2, False)
              kvng = ab(not isa, 1, False)
              gqt = ab(not isa, 2, False)
              gkt = ab(not isa, 2, False)
              B_w = Buf("w1")
              K.dma("sp", ln1rep, ln1_g[l:l + 1, :].broadcast_to([128, D]), [], [B_w], B_w)
              if isa:
                  load_w_cast(wA, w_in[l][:, 3072:6144], 8, NWX, B_w)
                  load_w_cast(wco, w_co[l], 6, D, B_w)
                  load_w_cast(wso, w_so[l], 6, D, B_w)
                  K.dma("pool", wsp, w_spT[l].rearrange("p g q -> p (g q)"), [], [B_w], B_w)
                  K.dma("pool", bsp[0:1, :], b_sp[l], [], [B_w], B_w)
                  K.dma("sp", sgrep, sg_g[l:l + 1, :].broadcast_to([128, DC]), [], [B_w], B_w)
                  K.dma("sp", cw, conv_wT[l].rearrange("p c j -> p (c j)"), [], [B_w], B_w)
                  K.dma("sp", cb, conv_b[l], [], [B_w], B_w)
                  K.dma("sp", cng, cn_g[l], [], [B_w], B_w)
                  K.dma("sp", cnb, cn_b[l], [], [B_w], B_w)
              else:
                  load_w_cast(wA, w_in[l][:, 6144:DIN], 8, NWX, B_w)
                  K.dma("pool", wpe.rearrange("p (k n) -> p k n", n=192),
                        w_pe[l].rearrange("(k p) a n -> p k (a n)", p=128), [], [B_w], B_w)
                  load_w_cast(wuq, w_uq[l], 2, NH * 96, B_w)
                  load_w_cast(wuqs, w_uqs[l], 2, NH * 96, B_w)
                  K.dma("pool", wukn, w_ukn[l], [], [B_w], B_w)
                  K.dma("pool", wukv, w_ukv[l], [], [B_w], B_w)
                  K.dma("sp", qng, qn_g[l], [], [B_w], B_w)
                  K.dma("sp", kvng, kvn_g[l], [], [B_w], B_w)
                  K.dma("sp", gqt[0:96, :], gq[l], [], [B_w], B_w)
                  K.dma("sp", gkt[0:96, :], gk[l], [], [B_w], B_w)
                  K.op("dve", lambda e: e.tensor_single_scalar(out=gqt[0:96, :], in_=gqt[0:96, :], scalar=float(96 ** -0.5),
                                                               op=ALU.mult), [B_w], [B_w])

              xin = P.f32(nb * D)
              xn = P.bf16(nb * D)
              xnT = P.bf16(8 * T1)
              ss = P.f32(4)
              junk = P.bf16(D)
              sqb = [P.bf16(T1) for _ in range(2)]
              xh = ab(isa, D, False)
              xnh = ab(isa, D, True)
              xnTh = ab(isa, 8 * 32, True)
              ssh = ab(isa, 1, False)
              ZW = T1 + 32
              zpad = ab(isa, 6 * ZW, False)
              sgs = [ab(isa, T1 + 32, False) for _ in range(2)]
              acc = ab(isa, 6 * T1, False)
              acb = [ab(isa, T1, True) for _ in range(2)]
              mean = ab(isa, T1, False)
              rstd = ab(isa, T1, False)
              tln = [ab(isa, T1, False) for _ in range(2)]
              aconv = ab(isa, 6 * T1, True)
              ubf = ab(isa, 6 * T1, True)
              vf = ab(isa, DC, False)
              ssv = ab(isa, 1, False)
              vn = ab(isa, nb * DC, True)
              uv = ab(isa, 6 * T1, True)
              yst = [ab(isa, 8 * T1, True)] * 2
              qn = ab(not isa, 2 * T1, True)
              kvn = ab(not isa, T1, True)
              rq = ab(not isa, T1, False)
              sqpe = ab(not isa, T1, True)
              kr = ab(not isa, T1, False)
              t1s = [ab(not isa, T1, False) for _ in range(2)]
              t2s = [ab(not isa, T1, False) for _ in range(2)]
              sqh = [ab(not isa, T1, True) for _ in range(2)]
              rh = [ab(not isa, T1, False) for _ in range(2)]
              qst = [ab(not isa, T1, True) for _ in range(3)]
              kst = [ab(not isa, T1, True) for _ in range(3)]
              vst = ab(not isa, nb * DC, True)
              cosT = ab(not isa, T1, False)
              sinT = ab(not isa, T1, False)
              (B_xin, B_xn, B_xnT, B_xh, B_xnh, B_xnTh, B_ss, B_ssh, B_junk, B_mean, B_rstd, B_aconv, B_ubf, B_vf,
               B_ssv, B_vn, B_uv, B_qn, B_kvn, B_rq, B_sqpe, B_kr, B_vst, B_cs) = [Buf(f"p1_{i}") for i in range(24)]
              B_zpad = [Buf() for _ in range(6)]
              B_acc = [Buf() for _ in range(6)]
              B_sgs = [Buf() for _ in range(2)]
              B_sqb = [Buf() for _ in range(2)]
              B_acb = [Buf() for _ in range(2)]
              B_tln = [Buf() for _ in range(2)]
              B_yst = [Buf()] * 2
              B_t1 = [Buf() for _ in range(2)]
              B_t2 = [Buf() for _ in range(2)]
              B_sqh = [Buf() for _ in range(2)]
              B_rh = [Buf() for _ in range(2)]
              B_qst = [Buf() for _ in range(3)]
              B_kst = [Buf() for _ in range(3)]
              rr = [0, 0, 0, 0]

              wA3 = wA.rearrange("p (k n) -> p k n", n=NWX)
              wpe3 = wpe.rearrange("p (k n) -> p k n", n=192) if not isa else None
              xnT3 = xnT.rearrange("p (k n) -> p k n", n=T1)
              xnTh3 = xnTh.rearrange("p (k n) -> p k n", n=32) if isa else None

              def mm8(bk, cols, ncol, col0, M, lhs_src, n0, n, rhs3, first=True, last=True, wcol0=None):
                  for k in range(8):
                      K.op("pe", lambda e: e.matmul(banks[bk][0:M, col0:col0 + n], lhsT=lhs_src[:, k, cols:cols + ncol],
                                                    rhs=rhs3[:, k, n0:n0 + n], start=(k == 0), stop=(k == 7)),
                           [B_w, B_xnT, B_xnTh], [bankB[bk]], inc=(k == 7 and last))

              for t in range(NT1):
                  tok0 = t * T1
                  norm_transposed(xsrc, tok0, nb, ln1rep, xin, xn, xnT, B_xin, B_xn, B_xnT, B_w, (skey, tok0 // T), ss, B_ss,
                                  junk, B_junk)
                  if sub == 'a':
                      if t > 0:
                          K.dma("sp", xh[0:15, :], xsrc[tok0 - 15:tok0, :], [dbuf((skey, (tok0 - 15) // T))], [B_xh], B_xh)
                      else:
                          K.dma("sp", xh[0:15, :], zrows_d[0:15, :], [], [B_xh], B_xh)
                      if t < NT1 - 1:
                          K.dma("sp", xh[15:30, :], xsrc[tok0 + T1:tok0 + T1 + 15, :], [dbuf((skey, (tok0 + T1) // T))], [B_xh], B_xh)
                      else:
                          K.dma("sp", xh[15:30, :], zrows_d[0:15, :], [], [B_xh], B_xh)
                      K.op("act", lambda e: e.activation(out=junk[0:30, :], in_=xh[0:30, :], func=AF.Square, accum_out=ssh[0:30, :]),
                           [B_xh], [B_junk, B_ssh])
                      rstd_from(ssh[0:30, :], ssh[0:30, :], 30, [B_ssh], [B_ssh], 1.0 / D)
                      K.op("dve", lambda e: e.tensor_tensor(out=ssh[0:30, :], in0=ssh[0:30, :], in1=hflag[0:30, t:t + 1], op=ALU.mult),
                           [B_ssh, B_const], [B_ssh], selfsync=True)
                      K.op("dve", lambda e: e.scalar_tensor_tensor(out=xnh[0:30, :], in0=xh[0:30, :], scalar=ssh[0:30, 0:1],
                                                                   in1=ln1rep[0:30, :], op0=ALU.mult, op1=ALU.mult),
                           [B_xh, B_ssh, B_w], [B_xnh], selfsync=True)
                      bk = nextbank()
                      pb = banks[bk][:, 0:128].bitcast(BF16)
                      for c in range(8):
                          K.op("pe", lambda e: e.transpose(out=pb[:, c * 32:c * 32 + 30], in_=xnh[0:30, c * 128:(c + 1) * 128],
                                                           identity=ident[0:30, 0:30]),
                               [B_xnh, B_const], [bankB[bk]], inc=(c == 7))
                      K.op("dve", lambda e: e.tensor_copy(out=xnTh3[:, :, 0:30], in_=pb.rearrange("p (k n) -> p k n", n=32)[:, :, 0:30]),
                           [bankB[bk]], [B_xnTh])

                      for c in range(6):
                          ba = nextbank()
                          mm8(ba, c * 128, 128, 0, 128, wA3, 0, T1, xnT3)
                          bg = nextbank()
                          mm8(bg, 768 + c * 128, 128, 0, 128, wA3, 0, T1, xnT3)
                          bh = nextbank()
                          mm8(bh, c * 128, 128, 0, 128, wA3, 0, 30, xnTh3)
                          mm8(bh, 768 + c * 128, 128, 32, 128, wA3, 0, 30, xnTh3)
                          si = rr[0] % 2
                          rr[0] += 1
                          K.op("act", lambda e: e.activation(out=sgs[si][:, 0:T1], in_=banks[bg][:, 0:T1], func=AF.Sigmoid),
                               [bankB[bg]], [B_sgs[si]])
                          K.op("act", lambda e: e.activation(out=sgs[si][:, T1:T1 + 30], in_=banks[bh][:, 32:62], func=AF.Sigmoid),
                               [bankB[bh]], [B_sgs[si]])
                          zc = zpad[:, c * ZW:(c + 1) * ZW]
                          K.op("dve", lambda e: e.tensor_tensor(out=zc[:, 15:15 + T1], in0=banks[ba][:, 0:T1], in1=sgs[si][:, 0:T1],
                                                                op=ALU.mult), [bankB[ba], B_sgs[si]], [B_zpad[c]])
                          K.op("dve", lambda e: e.tensor_tensor(out=zc[:, 0:15], in0=banks[bh][:, 0:15], in1=sgs[si][:, T1:T1 + 15],
                                                                op=ALU.mult), [bankB[bh], B_sgs[si]], [B_zpad[c]])
                          K.op("dve", lambda e: e.tensor_tensor(out=zc[:, 15 + T1:30 + T1], in0=banks[bh][:, 15:30],
                                                                in1=sgs[si][:, T1 + 15:T1 + 30], op=ALU.mult),
                               [bankB[bh], B_sgs[si]], [B_zpad[c]])
                      for c in range(6):
                          en = "dve"
                          zc = zpad[:, c * ZW:(c + 1) * ZW]
                          ac = acc[:, c * T1:(c + 1) * T1]
                          K.op(en, lambda e: e.tensor_scalar(out=ac, in0=zc[:, 0:T1], scalar1=cw[:, c * CK:c * CK + 1],
                                                             scalar2=cb[:, c:c + 1], op0=ALU.mult, op1=ALU.add),
                               [B_zpad[c], B_w], [B_acc[c]])
                          for j in range(1, CK):
                              if en == "dve":
                                  K.op(en, lambda e: e.scalar_tensor_tensor(out=ac, in0=zc[:, j:j + T1],
                                                                            scalar=cw[:, c * CK + j:c * CK + j + 1],
                                                                            in1=ac, op0=ALU.mult, op1=ALU.add),
                                       [B_zpad[c], B_w], [B_acc[c]])
                              else:
                                  K.op(en, lambda e: e.tensor_scalar(out=tln[0], in0=zc[:, j:j + T1], scalar1=cw[:, c * CK + j:c * CK + j + 1],
                                                                     scalar2=None, op0=ALU.mult), [B_zpad[c], B_w], [B_tln[0]])
                                  K.op(en, lambda e: e.tensor_tensor(out=ac, in0=ac, in1=tln[0], op=ALU.add), [B_tln[0]], [B_acc[c]])
                      b1 = nextbank()
                      b2 = nextbank()
                      for c in range(6):
                          ac = acc[:, c * T1:(c + 1) * T1]
                          si = rr[1] % 2
                          rr[1] += 1
                          K.op("act", lambda e: e.activation(out=sqb[si], in_=ac, func=AF.Square), [B_acc[c]], [B_sqb[si]])
                          K.op("pool", lambda e: e.tensor_copy(out=acb[si], in_=ac), [B_acc[c]], [B_acb[si]])
                          K.op("pe", lambda e: e.matmul(banks[b1][:, 0:T1], lhsT=ones, rhs=acb[si], start=(c == 0), stop=(c == 5)),
                               [B_const, B_acb[si]], [bankB[b1]], inc=True)
                          K.op("pe", lambda e: e.matmul(banks[b2][:, 0:T1], lhsT=ones, rhs=sqb[si], start=(c == 0), stop=(c == 5)),
                               [B_const, B_sqb[si]], [bankB[b2]], inc=True)
                      K.op("act", lambda e: e.mul(out=mean, in_=banks[b1][:, 0:T1], mul=1.0 / DC), [bankB[b1]], [B_mean])
                      K.op("dve", lambda e: e.tensor_tensor(out=rstd, in0=mean, in1=mean, op=ALU.mult), [B_mean], [B_rstd])
                      K.op("dve", lambda e: e.scalar_tensor_tensor(out=rstd, in0=banks[b2][:, 0:T1], scalar=1.0 / DC, in1=rstd,
                                                                   op0=ALU.mult, op1=ALU.subtract), [bankB[b2], B_rstd], [B_rstd])
                      rstd_from(rstd, rstd, 128, [B_rstd], [B_rstd], 1.0)
                      for c in range(6):
                          ac = acc[:, c * T1:(c + 1) * T1]
                          si = rr[2] % 2
                          rr[2] += 1
                          K.op("pool", lambda e: e.tensor_tensor(out=tln[si], in0=ac, in1=mean, op=ALU.subtract),
                               [B_acc[c], B_mean], [B_tln[si]])
                          K.op("pool", lambda e: e.tensor_tensor(out=tln[si], in0=tln[si], in1=rstd, op=ALU.mult),
                               [B_tln[si], B_rstd], [B_tln[si]])
                          K.op("act", lambda e: e.activation(out=aconv[:, c * T1:(c + 1) * T1], in_=tln[si], func=AF.Silu,
                                                             scale=cng[:, c:c + 1], bias=cnb[:, c:c + 1]),
                               [B_tln[si], B_w], [B_aconv])

                      def out_proj(src, B_src, w, dstT, key):
                          w3 = w.rearrange("p (k n) -> p k n", n=D)
                          yi = rr[3] % 2
                          rr[3] += 1
                          for oc in range(8):
                              bk = nextbank()
                              for c in range(6):
                                  K.op("pe", lambda e: e.matmul(banks[bk][:, 0:T1], lhsT=w3[:, c, oc * 128:(oc + 1) * 128],
                                                                rhs=src[:, c * T1:(c + 1) * T1], start=(c == 0), stop=(c == 5)),
                                       [B_w, B_src], [bankB[bk]], inc=(c == 5))
                              if oc % 2 == 0:
                                  K.op("act", lambda e: e.copy(out=yst[yi][:, oc * T1:(oc + 1) * T1], in_=banks[bk][:, 0:T1]),
                                       [bankB[bk]], [B_yst[yi]])
                              else:
                                  K.op("dve", lambda e: e.tensor_copy(out=yst[yi][:, oc * T1:(oc + 1) * T1], in_=banks[bk][:, 0:T1]),
                                       [bankB[bk]], [B_yst[yi]])
                          K.dma("sp", dstT.rearrange("(c p) n -> p c n", p=128)[:, :, tok0:tok0 + T1],
                                yst[yi].rearrange("p (c n) -> p c n", n=T1), [B_yst[yi]], [dbuf((key, l, tok0 // T))], B_yst[yi])

                      out_proj(aconv, B_aconv, wco, yaT, "ya")

                      for c in range(6):
                          bk = nextbank()
                          mm8(bk, 1536 + c * 128, 128, 0, 128, wA3, 0, T1, xnT3)
                          K.op("act", lambda e: e.activation(out=ubf[:, c * T1:(c + 1) * T1], in_=banks[bk][:, 0:T1],
                                                             func=AF.Gelu_apprx_tanh), [bankB[bk]], [B_ubf])
                      for b in range(nb):
                          bA = nextbank()
                          bB = nextbank()
                          for k in range(8):
                              K.op("pe", lambda e: e.matmul(banks[bA][:, 0:512], lhsT=xnT3[:, k, b * 128:(b + 1) * 128],
                                                            rhs=wA3[:, k, 2304:2816], start=(k == 0), stop=(k == 7)),
                                   [B_w, B_xnT], [bankB[bA]], inc=(k == 7))
                          for k in range(8):
                              K.op("pe", lambda e: e.matmul(banks[bB][:, 0:256], lhsT=xnT3[:, k, b * 128:(b + 1) * 128],
                                                            rhs=wA3[:, k, 2816:3072], start=(k == 0), stop=(k == 7)),
                                   [B_w, B_xnT], [bankB[bB]], inc=(k == 7))
                          K.op("act", lambda e: e.activation(out=vf[:, 0:512], in_=banks[bA][:, 0:512], func=AF.Gelu_apprx_tanh),
                               [bankB[bA]], [B_vf])
                          K.op("act", lambda e: e.activation(out=vf[:, 512:768], in_=banks[bB][:, 0:256], func=AF.Gelu_apprx_tanh),
                               [bankB[bB]], [B_vf])
                          K.op("act", lambda e: e.activation(out=junk[:, 0:DC], in_=vf, func=AF.Square, accum_out=ssv),
                               [B_vf], [B_junk, B_ssv])
                          rstd_from(ssv, ssv, 128, [B_ssv], [B_ssv], 1.0 / DC)
                          K.op("dve", lambda e: e.scalar_tensor_tensor(out=vn[:, b * DC:(b + 1) * DC], in0=vf, scalar=ssv[:, 0:1],
                                                                       in1=sgrep, op0=ALU.mult, op1=ALU.mult),
                               [B_vf, B_ssv, B_w], [B_vn], selfsync=True)
                      for g in range(6):
                          bk = nextbank()
                          for b in range(nb):
                              K.op("pe", lambda e: e.matmul(banks[bk][:, b * 128:(b + 1) * 128], lhsT=ones[0:1, :],
                                                            rhs=bsp[0:1, g * 128:(g + 1) * 128], start=True, stop=False),
                                   [B_const, B_w], [bankB[bk]], inc=False)
                              K.op("pe", lambda e: e.matmul(banks[bk][:, b * 128:(b + 1) * 128],
                                                            lhsT=vn[:, b * DC + g * 128:b * DC + (g + 1) * 128],
                                                            rhs=wsp[:, g * 128:(g + 1) * 128], start=False, stop=True),
                                   [B_vn, B_w], [bankB[bk]], inc=(b == nb - 1))
                          K.op("dve", lambda e: e.tensor_tensor(out=uv[:, g * T1:(g + 1) * T1], in0=banks[bk][:, 0:T1],
                                                                in1=ubf[:, g * T1:(g + 1) * T1], op=ALU.mult),
                               [bankB[bk], B_ubf], [B_uv])
                      out_proj(uv, B_uv, wso, ybT, "yb")

                  else:
                      K.dma("sp", cosT[0:96, :], cosT_d[:, tok0:tok0 + T1], [], [B_cs], B_cs)
                      K.dma("sp", sinT[0:96, :], sinT_d[:, tok0:tok0 + T1], [], [B_cs], B_cs)
                      bq = [nextbank(), nextbank()]
                      for c in range(2):
                          mm8(bq[c], c * 128, 128, 0, 128, wA3, 0, T1, xnT3)
                      bss = nextbank()
                      for c in range(2):
                          si = rr[1] % 2
                          rr[1] += 1
                          K.op("act", lambda e: e.activation(out=sqb[si], in_=banks[bq[c]][:, 0:T1], func=AF.Square),
                               [bankB[bq[c]]], [B_sqb[si]])
                          K.op("pe", lambda e: e.matmul(banks[bss][:, 0:T1], lhsT=ones, rhs=sqb[si], start=(c == 0), stop=(c == 1)),
                               [B_const, B_sqb[si]], [bankB[bss]], inc=True)
                      rstd_from(banks[bss][:, 0:T1], rq, 128, [bankB[bss]], [B_rq], 1.0 / 256)
                      for c in range(2):
                          K.op("dve", lambda e: e.scalar_tensor_tensor(out=qn[:, c * T1:(c + 1) * T1], in0=banks[bq[c]][:, 0:T1],
                                                                       scalar=qng[:, c:c + 1], in1=rq, op0=ALU.mult, op1=ALU.mult),
                               [bankB[bq[c]], B_rq, B_w], [B_qn])
                      bkv = nextbank()
                      mm8(bkv, 256, 128, 0, 128, wA3, 0, T1, xnT3)
                      bss = nextbank()
                      si = rr[1] % 2
                      rr[1] += 1
                      K.op("act", lambda e: e.activation(out=sqb[si], in_=banks[bkv][:, 0:T1], func=AF.Square),
                           [bankB[bkv]], [B_sqb[si]])
                      K.op("pe", lambda e: e.matmul(banks[bss][:, 0:T1], lhsT=ones, rhs=sqb[si], start=True, stop=True),
                           [B_const, B_sqb[si]], [bankB[bss]], inc=True)
                      rstd_from(banks[bss][:, 0:T1], rq, 128, [bankB[bss]], [B_rq], 1.0 / 128)
                      K.op("dve", lambda e: e.scalar_tensor_tensor(out=kvn, in0=banks[bkv][:, 0:T1], scalar=kvng[:, 0:1], in1=rq,
                                                                   op0=ALU.mult, op1=ALU.mult), [bankB[bkv], B_rq, B_w], [B_kvn])
                      bpe = nextbank()
                      mm8(bpe, 0, 96, 0, 96, wpe3, 0, T1, xnT3)
                      bpes = nextbank()
                      mm8(bpes, 96, 96, 0, 96, wpe3, 0, T1, xnT3)
                      K.op("act", lambda e: e.activation(out=sqpe[64:96, :], in_=banks[bpe][64:96, 0:T1], func=AF.Square),
                           [bankB[bpe]], [B_sqpe])
                      K.op("dve", lambda e: e.scalar_tensor_tensor(out=kr[64:96, :], in0=banks[bpe][64:96, 0:T1], scalar=gkt[64:96, 0:1],
                                                                   in1=cosT[64:96, :], op0=ALU.mult, op1=ALU.mult),
                           [bankB[bpe], B_w, B_cs], [B_kr])
                      K.op("dve", lambda e: e.scalar_tensor_tensor(out=t1s[0][64:96, :], in0=banks[bpes][64:96, 0:T1],
                                                                   scalar=gkt[64:96, 1:2], in1=sinT[64:96, :], op0=ALU.mult, op1=ALU.mult),
                           [bankB[bpes], B_w, B_cs], [B_t1[0]])
                      K.op("dve", lambda e: e.tensor_tensor(out=kr[64:96, :], in0=kr[64:96, :], in1=t1s[0][64:96, :], op=ALU.add),
                           [B_kr, B_t1[0]], [B_kr])
                      for b in range(nb):
                          bA = nextbank()
                          bB = nextbank()
                          K.op("pe", lambda e: e.matmul(banks[bA][:, 0:512], lhsT=kvn[:, b * 128:(b + 1) * 128], rhs=wukv[:, 0:512],
                                                        start=True, stop=True), [B_kvn, B_w], [bankB[bA]])
                          K.op("pe", lambda e: e.matmul(banks[bB][:, 0:256], lhsT=kvn[:, b * 128:(b + 1) * 128], rhs=wukv[:, 512:768],
                                                        start=True, stop=True), [B_kvn, B_w], [bankB[bB]])
                          K.op("act", lambda e: e.copy(out=vst[:, b * DC:b * DC + 512], in_=banks[bA][:, 0:512]), [bankB[bA]], [B_vst])
                          K.op("dve", lambda e: e.tensor_copy(out=vst[:, b * DC + 512:(b + 1) * DC], in_=banks[bB][:, 0:256]),
                               [bankB[bB]], [B_vst])
                      K.dma("sp", v_d[tok0:tok0 + T1, :].rearrange("(b p) d -> p b d", p=128), vst.rearrange("p (b d) -> p b d", d=DC),
                            [B_vst], [dbuf(("v", l))], B_vst)
                      wuq3 = wuq.rearrange("p (k n) -> p k n", n=NH * 96)
                      wuqs3 = wuqs.rearrange("p (k n) -> p k n", n=NH * 96)
                      for h in range(NH):
                          i2 = h % 2
                          i3 = h % 3
                          bkn = nextbank()
                          K.op("pe", lambda e: e.matmul(banks[bkn][0:64, 0:T1], lhsT=wukn[:, h * 64:(h + 1) * 64], rhs=kvn,
                                                        start=True, stop=True), [B_w, B_kvn], [bankB[bkn]])
                          K.op("act", lambda e: e.activation(out=sqh[i2][0:64, :], in_=banks[bkn][0:64, 0:T1], func=AF.Square),
                               [bankB[bkn]], [B_sqh[i2]])
                          bs = nextbank()
                          K.op("pe", lambda e: e.matmul(banks[bs][0:96, 0:T1], lhsT=ones[0:64, 0:96], rhs=sqh[i2][0:64, :],
                                                        start=True, stop=False), [B_const, B_sqh[i2]], [bankB[bs]], inc=False)
                          K.op("pe", lambda e: e.matmul(banks[bs][0:96, 0:T1], lhsT=ones[64:96, 0:96], rhs=sqpe[64:96, :],
                                                        start=False, stop=True), [B_const, B_sqpe], [bankB[bs]])
                          rstd_from(banks[bs][0:96, 0:T1], rh[i2][0:96, :], 96, [bankB[bs]], [B_rh[i2]], 1.0 / 96)
                          K.op("dve", lambda e: e.scalar_tensor_tensor(out=kst[i3][0:64, :], in0=banks[bkn][0:64, 0:T1],
                                                                       scalar=gkt[0:64, 0:1], in1=rh[i2][0:64, :],
                                                                       op0=ALU.mult, op1=ALU.mult),
                               [bankB[bkn], B_w, B_rh[i2]], [B_kst[i3]])
                          K.op("pool", lambda e: e.tensor_tensor(out=kst[i3][64:96, :], in0=kr[64:96, :], in1=rh[i2][64:96, :],
                                                                 op=ALU.mult), [B_kr, B_rh[i2]], [B_kst[i3]])
                          K.dma("sp", kT_d[h, :, tok0:tok0 + T1], kst[i3][0:96, :], [B_kst[i3]], [dbuf(("k", l, h))], B_kst[i3])
                          bqh = nextbank()
                          bqs = nextbank()
                          for c in range(2):
                              K.op("pe", lambda e: e.matmul(banks[bqh][0:96, 0:T1], lhsT=wuq3[:, c, h * 96:(h + 1) * 96],
                                                            rhs=qn[:, c * T1:(c + 1) * T1], start=(c == 0), stop=(c == 1)),
                                   [B_w, B_qn], [bankB[bqh]], inc=(c == 1))
                          for c in range(2):
                              K.op("pe", lambda e: e.matmul(banks[bqs][0:96, 0:T1], lhsT=wuqs3[:, c, h * 96:(h + 1) * 96],
                                                            rhs=qn[:, c * T1:(c + 1) * T1], start=(c == 0), stop=(c == 1)),
                                   [B_w, B_qn], [bankB[bqs]], inc=(c == 1))
                          j2 = (h + 1) % 2
                          K.op("act", lambda e: e.activation(out=sqh[j2][0:96, :], in_=banks[bqh][0:96, 0:T1], func=AF.Square),
                               [bankB[bqh]], [B_sqh[j2]])
                          bs = nextbank()
                          K.op("pe", lambda e: e.matmul(banks[bs][0:96, 0:T1], lhsT=ones[0:96, 0:96], rhs=sqh[j2][0:96, :],
                                                        start=True, stop=True), [B_const, B_sqh[j2]], [bankB[bs]])
                          rstd_from(banks[bs][0:96, 0:T1], rh[j2][0:96, :], 96, [bankB[bs]], [B_rh[j2]], 1.0 / 96)
                          K.op("dve", lambda e: e.scalar_tensor_tensor(out=t1s[i2][0:96, :], in0=banks[bqh][0:96, 0:T1],
                                                                       scalar=gqt[0:96, 0:1], in1=cosT[0:96, :],
                                                                       op0=ALU.mult, op1=ALU.mult),
                               [bankB[bqh], B_w, B_cs], [B_t1[i2]])
                          K.op("dve", lambda e: e.scalar_tensor_tensor(out=t2s[i2][0:96, :], in0=banks[bqs][0:96, 0:T1],
                                                                       scalar=gqt[0:96, 1:2], in1=sinT[0:96, :],
                                                                       op0=ALU.mult, op1=ALU.mult),
                               [bankB[bqs], B_w, B_cs], [B_t2[i2]])
                          K.op("pool", lambda e: e.tensor_tensor(out=t1s[i2][0:96, :], in0=t1s[i2][0:96, :], in1=t2s[i2][0:96, :],
                                                                 op=ALU.add), [B_t1[i2], B_t2[i2]], [B_t1[i2]])
                          K.op("pool", lambda e: e.tensor_tensor(out=qst[i3][0:96, :], in0=t1s[i2][0:96, :], in1=rh[j2][0:96, :],
                                                                 op=ALU.mult), [B_t1[i2], B_rh[j2]], [B_qst[i3]])
                          K.dma("sp", qT_d[h, :, tok0:tok0 + T1], qst[i3][0:96, :], [B_qst[i3]], [dbuf(("q", l, h))], B_qst[i3])

        if '2' in PASSES:
            K.barrier()
            P.reset()
            newbanks(p2=True)
            kT = [P.bf16(NTOK) for _ in range(2)]
            vA = [P.bf16(NKB * 128) for _ in range(2)]
            qT = [P.bf16(T) for _ in range(3)]
            pT = [P.bf16(2 * T) for _ in range(3)]
            rinv = [P.f32(T) for _ in range(2)]
            ost = [P.bf16(T) for _ in range(2)]
            B_kT = [Buf() for _ in range(2)]
            B_vA = [Buf() for _ in range(2)]
            B_qT = [Buf() for _ in range(3)]
            B_pT = [Buf() for _ in range(3)]
            B_sb = [Buf() for _ in range(3)]
            B_ob = [Buf() for _ in range(2)]
            B_rinv = [Buf() for _ in range(2)]
            B_ost = [Buf() for _ in range(2)]
            for i in range(2):
                K.dma("pool", kT[i][96:100, :], kmask_d, [], [B_kT[i]], B_kT[i])
                K.op("dve", lambda e: e.memset(vA[i].rearrange("p (k d) -> p k d", d=128)[:, :, 64:128], 1.0), [], [B_vA[i]])
            NP = NKB // 2
            NTILE = NH * NT

            def load_head(h):
                hs = h % 2
                K.dma("sp", kT[hs][0:96, :], kT_d[h], [dbuf(("k", l, h))], [B_kT[hs]], B_kT[hs])
                for part in range(0, NKB, 16):
                    pe_ = min(NKB, part + 16)
                    K.dma("sp", vA[hs].rearrange("p (k d) -> p k d", d=128)[:, part:pe_, 0:64],
                          v_d[part * 128:pe_ * 128, h * 64:(h + 1) * 64].rearrange("(k p) d -> p k d", p=128),
                          [dbuf(("v", l))], [B_vA[hs]], B_vA[hs])

            def load_q(n):
                h, qt = divmod(n, NT)
                qs = n % 3
                K.dma("sp", qT[qs][0:96, :], qT_d[h, :, qt * T:(qt + 1) * T], [dbuf(("q", l, h))], [B_qT[qs]], B_qT[qs])
                K.dma("pool", qT[qs][96:100, :], qmask_d[:, qt * T:(qt + 1) * T], [], [B_qT[qs]], B_qT[qs])

            def S_(i):
                n, p = divmod(i, NP)
                h, qt = divmod(n, NT)
                hs = h % 2
                qs = n % 3
                if p == 0:
                    if n + 1 < NTILE:
                        load_q(n + 1)
                    if qt == NT - 1 and h + 1 < NH:
                        load_head(h + 1)
                sb = i % 3
                for j in range(2):
                    kb = 2 * p + j
                    K.op("pe", lambda e: e.matmul(sbig[sb][:, j * T:(j + 1) * T], lhsT=kT[hs][0:100, kb * 128:(kb + 1) * 128],
                                                  rhs=qT[qs][0:100, :], start=True, stop=True),
                         [B_kT[hs], B_qT[qs]], [B_sb[sb]], inc=(j == 1))

            def PV_(i):
                n, p = divmod(i, NP)
                h, qt = divmod(n, NT)
                hs = h % 2
                sb = i % 3
                ob = n % 2
                vA3 = vA[hs].rearrange("p (k d) -> p k d", d=128)
                K.op("act", lambda e: e.activation(out=pT[sb], in_=sbig[sb][:, 0:2 * T], func=AF.Exp), [B_sb[sb]], [B_pT[sb]])
                for j in range(2):
                    kb = 2 * p + j
                    K.op("pe", lambda e: e.matmul(obank[ob][:, 0:T], lhsT=vA3[:, kb, :], rhs=pT[sb][:, j * T:(j + 1) * T],
                                                  start=(kb == 0), stop=(kb == NKB - 1)),
                         [B_vA[hs], B_pT[sb]], [B_ob[ob]], inc=(j == 1))
                if p == NP - 1:
                    K.op("dve", lambda e: e.reciprocal(out=rinv[ob][0:64, :], in_=obank[ob][64:128, 0:T]),
                         [B_ob[ob]], [B_rinv[ob]])
                    K.op("dve", lambda e: e.tensor_tensor(out=ost[ob][0:64, :], in0=obank[ob][0:64, 0:T], in1=rinv[ob][0:64, :],
                                                          op=ALU.mult), [B_ob[ob], B_rinv[ob]], [B_ost[ob]])
                    K.dma("sp", oT_d[h * 64:(h + 1) * 64, qt * T:(qt + 1) * T], ost[ob][0:64, :], [B_ost[ob]],
                          [dbuf(("o", l, qt))], B_ost[ob])

            load_head(0)
            load_q(0)
            NI = NTILE * NP
            LOOK = 2
            for i in range(min(LOOK, NI)):
                S_(i)
            for i in range(NI):
                if i + LOOK < NI:
                    S_(i + LOOK)
                PV_(i)

        if '3' in PASSES:
            K.barrier()
            P.reset()
            newbanks()
            nb = T // 128
            wg = P.bf16(8 * 3072)
            wo = P.bf16(6 * D)
            wout = P.bf16(8 * D)
            ln1rep = P.f32(D)
            bg = P.f32(24)
            B_w = Buf("w3")
            load_w_cast(wg, w_in[l][:, 0:3072], 8, 3072, B_w)
            load_w_cast(wo, w_o[l], 6, D, B_w)
            load_w_cast(wout, w_out[l], 8, D, B_w)
            K.dma("sp", ln1rep, ln1_g[l:l + 1, :].broadcast_to([128, D]), [], [B_w], B_w)
            K.dma("sp", bg, b_gate[l], [], [B_w], B_w)
            xin = P.f32(nb * D)
            xn = P.bf16(nb * D)
            xnT = P.bf16(8 * T)
            ss = P.f32(4)
            junk = P.bf16(D)
            oT = P.bf16(6 * T)
            ya = P.bf16(8 * T)
            yb = P.bf16(8 * T)
            mg = P.bf16(8 * T)
            gs = [[P.bf16(T) for _ in range(3)] for _ in range(2)]
            m1 = [P.f32(T) for _ in range(2)]
            m2 = [P.f32(T) for _ in range(2)]
            B_xin, B_xn, B_xnT, B_ss, B_junk, B_oT, B_ya, B_yb, B_mg = [Buf() for _ in range(9)]
            B_gs = [[Buf() for _ in range(3)] for _ in range(2)]
            B_m1 = [Buf() for _ in range(2)]
            B_m2 = [Buf() for _ in range(2)]
            wg3 = wg.rearrange("p (k n) -> p k n", n=3072)
            wo3 = wo.rearrange("p (k n) -> p k n", n=D)
            wout3 = wout.rearrange("p (k n) -> p k n", n=D)
            xnT3 = xnT.rearrange("p (k n) -> p k n", n=T)
            for t in range(NT):
                tok0 = t * T
                norm_transposed(xsrc, tok0, nb, ln1rep, xin, xn, xnT, B_xin, B_xn, B_xnT, B_w, (skey, t), ss, B_ss, junk, B_junk)
                K.dma("sp", oT.rearrange("p (c n) -> p c n", n=T), oT_d.rearrange("(c p) n -> p c n", p=128)[:, :, tok0:tok0 + T],
                      [dbuf(("o", l, t))], [B_oT], B_oT)
                K.dma("sp", ya.rearrange("p (c n) -> p c n", n=T), yaT.rearrange("(c p) n -> p c n", p=128)[:, :, tok0:tok0 + T],
                      [dbuf(("ya", l, t))], [B_ya], B_ya)
                K.dma("sp", yb.rearrange("p (c n) -> p c n", n=T), ybT.rearrange("(c p) n -> p c n", p=128)[:, :, tok0:tok0 + T],
                      [dbuf(("yb", l, t))], [B_yb], B_yb)
                for c in range(8):
                    i2 = c % 2
                    gb = []
                    for j in range(3):
                        bk = nextbank()
                        gb.append(bk)
                        for k in range(8):
                            K.op("pe", lambda e: e.matmul(banks[bk][:, 0:T], lhsT=wg3[:, k, j * D + c * 128:j * D + (c + 1) * 128],
                                                          rhs=xnT3[:, k, :], start=(k == 0), stop=(k == 7)),
                                 [B_w, B_xnT], [bankB[bk]], inc=(k == 7))
                        K.op("act", lambda e: e.activation(out=gs[i2][j], in_=banks[bk][:, 0:T], func=AF.Sigmoid,
                                                           bias=bg[:, j * 8 + c:j * 8 + c + 1]),
                             [bankB[bk], B_w], [B_gs[i2][j]])
                    bk = nextbank()
                    for k in range(6):
                        K.op("pe", lambda e: e.matmul(banks[bk][:, 0:T], lhsT=wo3[:, k, c * 128:(c + 1) * 128],
                                                      rhs=oT[:, k * T:(k + 1) * T], start=(k == 0), stop=(k == 5)),
                             [B_w, B_oT], [bankB[bk]], inc=(k == 5))
                    K.op("pool", lambda e: e.tensor_tensor(out=m1[i2], in0=gs[i2][0], in1=ya[:, c * T:(c + 1) * T], op=ALU.mult),
                         [B_gs[i2][0], B_ya], [B_m1[i2]])
                    K.op("pool", lambda e: e.tensor_tensor(out=m2[i2], in0=gs[i2][1], in1=yb[:, c * T:(c + 1) * T], op=ALU.mult),
                         [B_gs[i2][1], B_yb], [B_m2[i2]])
                    K.op("pool", lambda e: e.tensor_tensor(out=m1[i2], in0=m1[i2], in1=m2[i2], op=ALU.add),
                         [B_m1[i2], B_m2[i2]], [B_m1[i2]])
                    K.op("dve", lambda e: e.tensor_tensor(out=m2[i2], in0=banks[bk][:, 0:T], in1=gs[i2][2], op=ALU.mult),
                         [bankB[bk], B_gs[i2][2], B_m1[i2]], [B_m2[i2]])
                    K.op("dve", lambda e: e.tensor_tensor(out=mg[:, c * T:(c + 1) * T], in0=m1[i2], in1=m2[i2], op=ALU.add),
                         [B_m1[i2], B_m2[i2]], [B_mg])
                for b in range(nb):
                    for hf in range(2):
                        bk = nextbank()
                        for c in range(8):
                            K.op("pe", lambda e: e.matmul(banks[bk][:, 0:512], lhsT=mg[:, c * T + b * 128:c * T + (b + 1) * 128],
                                                          rhs=wout3[:, c, hf * 512:(hf + 1) * 512], start=(c == 0), stop=(c == 7)),
                                 [B_mg, B_w], [bankB[bk]], inc=(c == 7))
                        K.op("dve", lambda e: e.tensor_tensor(out=xin[:, b * D + hf * 512:b * D + (hf + 1) * 512],
                                                              in0=banks[bk][:, 0:512],
                                                              in1=xin[:, b * D + hf * 512:b * D + (hf + 1) * 512], op=ALU.add),
                             [bankB[bk], B_xin], [B_xin])
                K.dma("sp", xmid[tok0:tok0 + T, :].rearrange("(b p) d -> p b d", p=128), xin.rearrange("p (b d) -> p b d", d=D),
                      [B_xin], [dbuf(("xmid", l, t))], B_xin)

        if '4' in PASSES:
            K.barrier()
            P.reset()
            newbanks()
            wfi = P.bf16(8 * 2 * DFF)
            wfo = P.bf16(22 * D)
            ln2rep = P.f32(D)
            B_w = Buf("w4")
            wfi3 = wfi.rearrange("p (k n) -> p k n", n=2 * DFF)
            for k in range(8):
                K.dma("pool", wfi3[:, k, :], w_fi[l][k * 128:(k + 1) * 128, :], [], [B_w], B_w)
            load_w_cast(wfo, w_fo[l], 22, D, B_w)
            K.dma("sp", ln2rep, ln2_g[l:l + 1, :].broadcast_to([128, D]), [], [B_w], B_w)
            xin = P.f32(nb * D)
            xn = P.bf16(nb * D)
            xnT = P.bf16(8 * T)
            ss = P.f32(4)
            junk = P.bf16(D)
            aa = P.bf16(22 * T)
            sl = [P.f32(T) for _ in range(2)]
            B_xin, B_xn, B_xnT, B_ss, B_junk, B_aa = [Buf() for _ in range(6)]
            B_sl = [Buf() for _ in range(2)]
            wfo3 = wfo.rearrange("p (k n) -> p k n", n=D)
            xnT3 = xnT.rearrange("p (k n) -> p k n", n=T)
            for t in range(NT):
                tok0 = t * T
                norm_transposed(xmid, tok0, nb, ln2rep, xin, xn, xnT, B_xin, B_xn, B_xnT, B_w, ("xmid", l, t), ss, B_ss,
                                junk, B_junk)
                for j in range(22):
                    i2 = j % 2
                    bi = nextbank()
                    bgt = nextbank()
                    for k in range(8):
                        K.op("pe", lambda e: e.matmul(banks[bi][:, 0:T], lhsT=wfi3[:, k, j * 128:(j + 1) * 128], rhs=xnT3[:, k, :],
                                                      start=(k == 0), stop=(k == 7)), [B_w, B_xnT], [bankB[bi]], inc=(k == 7))
                    for k in range(8):
                        K.op("pe", lambda e: e.matmul(banks[bgt][:, 0:T], lhsT=wfi3[:, k, DFF + j * 128:DFF + (j + 1) * 128],
                                                      rhs=xnT3[:, k, :], start=(k == 0), stop=(k == 7)),
                             [B_w, B_xnT], [bankB[bgt]], inc=(k == 7))
                    K.op("act", lambda e: e.activation(out=sl[i2], in_=banks[bgt][:, 0:T], func=AF.Silu), [bankB[bgt]], [B_sl[i2]])
                    K.op("dve", lambda e: e.tensor_tensor(out=aa[:, j * T:(j + 1) * T], in0=banks[bi][:, 0:T], in1=sl[i2],
                                                          op=ALU.mult), [bankB[bi], B_sl[i2]], [B_aa])
                for b in range(nb):
                    for hf in range(2):
                        bk = nextbank()
                        for j in range(22):
                            K.op("pe", lambda e: e.matmul(banks[bk][:, 0:512], lhsT=aa[:, j * T + b * 128:j * T + (b + 1) * 128],
                                                          rhs=wfo3[:, j, hf * 512:(hf + 1) * 512], start=(j == 0), stop=(j == 21)),
                                 [B_aa, B_w], [bankB[bk]], inc=(j == 21))
                        K.op("dve", lambda e: e.tensor_tensor(out=xin[:, b * D + hf * 512:b * D + (hf + 1) * 512],
                                                              in0=banks[bk][:, 0:512],
                                                              in1=xin[:, b * D + hf * 512:b * D + (hf + 1) * 512], op=ALU.add),
                             [bankB[bk], B_xin], [B_xin])
                K.dma("sp", xdst[tok0:tok0 + T, :].rearrange("(b p) d -> p b d", p=128), xin.rearrange("p (b d) -> p b d", d=D),
                      [B_xin], [dbuf((dkey, t))], B_xin)

    K.barrier()
    return nc, K


def rope_tables_np(S):
    pos = np.arange(S, dtype=np.float32)
    inv = (np.float32(10000.0) ** (-np.arange(0, 32, 2, dtype=np.float32) / np.float32(32))).astype(np.float32)
    ang = (pos[:, None] * inv[None, :]).astype(np.float32)
    return np.cos(ang).astype(np.float32), np.sin(ang).astype(np.float32)


def core_tables(NTOK, nsub, T1=512):
    S = NTOK // nsub
    cos, sin = rope_tables_np(S)
    cosT = np.ones((96, NTOK), np.float32)
    sinT = np.zeros((96, NTOK), np.float32)
    c = np.tile(cos.T, (1, nsub))
    s_ = np.tile(sin.T, (1, nsub))
    cosT[64:80] = c
    cosT[80:96] = c
    sinT[64:80] = -s_
    sinT[80:96] = s_
    seq = np.arange(NTOK) // S
    qmask = np.zeros((4, NTOK), np.float32)
    kmask = np.zeros((4, NTOK), np.float32)
    for j in range(4):
        qmask[j] = (seq == j)
        kmask[j] = np.where(seq == j, 0.0, -30000.0) if nsub > 1 else 0.0
    if nsub == 1:
        qmask[:] = 0.0
    NT1 = NTOK // T1
    hflag = np.zeros((32, NT1), np.float32)
    for t in range(NT1):
        tok0 = t * T1
        if tok0 % S != 0:
            hflag[0:15, t] = 1.0
        if (tok0 + T1) % S != 0:
            hflag[15:30, t] = 1.0
    return dict(cosT=cosT, sinT=sinT, qmask=qmask, kmask=kmask, hflag=hflag)


def layout_weights(w):
    NL = w["w_in"].shape[0]
    o = {}
    o["ln1_g"] = np.ascontiguousarray(w["ln1_g"])
    o["w_in"] = np.ascontiguousarray(w["w_in"])
    pe = w["w_in"][:, :, 6528:6560]
    w_pe = np.zeros((NL, D, 2, 96), np.float32)
    w_pe[:, :, 0, 64:96] = pe
    w_pe[:, :, 1, 64:80] = pe[:, :, 16:32]
    w_pe[:, :, 1, 80:96] = pe[:, :, 0:16]
    o["w_pe"] = w_pe
    o["b_gateT"] = np.ascontiguousarray(w["b_gate"].reshape(NL, 24, 128).transpose(0, 2, 1))
    o["conv_wT"] = np.ascontiguousarray(w["conv_w"].reshape(NL, CK, 6, 128).transpose(0, 3, 2, 1))
    o["conv_bT"] = np.ascontiguousarray(w["conv_b"].reshape(NL, 6, 128).transpose(0, 2, 1))
    o["cn_gT"] = np.ascontiguousarray(w["conv_norm_g"].reshape(NL, 6, 128).transpose(0, 2, 1))
    o["cn_bT"] = np.ascontiguousarray(w["conv_norm_b"].reshape(NL, 6, 128).transpose(0, 2, 1))
    o["w_conv_out"] = np.ascontiguousarray(w["w_conv_out"])
    o["sg_norm_g"] = np.ascontiguousarray(w["sg_norm_g"])
    o["w_spT"] = np.ascontiguousarray(w["w_spatial"].transpose(0, 3, 1, 2))
    o["b_spatial"] = np.ascontiguousarray(w["b_spatial"].reshape(NL, 1, 6 * 128))
    o["w_sg_out"] = np.ascontiguousarray(w["w_sg_out"])
    o["qn_gT"] = np.ascontiguousarray(w["q_norm_g"].reshape(NL, 2, 128).transpose(0, 2, 1))
    uq = w["w_uq"].reshape(NL, 256, NH, 96)
    o["w_uq"] = np.ascontiguousarray(w["w_uq"])
    uqs = uq.copy()
    uqs[..., 64:80] = uq[..., 80:96]
    uqs[..., 80:96] = uq[..., 64:80]
    o["w_uqs"] = np.ascontiguousarray(uqs.reshape(NL, 256, NH * 96))
    o["kvn_gT"] = np.ascontiguousarray(w["kv_norm_g"].reshape(NL, 128, 1))
    ukv = w["w_ukv"].reshape(NL, 128, NH, 128)
    o["w_ukn"] = np.ascontiguousarray(ukv[..., 0:64].reshape(NL, 128, NH * 64))
    o["w_ukv_v"] = np.ascontiguousarray(ukv[..., 64:128].reshape(NL, 128, NH * 64))

    def sw(g):
        gs = g.copy()
        gs[:, 64:80] = g[:, 80:96]
        gs[:, 80:96] = g[:, 64:80]
        return np.ascontiguousarray(np.stack([g, gs], axis=-1))
    o["gq"] = sw(w["qk_q_g"])
    o["gk"] = sw(w["qk_k_g"])
    o["w_o"] = np.ascontiguousarray(w["w_o"])
    o["w_out"] = np.ascontiguousarray(w["w_out"])
    o["ln2_g"] = np.ascontiguousarray(w["ln2_g"])
    o["w_ffn_in"] = np.ascontiguousarray(w["w_ffn_in"])
    o["w_ffn_out"] = np.ascontiguousarray(w["w_ffn_out"])
    o["zrows"] = np.zeros((16, D), np.float32)
    o["ident"] = np.eye(128, dtype=np.float32)
    return o


_CACHE = {}


def run_cores(xs_list, nsubs, weights, NTOK, NL):
    key = (NTOK, NL)
    if key not in _CACHE:
        _CACHE[key] = build(NTOK, NL)[0]
    nc = _CACHE[key]
    wl = layout_weights(weights)
    in_maps = []
    for x, ns in zip(xs_list, nsubs):
        m = dict(wl)
        m.update(core_tables(NTOK, ns))
        m["x"] = np.ascontiguousarray(x, dtype=np.float32)
        in_maps.append(m)
    res = run_bass_kernel_spmd(nc, in_maps, core_ids=list(range(len(in_maps))))
    return [r["y"] for r in res.results]


def kernel(x_prompt, x_sample, **weights):
    weights = {k: np.asarray(v, dtype=np.float32) for k, v in weights.items()}
    x_prompt = np.asarray(x_prompt, dtype=np.float32)
    x_sample = np.asarray(x_sample, dtype=np.float32)
    NTOK = 8192
    NL = weights["w_in"].shape[0]
    xs_list = [x_sample[i] for i in range(4)]
    nsubs = [1, 1, 1, 1]
    for i in range(2):
        xs_list.append(x_prompt[4 * i:4 * i + 4].reshape(NTOK, D))
        nsubs.append(4)
    for i in range(2):
        xs_list.append(x_prompt[4 * i:4 * i + 4].reshape(NTOK, D))
        nsubs.append(4)
    ys = run_cores(xs_list, nsubs, weights, NTOK, NL)
    y_sample = np.stack(ys[0:4], axis=0).astype(np.float32)
    y_prompt = np.concatenate([ys[4].reshape(4, 2048, D), ys[5].reshape(4, 2048, D)], axis=0).astype(np.float32)
    return (y_prompt, y_sample)
```

```python
import numpy as np
import ml_dtypes
import concourse.bass as bass
import concourse.mybir as mybir
from concourse.bass_utils import run_bass_kernel_spmd

F32 = mybir.dt.float32
BF16 = mybir.dt.bfloat16
AF = mybir.ActivationFunctionType
ALU = mybir.AluOpType

D = 1024
DIN = 6560
DC = 768
CK = 31
NH = 12
DFF = 2816
EPS = 1e-6
NWA = 3488


class Sem:
    def __init__(s, h, idx):
        s.h = h
        s.idx = idx


class Stamp:
    __slots__ = ("eng", "sem", "val")

    def __init__(s, eng):
        s.eng = eng
        s.sem = None
        s.val = None


class Buf:
    __slots__ = ("name", "w", "r", "slot")

    def __init__(s, name=""):
        s.name = name
        s.w = {}
        s.r = {}
        s.slot = None


class Slot:
    def __init__(s):
        s.sem = None
        s.cnt = 0


class KB:
    def __init__(s, nc):
        s.nc = nc
        s.eng = {"pe": nc.tensor, "act": nc.scalar, "dve": nc.vector, "pool": nc.gpsimd, "sp": nc.sync}
        s.nsem = 0
        s.sem = {}
        s.cnt = {}
        s.seen = {k: {} for k in s.eng}
        s.pending = {k: [] for k in s.eng}
        s.slots = []
        s.free_slots = []
        s.owners = []
        s.allsems = []
        for k in s.eng:
            s.sem[k] = s.newsem()
            s.cnt[k] = 0
        s.nwaits = 0
        s.nops = 0

    def newsem(s):
        h = s.nc.semaphore(f"ks{s.nsem}").__enter__()
        sm = Sem(h, s.nsem)
        s.nsem += 1
        s.allsems.append(sm)
        return sm

    def _need(s, e, need, st):
        if st.eng == e and not s.selfsync:
            return
        assert st.val is not None, "dependency on a PE op with no milestone yet"
        k = st.sem.idx
        if k not in need or need[k][1] < st.val:
            need[k] = (st.sem, st.val)

    selfsync = False

    def _waits(s, e, reads, writes):
        need = {}
        for b in reads:
            for st in b.w.values():
                s._need(e, need, st)
        for b in writes:
            for st in b.w.values():
                s._need(e, need, st)
            for st in b.r.values():
                s._need(e, need, st)
        seen = s.seen[e]
        for k, (sm, val) in need.items():
            if seen.get(k, 0) < val:
                s.eng[e].wait_ge(sm.h, val)
                seen[k] = val
                s.nwaits += 1

    def _stamp(s, st, reads, writes):
        for b in reads:
            b.r[st.eng] = st
        for b in writes:
            b.w[st.eng] = st

    def op(s, e, fn, reads=(), writes=(), inc=True, selfsync=False):
        s.selfsync = selfsync
        s._waits(e, reads, writes)
        s.selfsync = False
        ins = fn(s.eng[e])
        s.nops += 1
        st = Stamp(e)
        if inc:
            if s.cnt[e] >= 30000:
                s.sem[e] = s.newsem()
                s.cnt[e] = 0
            s.cnt[e] += 1
            ins.then_inc(s.sem[e].h, 1)
            st.sem = s.sem[e]
            st.val = s.cnt[e]
            for p in s.pending[e]:
                p.sem = st.sem
                p.val = st.val
            s.pending[e] = []
        else:
            s.pending[e].append(st)
        s._stamp(st, reads, writes)
        return ins

    def dma(s, q, out, in_, reads, writes, slotbuf):
        if slotbuf.slot is None:
            if s.free_slots:
                slotbuf.slot = s.free_slots.pop()
            else:
                slotbuf.slot = Slot()
                slotbuf.slot.sem = s.newsem()
                s.slots.append(slotbuf.slot)
            s.owners.append(slotbuf)
        sl = slotbuf.slot
        if sl.cnt >= 30000:
            sl.sem = s.newsem()
            sl.cnt = 0
        s._waits(q, reads, writes)
        ins = s.eng[q].dma_start(out=out, in_=in_)
        s.nops += 1
        sl.cnt += 16
        ins.then_inc(sl.sem.h, 16)
        st = Stamp(("dma", sl.sem.idx))
        st.sem = sl.sem
        st.val = sl.cnt
        s._stamp(st, reads, writes)
        return ins

    def barrier(s):
        for e in s.eng:
            assert not s.pending[e], f"pending stamps on {e} at barrier"
        tgt = []
        for e in s.eng:
            if s.cnt[e] > 0:
                tgt.append((e, s.sem[e], s.cnt[e]))
        for sl in s.slots:
            if sl.cnt > 0:
                tgt.append((None, sl.sem, sl.cnt))
        for e in s.eng:
            seen = s.seen[e]
            for (src, sm, val) in tgt:
                if src == e:
                    continue
                if seen.get(sm.idx, 0) < val:
                    s.eng[e].wait_ge(sm.h, val)
                    seen[sm.idx] = val
                    s.nwaits += 1
        for b in s.owners:
            s.free_slots.append(b.slot)
            b.slot = None
        s.owners = []


class Pool:
    def __init__(s, nc):
        s.nc = nc
        s.stack = []
        s.uid = 0
        s.base = 0

    def _alloc(s, n, dt):
        cm = s.nc.sbuf_tensor(f"t{s.uid}", [128, n], dt)
        s.uid += 1
        t = cm.__enter__()
        s.stack.append(cm)
        return t[:, :]

    def f32(s, n):
        return s._alloc(n, F32)

    def bf16(s, n):
        return s._alloc(n, BF16)

    def reset(s):
        while len(s.stack) > s.base:
            s.stack.pop().__exit__(None, None, None)


PASSES = '1234'


def build(NTOK, NL, final_only=True):
    T1 = 512
    T = 512
    NT1 = NTOK // T1
    NT = NTOK // T
    NKB = NTOK // 128
    nc = bass.Bass("TRN2", target_bir_lowering=False)

    def din(name, shape, dt=F32):
        return nc.dram_tensor(name, list(shape), dt, kind="ExternalInput").ap()

    x_in = din("x", [NTOK, D])
    y_out = nc.dram_tensor("y", [NTOK, D], F32, kind="ExternalOutput").ap()
    cosT_d = din("cosT", [96, NTOK])
    sinT_d = din("sinT", [96, NTOK])
    qmask_d = din("qmask", [4, NTOK])
    kmask_d = din("kmask", [4, NTOK])
    hflag_d = din("hflag", [32, NT1])
    zrows_d = din("zrows", [16, D])
    ident_d = din("ident", [128, 128])
    ln1_g = din("ln1_g", [NL, D])
    w_in = din("w_in", [NL, D, DIN])
    w_pe = din("w_pe", [NL, D, 2, 96])
    b_gate = din("b_gateT", [NL, 128, 24])
    conv_wT = din("conv_wT", [NL, 128, 6, CK])
    conv_b = din("conv_bT", [NL, 128, 6])
    cn_g = din("cn_gT", [NL, 128, 6])
    cn_b = din("cn_bT", [NL, 128, 6])
    w_co = din("w_conv_out", [NL, DC, D])
    sg_g = din("sg_norm_g", [NL, DC])
    w_spT = din("w_spT", [NL, 128, 6, 128])
    b_sp = din("b_spatial", [NL, 1, 6 * 128])
    w_so = din("w_sg_out", [NL, DC, D])
    qn_g = din("qn_gT", [NL, 128, 2])
    w_uq = din("w_uq", [NL, 256, NH * 96])
    w_uqs = din("w_uqs", [NL, 256, NH * 96])
    kvn_g = din("kvn_gT", [NL, 128, 1])
    w_ukn = din("w_ukn", [NL, 128, NH * 64])
    w_ukv = din("w_ukv_v", [NL, 128, NH * 64])
    gq = din("gq", [NL, 96, 2])
    gk = din("gk", [NL, 96, 2])
    w_o = din("w_o", [NL, DC, D])
    w_out = din("w_out", [NL, D, D])
    ln2_g = din("ln2_g", [NL, D])
    w_fi = din("w_ffn_in", [NL, D, 2 * DFF])
    w_fo = din("w_ffn_out", [NL, DFF, D])

    xs = [nc.dram_tensor(f"xs{i}", [NTOK, D], F32).ap() for i in range(2)]
    xmid = nc.dram_tensor("xmid", [NTOK, D], F32).ap()
    yaT = nc.dram_tensor("yaT", [D, NTOK], BF16).ap()
    ybT = nc.dram_tensor("ybT", [D, NTOK], BF16).ap()
    qT_d = nc.dram_tensor("qTd", [NH, 96, NTOK], BF16).ap()
    kT_d = nc.dram_tensor("kTd", [NH, 96, NTOK], BF16).ap()
    v_d = nc.dram_tensor("vd", [NTOK, DC], BF16).ap()
    oT_d = nc.dram_tensor("oTd", [DC, NTOK], BF16).ap()

    P = Pool(nc)
    banks = []
    bank_cms = []
    bank_gen = [0]

    sbig = []
    obank = []

    def newbanks(p2=False):
        while bank_cms:
            bank_cms.pop().__exit__(None, None, None)
        banks[:] = []
        sbig[:] = []
        obank[:] = []
        if p2:
            for i in range(3):
                cm = nc.psum_tensor(f"sbig{bank_gen[0]}_{i}", [128, 1024], F32)
                sbig.append(cm.__enter__())
                bank_cms.append(cm)
            for i in range(2):
                cm = nc.psum_tensor(f"obank{bank_gen[0]}_{i}", [128, 512], F32)
                obank.append(cm.__enter__())
                bank_cms.append(cm)
        else:
            for i in range(8):
                cm = nc.psum_tensor(f"bank{bank_gen[0]}_{i}", [128, 512], F32)
                banks.append(cm.__enter__())
                bank_cms.append(cm)
        bank_gen[0] += 1

    newbanks()
    bankB = [Buf(f"bank{i}") for i in range(8)]
    K = KB(nc)

    ident = P.bf16(128)
    ones = P.bf16(128)
    epsT = P.f32(1)
    hflag = P.f32(NT1)
    B_const = Buf("const")
    K.dma("pool", ident, ident_d, [], [B_const], B_const)
    K.op("dve", lambda e: e.memset(ones, 1.0), [], [B_const])
    K.op("dve", lambda e: e.memset(epsT, EPS), [], [B_const])
    K.dma("sp", hflag[0:32, :], hflag_d, [], [B_const], B_const)
    P.base = len(P.stack)

    bank_rr = [0]

    def nextbank(lst=(0, 1, 2, 3, 4, 5, 6, 7)):
        i = lst[bank_rr[0] % len(lst)]
        bank_rr[0] += 1
        return i

    D_x = {}

    def dbuf(key):
        if key not in D_x:
            D_x[key] = Buf(str(key))
        return D_x[key]

    def rstd_from(ss_ap, out_ap, n, eng_bufs_r, eng_bufs_w, scale, lnexp=False):
        if lnexp:
            K.op("act", lambda e: e.activation(out=out_ap, in_=ss_ap, func=AF.Ln, bias=epsT[0:n, :], scale=scale),
                 eng_bufs_r + [B_const], eng_bufs_w, selfsync=True)
            K.op("act", lambda e: e.activation(out=out_ap, in_=out_ap, func=AF.Exp, scale=-0.5), eng_bufs_w, eng_bufs_w)
            return
        K.op("act", lambda e: e.activation(out=out_ap, in_=ss_ap, func=AF.Sqrt, bias=epsT[0:n, :], scale=scale),
             eng_bufs_r + [B_const], eng_bufs_w, selfsync=True)
        K.op("dve", lambda e: e.reciprocal(out=out_ap, in_=out_ap), eng_bufs_w, eng_bufs_w, selfsync=True)

    def load_w_cast(dst3, src2, kch, ncols, bufw):
        K.dma("pool", dst3.rearrange("p (k n) -> p k n", n=ncols), src2.rearrange("(k p) n -> p k n", p=128),
              [], [bufw], bufw)

    def norm_transposed(xsrc, tok0, nb, g_rep, xin, xn, xnT, B_xin, B_xn, B_xnT, B_w, tilekey, ss, B_ss, junk, B_junk):
        Tn = nb * 128
        K.dma("sp", xin.rearrange("p (b d) -> p b d", d=D)[:, 0:nb, :],
              xsrc[tok0:tok0 + Tn, :].rearrange("(b p) d -> p b d", p=128), [dbuf(tilekey)], [B_xin], B_xin)
        for b in range(nb):
            xb = xin[:, b * D:(b + 1) * D]
            K.op("act", lambda e: e.activation(out=junk, in_=xb, func=AF.Square, accum_out=ss[:, b:b + 1]),
                 [B_xin], [B_junk, B_ss])
        rstd_from(ss[:, 0:nb], ss[:, 0:nb], 128, [B_ss], [B_ss], 1.0 / D)
        for b in range(nb):
            xb = xin[:, b * D:(b + 1) * D]
            K.op("dve", lambda e: e.scalar_tensor_tensor(out=xn[:, b * D:(b + 1) * D], in0=xb, scalar=ss[:, b:b + 1],
                                                         in1=g_rep, op0=ALU.mult, op1=ALU.mult),
                 [B_xin, B_ss, B_w], [B_xn], selfsync=True)
        for c in range(8):
            bk = nextbank()
            pb = banks[bk][:, 0:256].bitcast(BF16)
            for b in range(nb):
                K.op("pe", lambda e: e.transpose(out=pb[:, b * 128:(b + 1) * 128],
                                                 in_=xn[:, b * D + c * 128:b * D + (c + 1) * 128], identity=ident),
                     [B_xn, B_const], [bankB[bk]], inc=(b == nb - 1))
            eng = "act" if c % 2 == 0 else "dve"
            if eng == "act":
                K.op("act", lambda e: e.copy(out=xnT[:, c * Tn:(c + 1) * Tn], in_=pb[:, 0:Tn]), [bankB[bk]], [B_xnT])
            else:
                K.op("dve", lambda e: e.tensor_copy(out=xnT[:, c * Tn:(c + 1) * Tn], in_=pb[:, 0:Tn]), [bankB[bk]], [B_xnT])

    for l in range(NL):
        xsrc = x_in if l == 0 else xs[(l - 1) % 2]
        xdst = y_out if l == NL - 1 else xs[l % 2]
        skey = ("x", l)
        dkey = ("x", l + 1)

        if '1' in PASSES:
          for sub in ('a', 'b'):
              K.barrier()
              P.reset()
              newbanks()
              nb = T1 // 128
              isa = (sub == 'a')

              def ab(use, n, bf):
                  if not use:
                      return None
                  return P.bf16(n) if bf else P.f32(n)

              NWX = 3072 if isa else 416
              wA = P.bf16(8 * NWX)
              wpe = ab(not isa, 8 * 192, True)
              wco = ab(isa, 6 * D, True)
              wso = ab(isa, 6 * D, True)
              wsp = ab(isa, 6 * 128, True)
              wuq = ab(not isa, 2 * NH * 96, True)
              wuqs = ab(not isa, 2 * NH * 96, True)
              wukn = ab(not isa, NH * 64, True)
              wukv = ab(not isa, NH * 64, True)
              bsp = ab(isa, 6 * 128, True)
              ln1rep = P.f32(D)
              sgrep = ab(isa, DC, False)
              cw = ab(isa, 6 * CK, False)
              cb = ab(isa, 6, False)
              cng = ab(isa, 6, False)
              cnb = ab(isa, 6, False)
              qng = ab(not isa, 2, False)
              kvng = ab(not isa, 1, False)
              gqt = ab(not isa, 2, False)
              gkt = ab(not isa, 2, False)
              B_w = Buf("w1")
              K.dma("sp", ln1rep, ln1_g[l:l + 1, :].broadcast_to([128, D]), [], [B_w], B_w)
              if isa:
                  load_w_cast(wA, w_in[l][:, 3072:6144], 8, NWX, B_w)
                  load_w_cast(wco, w_co[l], 6, D, B_w)
                  load_w_cast(wso, w_so[l], 6, D, B_w)
                  K.dma("pool", wsp, w_spT[l].rearrange("p g q -> p (g q)"), [], [B_w], B_w)
                  K.dma("pool", bsp[0:1, :], b_sp[l], [], [B_w], B_w)
                  K.dma("sp", sgrep, sg_g[l:l + 1, :].broadcast_to([128, DC]), [], [B_w], B_w)
                  K.dma("sp", cw, conv_wT[l].rearrange("p c j -> p (c j)"), [], [B_w], B_w)
                  K.dma("sp", cb, conv_b[l], [], [B_w], B_w)
                  K.dma("sp", cng, cn_g[l], [], [B_w], B_w)
                  K.dma("sp", cnb, cn_b[l], [], [B_w], B_w)
              else:
                  load_w_cast(wA, w_in[l][:, 6144:DIN], 8, NWX, B_w)
                  K.dma("pool", wpe.rearrange("p (k n) -> p k n", n=192),
                        w_pe[l].rearrange("(k p) a n -> p k (a n)", p=128), [], [B_w], B_w)
                  load_w_cast(wuq, w_uq[l], 2, NH * 96, B_w)
                  load_w_cast(wuqs, w_uqs[l], 2, NH * 96, B_w)
                  K.dma("pool", wukn, w_ukn[l], [], [B_w], B_w)
                  K.dma("pool", wukv, w_ukv[l], [], [B_w], B_w)
                  K.dma("sp", qng, qn_g[l], [], [B_w], B_w)
                  K.dma("sp", kvng, kvn_g[l], [], [B_w], B_w)
                  K.dma("sp", gqt[0:96, :], gq[l], [], [B_w], B_w)
                  K.dma("sp", gkt[0:96, :], gk[l], [], [B_w], B_w)
                  K.op("dve", lambda e: e.tensor_single_scalar(out=gqt[0:96, :], in_=gqt[0:96, :], scalar=float(96 ** -0.5),
                                                               op=ALU.mult), [B_w], [B_w])

              xin = P.f32(nb * D)
              xn = P.bf16(nb * D)
              xnT = P.bf16(8 * T1)
              ss = P.f32(4)
              junk = P.bf16(D)
              sqb = [P.bf16(T1) for _ in range(2)]
              xh = ab(isa, D, False)
              xnh = ab(isa, D, True)
              xnTh = ab(isa, 8 * 32, True)
              ssh = ab(isa, 1, False)
              ZW = T1 + 32
              zpad = ab(isa, 6 * ZW, False)
              sgs = [ab(isa, T1 + 32, False) for _ in range(2)]
              acc = ab(isa, 6 * T1, False)
              acb = [ab(isa, T1, True) for _ in range(2)]
              mean = ab(isa, T1, False)
              rstd = ab(isa, T1, False)
              tln = [ab(isa, T1, False) for _ in range(2)]
              aconv = ab(isa, 6 * T1, True)
              ubf = ab(isa, 6 * T1, True)
              vf = ab(isa, DC, False)
              ssv = ab(isa, 1, False)
              vn = ab(isa, nb * DC, True)
              uv = ab(isa, 6 * T1, True)
              yst = [ab(isa, 8 * T1, True)] * 2
              qn = ab(not isa, 2 * T1, True)
              kvn = ab(not isa, T1, True)
              rq = ab(not isa, T1, False)
              sqpe = ab(not isa, T1, True)
              kr = ab(not isa, T1, False)
              t1s = [ab(not isa, T1, False) for _ in range(2)]
              t2s = [ab(not isa, T1, False) for _ in range(2)]
              sqh = [ab(not isa, T1, True) for _ in range(2)]
              rh = [ab(not isa, T1, False) for _ in range(2)]
              qst = [ab(not isa, T1, True) for _ in range(3)]
              kst = [ab(not isa, T1, True) for _ in range(3)]
              vst = ab(not isa, nb * DC, True)
              cosT = ab(not isa, T1, False)
              sinT = ab(not isa, T1, False)
              (B_xin, B_xn, B_xnT, B_xh, B_xnh, B_xnTh, B_ss, B_ssh, B_junk, B_mean, B_rstd, B_aconv, B_ubf, B_vf,
               B_ssv, B_vn, B_uv, B_qn, B_kvn, B_rq, B_sqpe, B_kr, B_vst, B_cs) = [Buf(f"p1_{i}") for i in range(24)]
              B_zpad = [Buf() for _ in range(6)]
              B_acc = [Buf() for _ in range(6)]
              B_sgs = [Buf() for _ in range(2)]
              B_sqb = [Buf() for _ in range(2)]
              B_acb = [Buf() for _ in range(2)]
              B_tln = [Buf() for _ in range(2)]
              B_yst = [Buf()] * 2
              B_t1 = [Buf() for _ in range(2)]
              B_t2 = [Buf() for _ in range(2)]
              B_sqh = [Buf() for _ in range(2)]
              B_rh = [Buf() for _ in range(2)]
              B_qst = [Buf() for _ in range(3)]
              B_kst = [Buf() for _ in range(3)]
              rr = [0, 0, 0, 0]

              wA3 = wA.rearrange("p (k n) -> p k n", n=NWX)
              wpe3 = wpe.rearrange("p (k n) -> p k n", n=192) if not isa else None
              xnT3 = xnT.rearrange("p (k n) -> p k n", n=T1)
              xnTh3 = xnTh.rearrange("p (k n) -> p k n", n=32) if isa else None

              def mm8(bk, cols, ncol, col0, M, lhs_src, n0, n, rhs3, first=True, last=True, wcol0=None):
                  for k in range(8):
                      K.op("pe", lambda e: e.matmul(banks[bk][0:M, col0:col0 + n], lhsT=lhs_src[:, k, cols:cols + ncol],
                                                    rhs=rhs3[:, k, n0:n0 + n], start=(k == 0), stop=(k == 7)),
                           [B_w, B_xnT, B_xnTh], [bankB[bk]], inc=(k == 7 and last))

              for t in range(NT1):
                  tok0 = t * T1
                  norm_transposed(xsrc, tok0, nb, ln1rep, xin, xn, xnT, B_xin, B_xn, B_xnT, B_w, (skey, tok0 // T), ss, B_ss,
                                  junk, B_junk)
                  if sub == 'a':
                      if t > 0:
                          K.dma("sp", xh[0:15, :], xsrc[tok0 - 15:tok0, :], [dbuf((skey, (tok0 - 15) // T))], [B_xh], B_xh)
                      else:
                          K.dma("sp", xh[0:15, :], zrows_d[0:15, :], [], [B_xh], B_xh)
                      if t < NT1 - 1:
                          K.dma("sp", xh[15:30, :], xsrc[tok0 + T1:tok0 + T1 + 15, :], [dbuf((skey, (tok0 + T1) // T))], [B_xh], B_xh)
                      else:
                          K.dma("sp", xh[15:30, :], zrows_d[0:15, :], [], [B_xh], B_xh)
                      K.op("act", lambda e: e.activation(out=junk[0:30, :], in_=xh[0:30, :], func=AF.Square, accum_out=ssh[0:30, :]),
                           [B_xh], [B_junk, B_ssh])
                      rstd_from(ssh[0:30, :], ssh[0:30, :], 30, [B_ssh], [B_ssh], 1.0 / D)
                      K.op("dve", lambda e: e.tensor_tensor(out=ssh[0:30, :], in0=ssh[0:30, :], in1=hflag[0:30, t:t + 1], op=ALU.mult),
                           [B_ssh, B_const], [B_ssh], selfsync=True)
                      K.op("dve", lambda e: e.scalar_tensor_tensor(out=xnh[0:30, :], in0=xh[0:30, :], scalar=ssh[0:30, 0:1],
                                                                   in1=ln1rep[0:30, :], op0=ALU.mult, op1=ALU.mult),
                           [B_xh, B_ssh, B_w], [B_xnh], selfsync=True)
                      bk = nextbank()
                      pb = banks[bk][:, 0:128].bitcast(BF16)
                      for c in range(8):
                          K.op("pe", lambda e: e.transpose(out=pb[:, c * 32:c * 32 + 30], in_=xnh[0:30, c * 128:(c + 1) * 128],
                                                           identity=ident[0:30, 0:30]),
                               [B_xnh, B_const], [bankB[bk]], inc=(c == 7))
                      K.op("dve", lambda e: e.tensor_copy(out=xnTh3[:, :, 0:30], in_=pb.rearrange("p (k n) -> p k n", n=32)[:, :, 0:30]),
                           [bankB[bk]], [B_xnTh])

                      for c in range(6):
                          ba = nextbank()
                          mm8(ba, c * 128, 128, 0, 128, wA3, 0, T1, xnT3)
                          bg = nextbank()
                          mm8(bg, 768 + c * 128, 128, 0, 128, wA3, 0, T1, xnT3)
                          bh = nextbank()
                          mm8(bh, c * 128, 128, 0, 128, wA3, 0, 30, xnTh3)
                          mm8(bh, 768 + c * 128, 128, 32, 128, wA3, 0, 30, xnTh3)
                          si = rr[0] % 2
                          rr[0] += 1
                          K.op("act", lambda e: e.activation(out=sgs[si][:, 0:T1], in_=banks[bg][:, 0:T1], func=AF.Sigmoid),
                               [bankB[bg]], [B_sgs[si]])
                          K.op("act", lambda e: e.activation(out=sgs[si][:, T1:T1 + 30], in_=banks[bh][:, 32:62], func=AF.Sigmoid),
                               [bankB[bh]], [B_sgs[si]])
                          zc = zpad[:, c * ZW:(c + 1) * ZW]
                          K.op("dve", lambda e: e.tensor_tensor(out=zc[:, 15:15 + T1], in0=banks[ba][:, 0:T1], in1=sgs[si][:, 0:T1],
                                                                op=ALU.mult), [bankB[ba], B_sgs[si]], [B_zpad[c]])
                          K.op("dve", lambda e: e.tensor_tensor(out=zc[:, 0:15], in0=banks[bh][:, 0:15], in1=sgs[si][:, T1:T1 + 15],
                                                                op=ALU.mult), [bankB[bh], B_sgs[si]], [B_zpad[c]])
                          K.op("dve", lambda e: e.tensor_tensor(out=zc[:, 15 + T1:30 + T1], in0=banks[bh][:, 15:30],
                                                                in1=sgs[si][:, T1 + 15:T1 + 30], op=ALU.mult),
                               [bankB[bh], B_sgs[si]], [B_zpad[c]])
                      for c in range(6):
                          en = "dve"
                          zc = zpad[:, c * ZW:(c + 1) * ZW]
                          ac = acc[:, c * T1:(c + 1) * T1]
                          K.op(en, lambda e: e.tensor_scalar(out=ac, in0=zc[:, 0:T1], scalar1=cw[:, c * CK:c * CK + 1],
                                                             scalar2=cb[:, c:c + 1], op0=ALU.mult, op1=ALU.add),
                               [B_zpad[c], B_w], [B_acc[c]])
                          for j in range(1, CK):
                              if en == "dve":
                                  K.op(en, lambda e: e.scalar_tensor_tensor(out=ac, in0=zc[:, j:j + T1],
                                                                            scalar=cw[:, c * CK + j:c * CK + j + 1],
                                                                            in1=ac, op0=ALU.mult, op1=ALU.add),
                                       [B_zpad[c], B_w], [B_acc[c]])
                              else:
                                  K.op(en, lambda e: e.tensor_scalar(out=tln[0], in0=zc[:, j:j + T1], scalar1=cw[:, c * CK + j:c * CK + j + 1],
                                                                     scalar2=None, op0=ALU.mult), [B_zpad[c], B_w], [B_tln[0]])
                                  K.op(en, lambda e: e.tensor_tensor(out=ac, in0=ac, in1=tln[0], op=ALU.add), [B_tln[0]], [B_acc[c]])
                      b1 = nextbank()
                      b2 = nextbank()
                      for c in range(6):
                          ac = acc[:, c * T1:(c + 1) * T1]
                          si = rr[1] % 2
                          rr[1] += 1
                          K.op("act", lambda e: e.activation(out=sqb[si], in_=ac, func=AF.Square), [B_acc[c]], [B_sqb[si]])
                          K.op("act", lambda e: e.copy(out=acb[si], in_=ac), [B_acc[c]], [B_acb[si]])
                          K.op("pe", lambda e: e.matmul(banks[b1][:, 0:T1], lhsT=ones, rhs=acb[si], start=(c == 0), stop=(c == 5)),
                               [B_const, B_acb[si]], [bankB[b1]], inc=True)
                          K.op("pe", lambda e: e.matmul(banks[b2][:, 0:T1], lhsT=ones, rhs=sqb[si], start=(c == 0), stop=(c == 5)),
                               [B_const, B_sqb[si]], [bankB[b2]], inc=True)
                      K.op("act", lambda e: e.mul(out=mean, in_=banks[b1][:, 0:T1], mul=1.0 / DC), [bankB[b1]], [B_mean])
                      K.op("dve", lambda e: e.tensor_tensor(out=rstd, in0=mean, in1=mean, op=ALU.mult), [B_mean], [B_rstd])
                      K.op("dve", lambda e: e.scalar_tensor_tensor(out=rstd, in0=banks[b2][:, 0:T1], scalar=1.0 / DC, in1=rstd,
                                                                   op0=ALU.mult, op1=ALU.subtract), [bankB[b2], B_rstd], [B_rstd])
                      rstd_from(rstd, rstd, 128, [B_rstd], [B_rstd], 1.0)
                      for c in range(6):
                          ac = acc[:, c * T1:(c + 1) * T1]
                          si = rr[2] % 2
                          rr[2] += 1
                          K.op("pool", lambda e: e.tensor_tensor(out=tln[si], in0=ac, in1=mean, op=ALU.subtract),
                               [B_acc[c], B_mean], [B_tln[si]])
                          K.op("pool", lambda e: e.tensor_tensor(out=tln[si], in0=tln[si], in1=rstd, op=ALU.mult),
                               [B_tln[si], B_rstd], [B_tln[si]])
                          K.op("act", lambda e: e.activation(out=aconv[:, c * T1:(c + 1) * T1], in_=tln[si], func=AF.Silu,
                                                             scale=cng[:, c:c + 1], bias=cnb[:, c:c + 1]),
                               [B_tln[si], B_w], [B_aconv])

                      def out_proj(src, B_src, w, dstT, key):
                          w3 = w.rearrange("p (k n) -> p k n", n=D)
                          yi = rr[3] % 2
                          rr[3] += 1
                          for oc in range(8):
                              bk = nextbank()
                              for c in range(6):
                                  K.op("pe", lambda e: e.matmul(banks[bk][:, 0:T1], lhsT=w3[:, c, oc * 128:(oc + 1) * 128],
                                                                rhs=src[:, c * T1:(c + 1) * T1], start=(c == 0), stop=(c == 5)),
                                       [B_w, B_src], [bankB[bk]], inc=(c == 5))
                              if oc % 2 == 0:
                                  K.op("act", lambda e: e.copy(out=yst[yi][:, oc * T1:(oc + 1) * T1], in_=banks[bk][:, 0:T1]),
                                       [bankB[bk]], [B_yst[yi]])
                              else:
                                  K.op("dve", lambda e: e.tensor_copy(out=yst[yi][:, oc * T1:(oc + 1) * T1], in_=banks[bk][:, 0:T1]),
                                       [bankB[bk]], [B_yst[yi]])
                          K.dma("sp", dstT.rearrange("(c p) n -> p c n", p=128)[:, :, tok0:tok0 + T1],
                                yst[yi].rearrange("p (c n) -> p c n", n=T1), [B_yst[yi]], [dbuf((key, l, tok0 // T))], B_yst[yi])

                      out_proj(aconv, B_aconv, wco, yaT, "ya")

                      for c in range(6):
                          bk = nextbank()
                          mm8(bk, 1536 + c * 128, 128, 0, 128, wA3, 0, T1, xnT3)
                          K.op("act", lambda e: e.activation(out=ubf[:, c * T1:(c + 1) * T1], in_=banks[bk][:, 0:T1],
                                                             func=AF.Gelu_apprx_tanh), [bankB[bk]], [B_ubf])
                      for b in range(nb):
                          bA = nextbank()
                          bB = nextbank()
                          for k in range(8):
                              K.op("pe", lambda e: e.matmul(banks[bA][:, 0:512], lhsT=xnT3[:, k, b * 128:(b + 1) * 128],
                                                            rhs=wA3[:, k, 2304:2816], start=(k == 0), stop=(k == 7)),
                                   [B_w, B_xnT], [bankB[bA]], inc=(k == 7))
                          for k in range(8):
                              K.op("pe", lambda e: e.matmul(banks[bB][:, 0:256], lhsT=xnT3[:, k, b * 128:(b + 1) * 128],
                                                            rhs=wA3[:, k, 2816:3072], start=(k == 0), stop=(k == 7)),
                                   [B_w, B_xnT], [bankB[bB]], inc=(k == 7))
                          K.op("act", lambda e: e.activation(out=vf[:, 0:512], in_=banks[bA][:, 0:512], func=AF.Gelu_apprx_tanh),
                               [bankB[bA]], [B_vf])
                          K.op("act", lambda e: e.activation(out=vf[:, 512:768], in_=banks[bB][:, 0:256], func=AF.Gelu_apprx_tanh),
                               [bankB[bB]], [B_vf])
                          K.op("act", lambda e: e.activation(out=junk[:, 0:DC], in_=vf, func=AF.Square, accum_out=ssv),
                               [B_vf], [B_junk, B_ssv])
                          rstd_from(ssv, ssv, 128, [B_ssv], [B_ssv], 1.0 / DC)
                          K.op("dve", lambda e: e.scalar_tensor_tensor(out=vn[:, b * DC:(b + 1) * DC], in0=vf, scalar=ssv[:, 0:1],
                                                                       in1=sgrep, op0=ALU.mult, op1=ALU.mult),
                               [B_vf, B_ssv, B_w], [B_vn], selfsync=True)
                      for g in range(6):
                          bk = nextbank()
                          for b in range(nb):
                              K.op("pe", lambda e: e.matmul(banks[bk][:, b * 128:(b + 1) * 128], lhsT=ones[0:1, :],
                                                            rhs=bsp[0:1, g * 128:(g + 1) * 128], start=True, stop=False),
                                   [B_const, B_w], [bankB[bk]], inc=False)
                              K.op("pe", lambda e: e.matmul(banks[bk][:, b * 128:(b + 1) * 128],
                                                            lhsT=vn[:, b * DC + g * 128:b * DC + (g + 1) * 128],
                                                            rhs=wsp[:, g * 128:(g + 1) * 128], start=False, stop=True),
                                   [B_vn, B_w], [bankB[bk]], inc=(b == nb - 1))
                          K.op("dve", lambda e: e.tensor_tensor(out=uv[:, g * T1:(g + 1) * T1], in0=banks[bk][:, 0:T1],
                                                                in1=ubf[:, g * T1:(g + 1) * T1], op=ALU.mult),
                               [bankB[bk], B_ubf], [B_uv])
                      out_proj(uv, B_uv, wso, ybT, "yb")

                  else:
                      K.dma("sp", cosT[0:96, :], cosT_d[:, tok0:tok0 + T1], [], [B_cs], B_cs)
                      K.dma("sp", sinT[0:96, :], sinT_d[:, tok0:tok0 + T1], [], [B_cs], B_cs)
                      bq = [nextbank(), nextbank()]
                      for c in range(2):
                          mm8(bq[c], c * 128, 128, 0, 128, wA3, 0, T1, xnT3)
                      bss = nextbank()
                      for c in range(2):
                          si = rr[1] % 2
                          rr[1] += 1
                          K.op("act", lambda e: e.activation(out=sqb[si], in_=banks[bq[c]][:, 0:T1], func=AF.Square),
                               [bankB[bq[c]]], [B_sqb[si]])
                          K.op("pe", lambda e: e.matmul(banks[bss][:, 0:T1], lhsT=ones, rhs=sqb[si], start=(c == 0), stop=(c == 1)),
                               [B_const, B_sqb[si]], [bankB[bss]], inc=True)
                      rstd_from(banks[bss][:, 0:T1], rq, 128, [bankB[bss]], [B_rq], 1.0 / 256, lnexp=True)
                      for c in range(2):
                          K.op("dve", lambda e: e.scalar_tensor_tensor(out=qn[:, c * T1:(c + 1) * T1], in0=banks[bq[c]][:, 0:T1],
                                                                       scalar=qng[:, c:c + 1], in1=rq, op0=ALU.mult, op1=ALU.mult),
                               [bankB[bq[c]], B_rq, B_w], [B_qn])
                      bkv = nextbank()
                      mm8(bkv, 256, 128, 0, 128, wA3, 0, T1, xnT3)
                      bss = nextbank()
                      si = rr[1] % 2
                      rr[1] += 1
                      K.op("act", lambda e: e.activation(out=sqb[si], in_=banks[bkv][:, 0:T1], func=AF.Square),
                           [bankB[bkv]], [B_sqb[si]])
                      K.op("pe", lambda e: e.matmul(banks[bss][:, 0:T1], lhsT=ones, rhs=sqb[si], start=True, stop=True),
                           [B_const, B_sqb[si]], [bankB[bss]], inc=True)
                      rstd_from(banks[bss][:, 0:T1], rq, 128, [bankB[bss]], [B_rq], 1.0 / 128, lnexp=True)
                      K.op("dve", lambda e: e.scalar_tensor_tensor(out=kvn, in0=banks[bkv][:, 0:T1], scalar=kvng[:, 0:1], in1=rq,
                                                                   op0=ALU.mult, op1=ALU.mult), [bankB[bkv], B_rq, B_w], [B_kvn])
                      bpe = nextbank()
                      mm8(bpe, 0, 96, 0, 96, wpe3, 0, T1, xnT3)
                      bpes = nextbank()
                      mm8(bpes, 96, 96, 0, 96, wpe3, 0, T1, xnT3)
                      K.op("act", lambda e: e.activation(out=sqpe[64:96, :], in_=banks[bpe][64:96, 0:T1], func=AF.Square),
                           [bankB[bpe]], [B_sqpe])
                      K.op("dve", lambda e: e.scalar_tensor_tensor(out=kr[64:96, :], in0=banks[bpe][64:96, 0:T1], scalar=gkt[64:96, 0:1],
                                                                   in1=cosT[64:96, :], op0=ALU.mult, op1=ALU.mult),
                           [bankB[bpe], B_w, B_cs], [B_kr])
                      K.op("dve", lambda e: e.scalar_tensor_tensor(out=t1s[0][64:96, :], in0=banks[bpes][64:96, 0:T1],
                                                                   scalar=gkt[64:96, 1:2], in1=sinT[64:96, :], op0=ALU.mult, op1=ALU.mult),
                           [bankB[bpes], B_w, B_cs], [B_t1[0]])
                      K.op("dve", lambda e: e.tensor_tensor(out=kr[64:96, :], in0=kr[64:96, :], in1=t1s[0][64:96, :], op=ALU.add),
                           [B_kr, B_t1[0]], [B_kr])
                      for b in range(nb):
                          bA = nextbank()
                          bB = nextbank()
                          K.op("pe", lambda e: e.matmul(banks[bA][:, 0:512], lhsT=kvn[:, b * 128:(b + 1) * 128], rhs=wukv[:, 0:512],
                                                        start=True, stop=True), [B_kvn, B_w], [bankB[bA]])
                          K.op("pe", lambda e: e.matmul(banks[bB][:, 0:256], lhsT=kvn[:, b * 128:(b + 1) * 128], rhs=wukv[:, 512:768],
                                                        start=True, stop=True), [B_kvn, B_w], [bankB[bB]])
                          K.op("act", lambda e: e.copy(out=vst[:, b * DC:b * DC + 512], in_=banks[bA][:, 0:512]), [bankB[bA]], [B_vst])
                          K.op("dve", lambda e: e.tensor_copy(out=vst[:, b * DC + 512:(b + 1) * DC], in_=banks[bB][:, 0:256]),
                               [bankB[bB]], [B_vst])
                      K.dma("sp", v_d[tok0:tok0 + T1, :].rearrange("(b p) d -> p b d", p=128), vst.rearrange("p (b d) -> p b d", d=DC),
                            [B_vst], [dbuf(("v", l))], B_vst)
                      wuq3 = wuq.rearrange("p (k n) -> p k n", n=NH * 96)
                      wuqs3 = wuqs.rearrange("p (k n) -> p k n", n=NH * 96)
                      for h in range(NH):
                          i2 = h % 2
                          i3 = h % 3
                          bkn = nextbank()
                          K.op("pe", lambda e: e.matmul(banks[bkn][0:64, 0:T1], lhsT=wukn[:, h * 64:(h + 1) * 64], rhs=kvn,
                                                        start=True, stop=True), [B_w, B_kvn], [bankB[bkn]])
                          K.op("act", lambda e: e.activation(out=sqh[i2][0:64, :], in_=banks[bkn][0:64, 0:T1], func=AF.Square),
                               [bankB[bkn]], [B_sqh[i2]])
                          bs = nextbank()
                          K.op("pe", lambda e: e.matmul(banks[bs][0:96, 0:T1], lhsT=ones[0:64, 0:96], rhs=sqh[i2][0:64, :],
                                                        start=True, stop=False), [B_const, B_sqh[i2]], [bankB[bs]], inc=False)
                          K.op("pe", lambda e: e.matmul(banks[bs][0:96, 0:T1], lhsT=ones[64:96, 0:96], rhs=sqpe[64:96, :],
                                                        start=False, stop=True), [B_const, B_sqpe], [bankB[bs]])
                          rstd_from(banks[bs][0:96, 0:T1], rh[i2][0:96, :], 96, [bankB[bs]], [B_rh[i2]], 1.0 / 96, lnexp=True)
                          K.op("dve", lambda e: e.scalar_tensor_tensor(out=kst[i3][0:64, :], in0=banks[bkn][0:64, 0:T1],
                                                                       scalar=gkt[0:64, 0:1], in1=rh[i2][0:64, :],
                                                                       op0=ALU.mult, op1=ALU.mult),
                               [bankB[bkn], B_w, B_rh[i2]], [B_kst[i3]])
                          K.op("pool", lambda e: e.tensor_tensor(out=kst[i3][64:96, :], in0=kr[64:96, :], in1=rh[i2][64:96, :],
                                                                 op=ALU.mult), [B_kr, B_rh[i2]], [B_kst[i3]])
                          K.dma("sp", kT_d[h, :, tok0:tok0 + T1], kst[i3][0:96, :], [B_kst[i3]], [dbuf(("k", l, h))], B_kst[i3])
                          bqh = nextbank()
                          bqs = nextbank()
                          for c in range(2):
                              K.op("pe", lambda e: e.matmul(banks[bqh][0:96, 0:T1], lhsT=wuq3[:, c, h * 96:(h + 1) * 96],
                                                            rhs=qn[:, c * T1:(c + 1) * T1], start=(c == 0), stop=(c == 1)),
                                   [B_w, B_qn], [bankB[bqh]], inc=(c == 1))
                          for c in range(2):
                              K.op("pe", lambda e: e.matmul(banks[bqs][0:96, 0:T1], lhsT=wuqs3[:, c, h * 96:(h + 1) * 96],
                                                            rhs=qn[:, c * T1:(c + 1) * T1], start=(c == 0), stop=(c == 1)),
                                   [B_w, B_qn], [bankB[bqs]], inc=(c == 1))
                          j2 = (h + 1) % 2
                          K.op("act", lambda e: e.activation(out=sqh[j2][0:96, :], in_=banks[bqh][0:96, 0:T1], func=AF.Square),
                               [bankB[bqh]], [B_sqh[j2]])
                          bs = nextbank()
                          K.op("pe", lambda e: e.matmul(banks[bs][0:96, 0:T1], lhsT=ones[0:96, 0:96], rhs=sqh[j2][0:96, :],
                                                        start=True, stop=True), [B_const, B_sqh[j2]], [bankB[bs]])
                          rstd_from(banks[bs][0:96, 0:T1], rh[j2][0:96, :], 96, [bankB[bs]], [B_rh[j2]], 1.0 / 96, lnexp=True)
                          K.op("dve", lambda e: e.scalar_tensor_tensor(out=t1s[i2][0:96, :], in0=banks[bqh][0:96, 0:T1],
                                                                       scalar=gqt[0:96, 0:1], in1=cosT[0:96, :],
                                                                       op0=ALU.mult, op1=ALU.mult),
                               [bankB[bqh], B_w, B_cs], [B_t1[i2]])
                          K.op("dve", lambda e: e.scalar_tensor_tensor(out=t2s[i2][0:96, :], in0=banks[bqs][0:96, 0:T1],
                                                                       scalar=gqt[0:96, 1:2], in1=sinT[0:96, :],
                                                                       op0=ALU.mult, op1=ALU.mult),
                               [bankB[bqs], B_w, B_cs], [B_t2[i2]])
                          K.op("pool", lambda e: e.tensor_tensor(out=t1s[i2][0:96, :], in0=t1s[i2][0:96, :], in1=t2s[i2][0:96, :],
                                                                 op=ALU.add), [B_t1[i2], B_t2[i2]], [B_t1[i2]])
                          K.op("pool", lambda e: e.tensor_tensor(out=qst[i3][0:96, :], in0=t1s[i2][0:96, :], in1=rh[j2][0:96, :],
                                                                 op=ALU.mult), [B_t1[i2], B_rh[j2]], [B_qst[i3]])
                          K.dma("sp", qT_d[h, :, tok0:tok0 + T1], qst[i3][0:96, :], [B_qst[i3]], [dbuf(("q", l, h))], B_qst[i3])

        if '2' in PASSES:
            K.barrier()
            P.reset()
            newbanks(p2=True)
            kT = [P.bf16(NTOK) for _ in range(2)]
            vA = [P.bf16(NKB * 128) for _ in range(2)]
            qT = [P.bf16(T) for _ in range(3)]
            pT = [P.bf16(2 * T) for _ in range(3)]
            rinv = [P.f32(T) for _ in range(2)]
            ost = [P.bf16(T) for _ in range(2)]
            B_kT = [Buf() for _ in range(2)]
            B_vA = [Buf() for _ in range(2)]
            B_qT = [Buf() for _ in range(3)]
            B_pT = [Buf() for _ in range(3)]
            B_sb = [Buf() for _ in range(3)]
            B_ob = [Buf() for _ in range(2)]
            B_rinv = [Buf() for _ in range(2)]
            B_ost = [Buf() for _ in range(2)]
            for i in range(2):
                K.dma("pool", kT[i][96:100, :], kmask_d, [], [B_kT[i]], B_kT[i])
                K.op("dve", lambda e: e.memset(vA[i].rearrange("p (k d) -> p k d", d=128)[:, :, 64:128], 1.0), [], [B_vA[i]])
            NP = NKB // 2
            NTILE = NH * NT

            def load_head(h):
                hs = h % 2
                K.dma("sp", kT[hs][0:96, :], kT_d[h], [dbuf(("k", l, h))], [B_kT[hs]], B_kT[hs])
                for part in range(0, NKB, 16):
                    pe_ = min(NKB, part + 16)
                    K.dma("sp", vA[hs].rearrange("p (k d) -> p k d", d=128)[:, part:pe_, 0:64],
                          v_d[part * 128:pe_ * 128, h * 64:(h + 1) * 64].rearrange("(k p) d -> p k d", p=128),
                          [dbuf(("v", l))], [B_vA[hs]], B_vA[hs])

            def load_q(n):
                h, qt = divmod(n, NT)
                qs = n % 3
                K.dma("sp", qT[qs][0:96, :], qT_d[h, :, qt * T:(qt + 1) * T], [dbuf(("q", l, h))], [B_qT[qs]], B_qT[qs])
                K.dma("pool", qT[qs][96:100, :], qmask_d[:, qt * T:(qt + 1) * T], [], [B_qT[qs]], B_qT[qs])

            def S_(i):
                n, p = divmod(i, NP)
                h, qt = divmod(n, NT)
                hs = h % 2
                qs = n % 3
                if p == 0:
                    if n + 1 < NTILE:
                        load_q(n + 1)
                    if qt == NT - 1 and h + 1 < NH:
                        load_head(h + 1)
                sb = i % 3
                for j in range(2):
                    kb = 2 * p + j
                    K.op("pe", lambda e: e.matmul(sbig[sb][:, j * T:(j + 1) * T], lhsT=kT[hs][0:100, kb * 128:(kb + 1) * 128],
                                                  rhs=qT[qs][0:100, :], start=True, stop=True),
                         [B_kT[hs], B_qT[qs]], [B_sb[sb]], inc=(j == 1))

            def PV_(i):
                n, p = divmod(i, NP)
                h, qt = divmod(n, NT)
                hs = h % 2
                sb = i % 3
                ob = n % 2
                vA3 = vA[hs].rearrange("p (k d) -> p k d", d=128)
                K.op("act", lambda e: e.activation(out=pT[sb], in_=sbig[sb][:, 0:2 * T], func=AF.Exp), [B_sb[sb]], [B_pT[sb]])
                for j in range(2):
                    kb = 2 * p + j
                    K.op("pe", lambda e: e.matmul(obank[ob][:, 0:T], lhsT=vA3[:, kb, :], rhs=pT[sb][:, j * T:(j + 1) * T],
                                                  start=(kb == 0), stop=(kb == NKB - 1)),
                         [B_vA[hs], B_pT[sb]], [B_ob[ob]], inc=(j == 1))
                if p == NP - 1:
                    K.op("dve", lambda e: e.reciprocal(out=rinv[ob][0:64, :], in_=obank[ob][64:128, 0:T]),
                         [B_ob[ob]], [B_rinv[ob]])
                    K.op("dve", lambda e: e.tensor_tensor(out=ost[ob][0:64, :], in0=obank[ob][0:64, 0:T], in1=rinv[ob][0:64, :],
                                                          op=ALU.mult), [B_ob[ob], B_rinv[ob]], [B_ost[ob]])
                    K.dma("sp", oT_d[h * 64:(h + 1) * 64, qt * T:(qt + 1) * T], ost[ob][0:64, :], [B_ost[ob]],
                          [dbuf(("o", l, qt))], B_ost[ob])

            load_head(0)
            load_q(0)
            NI = NTILE * NP
            LOOK = 2
            for i in range(min(LOOK, NI)):
                S_(i)
            for i in range(NI):
                if i + LOOK < NI:
                    S_(i + LOOK)
                PV_(i)

        if '3' in PASSES:
            K.barrier()
            P.reset()
            newbanks()
            nb = T // 128
            wg = P.bf16(8 * 3072)
            wo = P.bf16(6 * D)
            wout = P.bf16(8 * D)
            ln1rep = P.f32(D)
            bg = P.f32(24)
            B_w = Buf("w3")
            load_w_cast(wg, w_in[l][:, 0:3072], 8, 3072, B_w)
            load_w_cast(wo, w_o[l], 6, D, B_w)
            load_w_cast(wout, w_out[l], 8, D, B_w)
            K.dma("sp", ln1rep, ln1_g[l:l + 1, :].broadcast_to([128, D]), [], [B_w], B_w)
            K.dma("sp", bg, b_gate[l], [], [B_w], B_w)
            xin = P.f32(nb * D)
            xn = P.bf16(nb * D)
            xnT = P.bf16(8 * T)
            ss = P.f32(4)
            junk = P.bf16(D)
            oT = P.bf16(6 * T)
            ya = P.bf16(8 * T)
            yb = P.bf16(8 * T)
            mg = P.bf16(8 * T)
            gs = [[P.bf16(T) for _ in range(3)] for _ in range(2)]
            m1 = [P.f32(T) for _ in range(2)]
            m2 = [P.f32(T) for _ in range(2)]
            B_xin, B_xn, B_xnT, B_ss, B_junk, B_oT, B_ya, B_yb, B_mg = [Buf() for _ in range(9)]
            B_gs = [[Buf() for _ in range(3)] for _ in range(2)]
            B_m1 = [Buf() for _ in range(2)]
            B_m2 = [Buf() for _ in range(2)]
            wg3 = wg.rearrange("p (k n) -> p k n", n=3072)
            wo3 = wo.rearrange("p (k n) -> p k n", n=D)
            wout3 = wout.rearrange("p (k n) -> p k n", n=D)
            xnT3 = xnT.rearrange("p (k n) -> p k n", n=T)
            for t in range(NT):
                tok0 = t * T
                norm_transposed(xsrc, tok0, nb, ln1rep, xin, xn, xnT, B_xin, B_xn, B_xnT, B_w, (skey, t), ss, B_ss, junk, B_junk)
                K.dma("sp", oT.rearrange("p (c n) -> p c n", n=T), oT_d.rearrange("(c p) n -> p c n", p=128)[:, :, tok0:tok0 + T],
                      [dbuf(("o", l, t))], [B_oT], B_oT)
                K.dma("sp", ya.rearrange("p (c n) -> p c n", n=T), yaT.rearrange("(c p) n -> p c n", p=128)[:, :, tok0:tok0 + T],
                      [dbuf(("ya", l, t))], [B_ya], B_ya)
                K.dma("sp", yb.rearrange("p (c n) -> p c n", n=T), ybT.rearrange("(c p) n -> p c n", p=128)[:, :, tok0:tok0 + T],
                      [dbuf(("yb", l, t))], [B_yb], B_yb)
                for c in range(8):
                    i2 = c % 2
                    gb = []
                    for j in range(3):
                        bk = nextbank()
                        gb.append(bk)
                        for k in range(8):
                            K.op("pe", lambda e: e.matmul(banks[bk][:, 0:T], lhsT=wg3[:, k, j * D + c * 128:j * D + (c + 1) * 128],
                                                          rhs=xnT3[:, k, :], start=(k == 0), stop=(k == 7)),
                                 [B_w, B_xnT], [bankB[bk]], inc=(k == 7))
                        K.op("act", lambda e: e.activation(out=gs[i2][j], in_=banks[bk][:, 0:T], func=AF.Sigmoid,
                                                           bias=bg[:, j * 8 + c:j * 8 + c + 1]),
                             [bankB[bk], B_w], [B_gs[i2][j]])
                    bk = nextbank()
                    for k in range(6):
                        K.op("pe", lambda e: e.matmul(banks[bk][:, 0:T], lhsT=wo3[:, k, c * 128:(c + 1) * 128],
                                                      rhs=oT[:, k * T:(k + 1) * T], start=(k == 0), stop=(k == 5)),
                             [B_w, B_oT], [bankB[bk]], inc=(k == 5))
                    K.op("pool", lambda e: e.tensor_tensor(out=m1[i2], in0=gs[i2][0], in1=ya[:, c * T:(c + 1) * T], op=ALU.mult),
                         [B_gs[i2][0], B_ya], [B_m1[i2]])
                    K.op("pool", lambda e: e.tensor_tensor(out=m2[i2], in0=gs[i2][1], in1=yb[:, c * T:(c + 1) * T], op=ALU.mult),
                         [B_gs[i2][1], B_yb], [B_m2[i2]])
                    K.op("pool", lambda e: e.tensor_tensor(out=m1[i2], in0=m1[i2], in1=m2[i2], op=ALU.add),
                         [B_m1[i2], B_m2[i2]], [B_m1[i2]])
                    K.op("dve", lambda e: e.tensor_tensor(out=m2[i2], in0=banks[bk][:, 0:T], in1=gs[i2][2], op=ALU.mult),
                         [bankB[bk], B_gs[i2][2], B_m1[i2]], [B_m2[i2]])
                    K.op("dve", lambda e: e.tensor_tensor(out=mg[:, c * T:(c + 1) * T], in0=m1[i2], in1=m2[i2], op=ALU.add),
                         [B_m1[i2], B_m2[i2]], [B_mg])
                for b in range(nb):
                    for hf in range(2):
                        bk = nextbank()
                        for c in range(8):
                            K.op("pe", lambda e: e.matmul(banks[bk][:, 0:512], lhsT=mg[:, c * T + b * 128:c * T + (b + 1) * 128],
                                                          rhs=wout3[:, c, hf * 512:(hf + 1) * 512], start=(c == 0), stop=(c == 7)),
                                 [B_mg, B_w], [bankB[bk]], inc=(c == 7))
                        K.op("dve", lambda e: e.tensor_tensor(out=xin[:, b * D + hf * 512:b * D + (hf + 1) * 512],
                                                              in0=banks[bk][:, 0:512],
                                                              in1=xin[:, b * D + hf * 512:b * D + (hf + 1) * 512], op=ALU.add),
                             [bankB[bk], B_xin], [B_xin])
                K.dma("sp", xmid[tok0:tok0 + T, :].rearrange("(b p) d -> p b d", p=128), xin.rearrange("p (b d) -> p b d", d=D),
                      [B_xin], [dbuf(("xmid", l, t))], B_xin)

        if '4' in PASSES:
            K.barrier()
            P.reset()
            newbanks()
            nb = T // 128
            wfi = P.bf16(8 * 2 * DFF)
            wfo = P.bf16(22 * D)
            ln2rep = P.f32(D)
            B_w = Buf("w4")
            wfi3 = wfi.rearrange("p (k n) -> p k n", n=2 * DFF)
            for k in range(8):
                K.dma("pool", wfi3[:, k, :], w_fi[l][k * 128:(k + 1) * 128, :], [], [B_w], B_w)
            load_w_cast(wfo, w_fo[l], 22, D, B_w)
            K.dma("sp", ln2rep, ln2_g[l:l + 1, :].broadcast_to([128, D]), [], [B_w], B_w)
            xin = P.f32(nb * D)
            xn = P.bf16(nb * D)
            xnT = P.bf16(8 * T)
            ss = P.f32(4)
            junk = P.bf16(D)
            aa = P.bf16(22 * T)
            sl = [P.f32(T) for _ in range(2)]
            B_xin, B_xn, B_xnT, B_ss, B_junk, B_aa = [Buf() for _ in range(6)]
            B_sl = [Buf() for _ in range(2)]
            wfo3 = wfo.rearrange("p (k n) -> p k n", n=D)
            xnT3 = xnT.rearrange("p (k n) -> p k n", n=T)
            for t in range(NT):
                tok0 = t * T
                norm_transposed(xmid, tok0, nb, ln2rep, xin, xn, xnT, B_xin, B_xn, B_xnT, B_w, ("xmid", l, t), ss, B_ss,
                                junk, B_junk)
                for j in range(22):
                    i2 = j % 2
                    bi = nextbank()
                    bgt = nextbank()
                    for k in range(8):
                        K.op("pe", lambda e: e.matmul(banks[bi][:, 0:T], lhsT=wfi3[:, k, j * 128:(j + 1) * 128], rhs=xnT3[:, k, :],
                                                      start=(k == 0), stop=(k == 7)), [B_w, B_xnT], [bankB[bi]], inc=(k == 7))
                    for k in range(8):
                        K.op("pe", lambda e: e.matmul(banks[bgt][:, 0:T], lhsT=wfi3[:, k, DFF + j * 128:DFF + (j + 1) * 128],
                                                      rhs=xnT3[:, k, :], start=(k == 0), stop=(k == 7)),
                             [B_w, B_xnT], [bankB[bgt]], inc=(k == 7))
                    K.op("act", lambda e: e.activation(out=sl[i2], in_=banks[bgt][:, 0:T], func=AF.Silu), [bankB[bgt]], [B_sl[i2]])
                    K.op("dve", lambda e: e.tensor_tensor(out=aa[:, j * T:(j + 1) * T], in0=banks[bi][:, 0:T], in1=sl[i2],
                                                          op=ALU.mult), [bankB[bi], B_sl[i2]], [B_aa])
                for b in range(nb):
                    for hf in range(2):
                        bk = nextbank()
                        for j in range(22):
                            K.op("pe", lambda e: e.matmul(banks[bk][:, 0:512], lhsT=aa[:, j * T + b * 128:j * T + (b + 1) * 128],
                                                          rhs=wfo3[:, j, hf * 512:(hf + 1) * 512], start=(j == 0), stop=(j == 21)),
                                 [B_aa, B_w], [bankB[bk]], inc=(j == 21))
                        K.op("dve", lambda e: e.tensor_tensor(out=xin[:, b * D + hf * 512:b * D + (hf + 1) * 512],
                                                              in0=banks[bk][:, 0:512],
                                                              in1=xin[:, b * D + hf * 512:b * D + (hf + 1) * 512], op=ALU.add),
                             [bankB[bk], B_xin], [B_xin])
                K.dma("sp", xdst[tok0:tok0 + T, :].rearrange("(b p) d -> p b d", p=128), xin.rearrange("p (b d) -> p b d", d=D),
                      [B_xin], [dbuf((dkey, t))], B_xin)

    K.barrier()
    return nc, K


def rope_tables_np(S):
    pos = np.arange(S, dtype=np.float32)
    inv = (np.float32(10000.0) ** (-np.arange(0, 32, 2, dtype=np.float32) / np.float32(32))).astype(np.float32)
    ang = (pos[:, None] * inv[None, :]).astype(np.float32)
    return np.cos(ang).astype(np.float32), np.sin(ang).astype(np.float32)


def core_tables(NTOK, nsub, T1=512):
    S = NTOK // nsub
    cos, sin = rope_tables_np(S)
    cosT = np.ones((96, NTOK), np.float32)
    sinT = np.zeros((96, NTOK), np.float32)
    c = np.tile(cos.T, (1, nsub))
    s_ = np.tile(sin.T, (1, nsub))
    cosT[64:80] = c
    cosT[80:96] = c
    sinT[64:80] = -s_
    sinT[80:96] = s_
    seq = np.arange(NTOK) // S
    qmask = np.zeros((4, NTOK), np.float32)
    kmask = np.zeros((4, NTOK), np.float32)
    for j in range(4):
        qmask[j] = (seq == j)
        kmask[j] = np.where(seq == j, 0.0, -30000.0) if nsub > 1 else 0.0
    if nsub == 1:
        qmask[:] = 0.0
    NT1 = NTOK // T1
    hflag = np.zeros((32, NT1), np.float32)
    for t in range(NT1):
        tok0 = t * T1
        if tok0 % S != 0:
            hflag[0:15, t] = 1.0
        if (tok0 + T1) % S != 0:
            hflag[15:30, t] = 1.0
    return dict(cosT=cosT, sinT=sinT, qmask=qmask, kmask=kmask, hflag=hflag)


def layout_weights(w):
    NL = w["w_in"].shape[0]
    o = {}
    o["ln1_g"] = np.ascontiguousarray(w["ln1_g"])
    o["w_in"] = np.ascontiguousarray(w["w_in"])
    pe = w["w_in"][:, :, 6528:6560]
    w_pe = np.zeros((NL, D, 2, 96), np.float32)
    w_pe[:, :, 0, 64:96] = pe
    w_pe[:, :, 1, 64:80] = pe[:, :, 16:32]
    w_pe[:, :, 1, 80:96] = pe[:, :, 0:16]
    o["w_pe"] = w_pe
    o["b_gateT"] = np.ascontiguousarray(w["b_gate"].reshape(NL, 24, 128).transpose(0, 2, 1))
    o["conv_wT"] = np.ascontiguousarray(w["conv_w"].reshape(NL, CK, 6, 128).transpose(0, 3, 2, 1))
    o["conv_bT"] = np.ascontiguousarray(w["conv_b"].reshape(NL, 6, 128).transpose(0, 2, 1))
    o["cn_gT"] = np.ascontiguousarray(w["conv_norm_g"].reshape(NL, 6, 128).transpose(0, 2, 1))
    o["cn_bT"] = np.ascontiguousarray(w["conv_norm_b"].reshape(NL, 6, 128).transpose(0, 2, 1))
    o["w_conv_out"] = np.ascontiguousarray(w["w_conv_out"])
    o["sg_norm_g"] = np.ascontiguousarray(w["sg_norm_g"])
    o["w_spT"] = np.ascontiguousarray(w["w_spatial"].transpose(0, 3, 1, 2))
    o["b_spatial"] = np.ascontiguousarray(w["b_spatial"].reshape(NL, 1, 6 * 128))
    o["w_sg_out"] = np.ascontiguousarray(w["w_sg_out"])
    o["qn_gT"] = np.ascontiguousarray(w["q_norm_g"].reshape(NL, 2, 128).transpose(0, 2, 1))
    uq = w["w_uq"].reshape(NL, 256, NH, 96)
    o["w_uq"] = np.ascontiguousarray(w["w_uq"])
    uqs = uq.copy()
    uqs[..., 64:80] = uq[..., 80:96]
    uqs[..., 80:96] = uq[..., 64:80]
    o["w_uqs"] = np.ascontiguousarray(uqs.reshape(NL, 256, NH * 96))
    o["kvn_gT"] = np.ascontiguousarray(w["kv_norm_g"].reshape(NL, 128, 1))
    ukv = w["w_ukv"].reshape(NL, 128, NH, 128)
    o["w_ukn"] = np.ascontiguousarray(ukv[..., 0:64].reshape(NL, 128, NH * 64))
    o["w_ukv_v"] = np.ascontiguousarray(ukv[..., 64:128].reshape(NL, 128, NH * 64))

    def sw(g):
        gs = g.copy()
        gs[:, 64:80] = g[:, 80:96]
        gs[:, 80:96] = g[:, 64:80]
        return np.ascontiguousarray(np.stack([g, gs], axis=-1))
    o["gq"] = sw(w["qk_q_g"])
    o["gk"] = sw(w["qk_k_g"])
    o["w_o"] = np.ascontiguousarray(w["w_o"])
    o["w_out"] = np.ascontiguousarray(w["w_out"])
    o["ln2_g"] = np.ascontiguousarray(w["ln2_g"])
    o["w_ffn_in"] = np.ascontiguousarray(w["w_ffn_in"])
    o["w_ffn_out"] = np.ascontiguousarray(w["w_ffn_out"])
    o["zrows"] = np.zeros((16, D), np.float32)
    o["ident"] = np.eye(128, dtype=np.float32)
    return o


_CACHE = {}


def run_cores(xs_list, nsubs, weights, NTOK, NL):
    key = (NTOK, NL)
    if key not in _CACHE:
        _CACHE[key] = build(NTOK, NL)[0]
    nc = _CACHE[key]
    wl = layout_weights(weights)
    in_maps = []
    for x, ns in zip(xs_list, nsubs):
        m = dict(wl)
        m.update(core_tables(NTOK, ns))
        m["x"] = np.ascontiguousarray(x, dtype=np.float32)
        in_maps.append(m)
    res = run_bass_kernel_spmd(nc, in_maps, core_ids=list(range(len(in_maps))))
    return [r["y"] for r in res.results]


def kernel(x_prompt, x_sample, **weights):
    weights = {k: np.asarray(v, dtype=np.float32) for k, v in weights.items()}
    x_prompt = np.asarray(x_prompt, dtype=np.float32)
    x_sample = np.asarray(x_sample, dtype=np.float32)
    NTOK = 8192
    NL = weights["w_in"].shape[0]
    xs_list = [x_sample[i] for i in range(4)]
    nsubs = [1, 1, 1, 1]
    for i in range(2):
        xs_list.append(x_prompt[4 * i:4 * i + 4].reshape(NTOK, D))
        nsubs.append(4)
    for i in range(2):
        xs_list.append(x_prompt[4 * i:4 * i + 4].reshape(NTOK, D))
        nsubs.append(4)
    ys = run_cores(xs_list, nsubs, weights, NTOK, NL)
    y_sample = np.stack(ys[0:4], axis=0).astype(np.float32)
    y_prompt = np.concatenate([ys[4].reshape(4, 2048, D), ys[5].reshape(4, 2048, D)], axis=0).astype(np.float32)
    return (y_prompt, y_sample)
```
